# Optimizing a Trainium2 kernel written in Bass

```python
import jax
import jax.numpy as jnp
from jax import lax
import numpy as np

D_MODEL = 1024
BATCH = 32
SEQ = 256
DEPTH = 2
DEC_BATCH = 4
DEC_SEQ = 4096
PAST_LEN = 512

GRID_W = 64
EPS = 1e-6
D_CONV = 512
CONV_K = 31
NA_HEADS = 8
HEAD_DIM = 64
D_ATT = NA_HEADS * HEAD_DIM
NA_ROWS = 8
NA_COLS = 16
NA_QCB = 16
NA_KCB = NA_QCB + NA_COLS
ATT_SCALE = HEAD_DIM ** -0.5
Q_BLOCK = 128
D_IN_EVEN = 3 * D_CONV + 4 * D_ATT
SPLITS_EVEN = (D_CONV, 2 * D_CONV, 3 * D_CONV, 3 * D_CONV + D_ATT, 3 * D_CONV + 2 * D_ATT, 3 * D_CONV + 3 * D_ATT)
D_POOL = D_MODEL
POOL_WINDOWS = (2, 4, 8, 16)
POOL_GROUPS = 4
POOL_GD = D_POOL // POOL_GROUPS

kernel_name = 'hybrid_conv_natten_pool_diffusion_step'


def rms_norm(x, g):
    xf = x.astype(jnp.float32)
    y = xf * lax.rsqrt(jnp.mean(xf * xf, axis=-1, keepdims=True) + EPS)
    return (y * g.astype(jnp.float32)).astype(x.dtype)


def layer_norm(x, g, b):
    xf = x.astype(jnp.float32)
    mu = jnp.mean(xf, axis=-1, keepdims=True)
    var = jnp.mean(jnp.square(xf - mu), axis=-1, keepdims=True)
    y = (xf - mu) * lax.rsqrt(var + EPS)
    return (y * g.astype(jnp.float32) + b.astype(jnp.float32)).astype(x.dtype)


def ada_modulate(x, cond, norm_g, w_ada, b_ada):
    m = jax.nn.silu(cond) @ w_ada + b_ada
    shift, scale, gate = jnp.split(m, 3, axis=-1)
    h = rms_norm(x, norm_g) * (1.0 + scale[:, None, :]) + shift[:, None, :]
    return h, gate[:, None, :]


def conformer_conv(a, b, conv_w, conv_b, ln_g, ln_b):
    u = a * jax.nn.sigmoid(b)
    y = lax.conv_general_dilated(u, conv_w[:, None, :], window_strides=(1,),
                                 padding=((CONV_K // 2, CONV_K // 2),),
                                 dimension_numbers=('NWC', 'WIO', 'NWC'),
                                 feature_group_count=D_CONV)
    y = layer_norm(y + conv_b, ln_g, ln_b)
    return jax.nn.silu(y)


def even_projections(h, w_in, q_norm, k_norm):
    bsz, L, _ = h.shape
    a, b, ga, q, k, v, gb = jnp.split(h @ w_in, SPLITS_EVEN, axis=-1)
    q = rms_norm(q.reshape(bsz, L, NA_HEADS, HEAD_DIM), q_norm)
    k = rms_norm(k.reshape(bsz, L, NA_HEADS, HEAD_DIM), k_norm)
    v = v.reshape(bsz, L, NA_HEADS, HEAD_DIM)
    return a, b, ga, q, k, v, gb


def even_output(y_conv, ga, y_att, gb, w_out):
    bsz, L = y_conv.shape[0], y_conv.shape[1]
    z = jnp.concatenate([y_conv * jax.nn.silu(ga),
                         y_att.reshape(bsz, L, D_ATT) * jax.nn.silu(gb)], axis=-1)
    return z @ w_out


def context_attention(q, k, v):
    bsz, L, H, Dh = q.shape
    qb = q.reshape(bsz, L // Q_BLOCK, Q_BLOCK, H, Dh).transpose(1, 0, 2, 3, 4)

    def block(qi):
        s = jnp.einsum('bqhd,bkhd->bhqk', qi, k).astype(jnp.float32) * ATT_SCALE
        p = jax.nn.softmax(s, axis=-1).astype(v.dtype)
        return jnp.einsum('bhqk,bkhd->bqhd', p, v)

    o = lax.map(block, qb)
    return o.transpose(1, 0, 2, 3, 4).reshape(bsz, L, H, Dh)


def neighbourhood_attention(q, k, v, k_ctx, v_ctx, rpb):
    bsz, L, H, Dh = q.shape
    rows = L // GRID_W
    kr = min(NA_ROWS, rows)
    ncb = GRID_W // NA_QCB
    r_all = jnp.arange(rows)
    row_start = jnp.clip(r_all - NA_ROWS // 2, 0, rows - kr)
    row_idx = row_start[:, None] + jnp.arange(kr)[None, :]
    cols = jnp.arange(GRID_W)
    q_col_start = jnp.clip(cols - NA_COLS // 2, 0, GRID_W - NA_COLS).reshape(ncb, NA_QCB)
    cb_start = jnp.clip(jnp.arange(ncb) * NA_QCB - NA_COLS // 2, 0, GRID_W - NA_KCB)
    col_idx = cb_start[:, None] + jnp.arange(NA_KCB)[None, :]
    key_col = col_idx[:, None, :]
    col_ok = (key_col >= q_col_start[:, :, None]) & (key_col < q_col_start[:, :, None] + NA_COLS)
    rel_c = key_col - cols.reshape(ncb, NA_QCB)[:, :, None]
    rel_c_idx = jnp.clip(rel_c + NA_COLS - 1, 0, 2 * NA_COLS - 2)
    rpb_c = rpb.astype(jnp.float32)[:, :, rel_c_idx]
    kg = k.reshape(bsz, rows, GRID_W, H, Dh)
    vg = v.reshape(bsz, rows, GRID_W, H, Dh)
    q_rows = q.reshape(bsz, rows, ncb, NA_QCB, H, Dh).transpose(1, 0, 2, 3, 4, 5)
    n_win = kr * NA_KCB

    def row_step(args):
        q_r, r_idx, r_id = args
        gidx = (r_idx[:, None, None], col_idx[None, :, :])
        kw = kg[:, gidx[0], gidx[1]]
        vw = vg[:, gidx[0], gidx[1]]
        s_win = jnp.einsum('bjqhd,bkjwhd->bhjqkw', q_r, kw).astype(jnp.float32) * ATT_SCALE
        bias = rpb_c[:, r_idx - r_id + NA_ROWS - 1].transpose(0, 2, 3, 1, 4)
        s_win = jnp.where(col_ok[None, :, :, None, :], s_win + bias, -jnp.inf)
        s_ctx = jnp.einsum('bjqhd,bmhd->bhjqm', q_r, k_ctx).astype(jnp.float32) * ATT_SCALE
        s = jnp.concatenate([s_win.reshape(bsz, H, ncb, NA_QCB, n_win), s_ctx], axis=-1)
        p = jax.nn.softmax(s, axis=-1).astype(v.dtype)
        p_win = p[..., :n_win].reshape(bsz, H, ncb, NA_QCB, kr, NA_KCB)
        p_ctx = p[..., n_win:]
        return (jnp.einsum('bhjqkw,bkjwhd->bjqhd', p_win, vw)
                + jnp.einsum('bhjqm,bmhd->bjqhd', p_ctx, v_ctx))

    o = lax.map(row_step, (q_rows, row_idx, r_all))
    return o.transpose(1, 0, 2, 3, 4, 5).reshape(bsz, L, H, Dh)


def multiscale_pool(u, pool_w, pool_scale):
    bsz, L, C = u.shape
    uf = u.astype(jnp.float32)
    csum = jnp.concatenate([jnp.zeros((bsz, 1, C), jnp.float32), jnp.cumsum(uf, axis=1)], axis=1)
    t = jnp.arange(L)
    outs = []
    for gi, w in enumerate(POOL_WINDOWS):
        lo = jnp.clip(t - w // 2, 0, L)
        hi = jnp.clip(t + w - w // 2, 0, L)
        sl = slice(gi * POOL_GD, (gi + 1) * POOL_GD)
        cg = csum[:, :, sl]
        mean = (cg[:, hi] - cg[:, lo]) / (hi - lo).astype(jnp.float32)[:, None]
        outs.append(mean - uf[:, :, sl])
    d = jnp.stack(outs, axis=2).astype(u.dtype)
    y = jnp.einsum('blgc,gce->blge', d, pool_w).reshape(bsz, L, C)
    return y * pool_scale


def odd_mixer(h, w_in, pool_w, pool_scale, w_out):
    u, g = jnp.split(h @ w_in, 2, axis=-1)
    return (multiscale_pool(u, pool_w, pool_scale) * jax.nn.silu(g)) @ w_out


def setup_inputs(seed: int = 0) -> dict:
    key = jax.random.key(seed)
    ks = jax.random.split(key, 32)

    def nrm(k, shape, s):
        return jax.random.normal(k, shape, jnp.float32) * s

    d = D_MODEL
    return {
        'x_prompt': nrm(ks[0], (BATCH, SEQ, d), 1.0),
        'x_sample': nrm(ks[1], (DEC_BATCH, DEC_SEQ, d), 1.0),
        'cache_k_0': nrm(ks[2], (DEC_BATCH, PAST_LEN, NA_HEADS, HEAD_DIM), 1.0),
        'cache_v_0': nrm(ks[3], (DEC_BATCH, PAST_LEN, NA_HEADS, HEAD_DIM), 1.0),
        'c': nrm(ks[4], (DEC_BATCH, d), 1.0),
        'c_ctx': nrm(ks[5], (d,), 1.0),
        'norm_g_0': 1.0 + nrm(ks[6], (d,), 0.02),
        'w_ada_0': nrm(ks[7], (d, 3 * d), 0.5 * d ** -0.5),
        'b_ada_0': nrm(ks[8], (3 * d,), 0.02),
        'w_in_0': nrm(ks[9], (d, D_IN_EVEN), d ** -0.5),
        'conv_w_0': nrm(ks[10], (CONV_K, D_CONV), CONV_K ** -0.5),
        'conv_b_0': nrm(ks[11], (D_CONV,), 0.01),
        'conv_ln_g_0': 1.0 + nrm(ks[12], (D_CONV,), 0.02),
        'conv_ln_b_0': nrm(ks[13], (D_CONV,), 0.01),
        'q_norm_0': 1.0 + nrm(ks[14], (HEAD_DIM,), 0.02),
        'k_norm_0': 1.0 + nrm(ks[15], (HEAD_DIM,), 0.02),
        'rpb_0': nrm(ks[16], (NA_HEADS, 2 * NA_ROWS - 1, 2 * NA_COLS - 1), 0.1),
        'w_out_0': nrm(ks[17], (D_CONV + D_ATT, d), (D_CONV + D_ATT) ** -0.5),
        'norm_g_1': 1.0 + nrm(ks[18], (d,), 0.02),
        'w_ada_1': nrm(ks[19], (d, 3 * d), 0.5 * d ** -0.5),
        'b_ada_1': nrm(ks[20], (3 * d,), 0.02),
        'w_in_1': nrm(ks[21], (d, 2 * D_POOL), d ** -0.5),
        'pool_w_1': nrm(ks[22], (POOL_GROUPS, POOL_GD, POOL_GD), POOL_GD ** -0.5),
        'pool_scale_1': 1.0 + nrm(ks[23], (D_POOL,), 0.02),
        'w_out_1': nrm(ks[24], (D_POOL, d), D_POOL ** -0.5),
    }


def reference(x_prompt, x_sample, cache_k_0, cache_v_0, c, c_ctx,
              norm_g_0, w_ada_0, b_ada_0, w_in_0, conv_w_0, conv_b_0, conv_ln_g_0, conv_ln_b_0,
              q_norm_0, k_norm_0, rpb_0, w_out_0,
              norm_g_1, w_ada_1, b_ada_1, w_in_1, pool_w_1, pool_scale_1, w_out_1):
    even_params = {0: (norm_g_0, w_ada_0, b_ada_0, w_in_0, conv_w_0, conv_b_0, conv_ln_g_0, conv_ln_b_0,
                       q_norm_0, k_norm_0, rpb_0, w_out_0)}
    odd_params = {1: (norm_g_1, w_ada_1, b_ada_1, w_in_1, pool_w_1, pool_scale_1, w_out_1)}
    caches = {0: (cache_k_0, cache_v_0)}
    cond_ctx = jnp.broadcast_to(c_ctx[None, :], (x_prompt.shape[0], D_MODEL))
    y_prompt, y_sample = x_prompt, x_sample
    new_state = {}
    for i in range(DEPTH):
        if i % 2 == 0:
            (norm_g, w_ada, b_ada, w_in, conv_w, conv_b, ln_g, ln_b, qn, kn, rpb, w_out) = even_params[i]
            h, gate = ada_modulate(y_prompt, cond_ctx, norm_g, w_ada, b_ada)
            a, b, ga, q, k, v, gb = even_projections(h, w_in, qn, kn)
            out = even_output(conformer_conv(a, b, conv_w, conv_b, ln_g, ln_b), ga,
                              context_attention(q, k, v), gb, w_out)
            y_prompt = y_prompt + gate * out
            new_state[i] = (k, v)
            h, gate = ada_modulate(y_sample, c, norm_g, w_ada, b_ada)
            a, b, ga, q, k, v, gb = even_projections(h, w_in, qn, kn)
            k_c, v_c = caches[i]
            out = even_output(conformer_conv(a, b, conv_w, conv_b, ln_g, ln_b), ga,
                              neighbourhood_attention(q, k, v, k_c, v_c, rpb), gb, w_out)
            y_sample = y_sample + gate * out
        else:
            (norm_g, w_ada, b_ada, w_in, pool_w, pool_scale, w_out) = odd_params[i]
            h, gate = ada_modulate(y_prompt, cond_ctx, norm_g, w_ada, b_ada)
            y_prompt = y_prompt + gate * odd_mixer(h, w_in, pool_w, pool_scale, w_out)
            h, gate = ada_modulate(y_sample, c, norm_g, w_ada, b_ada)
            y_sample = y_sample + gate * odd_mixer(h, w_in, pool_w, pool_scale, w_out)
    k_ctx_0, v_ctx_0 = new_state[0]
    return (y_prompt, y_sample, k_ctx_0, v_ctx_0)
```

```python
import numpy as np
from contextlib import ExitStack
import concourse.bass as bass
import concourse.mybir as mybir
from concourse.bass_utils import run_bass_kernel_spmd

F32 = mybir.dt.float32
BF16 = mybir.dt.bfloat16
ALU = mybir.AluOpType
AF = mybir.ActivationFunctionType
EPS = 1e-6
NTS = 19
NTB = 17
TS = NTS * 128
TTOT = 1024 + TS
UB = 4 * 288
ULEN = UB + 16 + TS + 16
NG0, NG1, BA0, BA1, CW, CB, LG, LB, QN, KN, PSC, CVEC, KNR, NSM = 0, 8, 16, 40, 64, 188, 192, 196, 200, 201, 202, 210, 226, 290


class Buf:
    __slots__ = ("name", "lw", "rs", "excl")

    def __init__(self, name, excl=False):
        self.name = name
        self.excl = excl
        self.lw = None
        self.rs = []


class Sched:
    ENG = ["pe", "act", "dve", "pool", "sp"]

    def __init__(self, ndma=24):
        self.ops = {e: [] for e in self.ENG}
        self.ndma = ndma
        self.dma_count = [0] * ndma
        self.dma_next = [0, 0]
        self.bar = {e: [] for e in self.ENG}

    def add(self, eng, fn, reads=(), writes=(), dma=False):
        deps = set(self.bar[eng])
        self.bar[eng] = []
        idx = len(self.ops[eng])
        if dma:
            half = self.ndma // 2
            qi = 0 if eng == "sp" else 1
            k = qi * half + self.dma_next[qi]
            self.dma_next[qi] = (self.dma_next[qi] + 1) % half
            if self.dma_count[k] > 0:
                deps.add(("dma", k, 16 * self.dma_count[k]))
            self.dma_count[k] += 1
            ref = ("dma", k, 16 * self.dma_count[k])
        else:
            ref = ("op", eng, idx)
        excl_reads = [b for b in reads if b.excl]
        if excl_reads:
            reads = [b for b in reads if not b.excl]
            writes = list(writes) + excl_reads
        for b in reads:
            if b.lw is not None:
                deps.add(b.lw)
        for b in writes:
            if b.lw is not None:
                deps.add(b.lw)
            deps.update(b.rs)
        for b in reads:
            b.rs.append(ref)
        for b in writes:
            b.lw = ref
            b.rs = []
        deps.discard(ref)
        if eng == "pe":
            deps = {d for d in deps if not (d[0] == "op" and d[1] == "pe")}
        self.ops[eng].append(dict(fn=fn, deps=deps, ref=ref, dma=dma))
        return ref

    def barrier(self, skip_sems=()):
        refs = []
        for e in self.ENG:
            for op in reversed(self.ops[e]):
                if op["fn"] is not None and not op["dma"]:
                    refs.append(op["ref"])
                    break
        for k in range(self.ndma):
            if self.dma_count[k] > 0 and k not in skip_sems:
                refs.append(("dma", k, 16 * self.dma_count[k]))
        for e in self.ENG:
            self.bar[e] = list(refs)

    def finish(self):
        self.barrier()
        for e in self.ENG:
            self.ops[e].append(dict(fn=None, deps=set(self.bar[e]), ref=None, dma=False))
            self.bar[e] = []

    def emit(self, block, sems, dma_sems):
        need = {e: set() for e in self.ENG}
        for e in self.ENG:
            for op in self.ops[e]:
                for d in op["deps"]:
                    if d[0] == "op":
                        need[d[1]].add(d[2])
        count = {}
        for e in self.ENG:
            count[e] = {}
            c = 0
            for idx in sorted(need[e]):
                c += 1
                count[e][idx] = c

        def run(e):
            def f(eng):
                waited = {}
                for idx, op in enumerate(self.ops[e]):
                    for d in sorted(op["deps"], key=str):
                        if d[0] == "op":
                            key = ("op", d[1])
                            val = count[d[1]][d[2]]
                            sem = sems[d[1]]
                        else:
                            key = ("dma", d[1])
                            val = d[2]
                            sem = dma_sems[d[1]]
                        if waited.get(key, 0) >= val:
                            continue
                        eng.wait_ge(sem, val)
                        waited[key] = val
                    if op["fn"] is None:
                        continue
                    ins = op["fn"](eng)
                    if op["dma"]:
                        ins.then_inc(dma_sems[op["ref"][1]], 16)
                    elif idx in count[e]:
                        ins.then_inc(sems[e], 1)
            return f

        block.tensor(run("pe"))
        block.scalar(run("act"))
        block.vector(run("dve"))
        block.gpsimd(run("pool"))
        block.sync(run("sp"))


def pat_index(t, j):
    if t >= 2:
        return j - t + 2
    return 5 + 4 * t + j


def key_tiles(t):
    return list(range(max(t - 2, 0), max(t + 2, 3) + 1))


def build_program():
    nc = bass.Bass("TRN2", target_bir_lowering=False)
    S = Sched()

    def din(name, shape):
        return nc.dram_tensor(name, list(shape), F32, kind="ExternalInput").ap()

    def dout(name, shape):
        return nc.dram_tensor(name, list(shape), F32, kind="ExternalOutput").ap()

    xp = din("xp", [1024, 1024])
    xs = din("xs", [TS, 1024])
    ckT = din("ckT", [512, 512])
    cvv = din("cvv", [512, 512])
    smallp = din("smallp", [128, NSM])
    bg = din("bg", [128, 2048])
    w_ada = [din("w_ada_0", [1024, 3072]), din("w_ada_1", [1024, 3072])]
    w_in_0 = din("w_in_0", [1024, 3584])
    w_out_0 = din("w_out_0", [1024, 1024])
    w_in_1 = din("w_in_1", [1024, 2048])
    pool_w = din("pool_w", [4, 256, 256])
    w_out_1 = din("w_out_1", [1024, 1024])
    ebias = din("ebias", [128, 8 * 13 * 128])
    emask = din("emask", [128, 13 * 128])
    band = din("band", [128, 32 * 128])
    ident = din("ident", [128, 128])
    bdiag = din("bdiag", [128, 128])
    ones32 = din("ones32", [128, 128])
    yp = dout("yp", [1024, 1024])
    ys = dout("ys", [2048, 1024])
    ko = dout("ko", [1024, 512])
    vo = dout("vo", [1024, 512])
    sU = nc.dram_tensor("sU", [4, 128, ULEN], BF16).ap()
    sGA = nc.dram_tensor("sGA", [4, 128, TTOT], BF16).ap()
    sQ = nc.dram_tensor("sQ", [4, 128, TTOT], BF16).ap()
    sK = nc.dram_tensor("sK", [4, 128, TTOT], BF16).ap()
    sV = nc.dram_tensor("sV", [27, 128, 520], BF16).ap()
    sGB = nc.dram_tensor("sGB", [27, 128, 512], BF16).ap()
    sY0 = nc.dram_tensor("sY0", [25, 128, 1024], F32).ap()
    bU, bGA, bQ, bK, bV, bGB, bY0 = (Buf(n) for n in ["sU", "sGA", "sQ", "sK", "sV", "sGB", "sY0"])

    with ExitStack() as es:
        ZB = es.enter_context(nc.sbuf_tensor("ZB", [128, 43008], BF16))
        WKB = es.enter_context(nc.sbuf_tensor("WKB", [128, 29440], BF16))
        WK = es.enter_context(nc.sbuf_tensor("WK", [128, 10752], F32))
        SM = es.enter_context(nc.sbuf_tensor("SM", [128, NSM], F32))
        G = es.enter_context(nc.sbuf_tensor("G", [128, 4, 1024], F32))
        MOD = es.enter_context(nc.sbuf_tensor("MOD", [128, 2, 32], F32))
        AM = es.enter_context(nc.sbuf_tensor("AM", [128, 4, 8], F32))
        IDB = es.enter_context(nc.sbuf_tensor("IDB", [128, 128], BF16))
        BDB = es.enter_context(nc.sbuf_tensor("BDB", [128, 128], BF16))
        ON32 = es.enter_context(nc.sbuf_tensor("ON32", [128, 128], F32))
        SC = es.enter_context(nc.sbuf_tensor("SC", [128, 16], F32))
        ST = es.enter_context(nc.sbuf_tensor("ST", [128, 64], F32))
        PRM = es.enter_context(nc.sbuf_tensor("PRM", [128, 80], F32))
        CWHT = es.enter_context(nc.sbuf_tensor("CWHT", [128, 124], F32))
        RS1 = es.enter_context(nc.sbuf_tensor("RS1", [128, 32], F32))
        bRS1 = [Buf(f"rs1_{i}") for i in range(32)]
        PSF = [es.enter_context(nc.psum_tensor(f"psf{i}", [128, 512], F32)) for i in range(7)]
        PSB0 = es.enter_context(nc.psum_tensor("psb0", [128, 1024], BF16))
        bPSF = [Buf(f"psf{i}", True) for i in range(7)]
        PSB = [PSB0, PSF[6][:, :].bitcast(BF16)]
        bPSB = [Buf("psb0", True), bPSF[6]]
        sems = {e: es.enter_context(nc.semaphore("s_" + e)) for e in S.ENG}
        dsems = [es.enter_context(nc.semaphore(f"d{i}")) for i in range(S.ndma)]
        block = es.enter_context(nc.Block())

        bSM, bG, bMOD, bAM, bIDB, bBDB, bON, bSC, bPRM = (Buf(n) for n in
                                                             ["SM", "G", "MOD", "AM", "IDB", "BDB", "ON", "SC", "PRM"])
        st_ctr = [0]
        stat_bufs = [Buf(f"stc{i}") for i in range(8)]
        TC = es.enter_context(nc.sbuf_tensor("TC", [128, 16], F32))

        def stat():
            s_ = st_ctr[0] % 8
            st_ctr[0] += 1
            return ST[:, 8 * s_:8 * s_ + 8], stat_bufs[s_]

        psf_i = [0]
        psb_i = [0]
        psf_n = [4]
        psb_n = [2]
        rot = [[0, 1, 2, 3, 4, 5]]

        def bankf():
            i = psf_i[0]
            i = i % psf_n[0]
            psf_i[0] = (i + 1) % psf_n[0]
            i = rot[0][i]
            return PSF[i], bPSF[i]

        def bankb():
            i = psb_i[0] % psb_n[0]
            psb_i[0] = (i + 1) % psb_n[0]
            return PSB[i], bPSB[i]

        class Carver:
            def __init__(self, t, size):
                self.t, self.size, self.off = t, size, 0

            def reset(self):
                self.off = 0

            def get(self, name, shape):
                n = int(np.prod(shape))
                assert self.off + n <= self.size, (name, self.off, n, self.size)
                v = self.t[:, self.off:self.off + n]
                self.off += n
                if len(shape) == 2:
                    v = v.rearrange("p (a b) -> p a b", a=shape[0])
                elif len(shape) == 3:
                    v = v.rearrange("p (a b c) -> p a b c", a=shape[0], b=shape[1])
                return v, Buf(name)

        cz, cb, cf = Carver(ZB, 43008), Carver(WKB, 29440), Carver(WK, 10752)

        def dma(q, out, in_, reads=(), writes=()):
            return S.add(q, lambda e: e.dma_start(out=out, in_=in_), reads, writes, dma=True)

        def act(out, in_, func, reads, writes, bias=None, scale=None, accum=None):
            def fn(e):
                kw = {}
                if bias is not None:
                    kw["bias"] = bias
                if scale is not None:
                    kw["scale"] = scale
                if accum is not None:
                    kw["accum_out"] = accum
                return e.activation(out=out, in_=in_, func=func, **kw)
            return S.add("act", fn, reads, writes)

        def tt(out, in0, in1, op, reads, writes, eng="dve"):
            return S.add(eng, lambda e: e.tensor_tensor(out=out, in0=in0, in1=in1, op=op), reads, writes)

        def tsc(out, in0, s1, s2, op0, op1, reads, writes, eng="dve"):
            if s2 is None:
                return S.add(eng, lambda e: e.tensor_scalar(out=out, in0=in0, scalar1=s1, scalar2=None, op0=op0),
                             reads, writes)
            return S.add(eng, lambda e: e.tensor_scalar(out=out, in0=in0, scalar1=s1, scalar2=s2, op0=op0, op1=op1),
                         reads, writes)

        def stt(out, in0, scalar, in1, op0, op1, reads, writes, eng="dve"):
            return S.add(eng, lambda e: e.scalar_tensor_tensor(out=out, in0=in0, scalar=scalar, in1=in1,
                                                               op0=op0, op1=op1), reads, writes)

        def cpy(out, in_, reads, writes, eng="dve"):
            return S.add(eng, lambda e: e.tensor_copy(out=out, in_=in_), reads, writes)

        def mms(lst, reads, writes):
            def fn(e):
                ins = None
                for (o, l, r, st, sp) in lst:
                    ins = e.matmul(o, lhsT=l, rhs=r, start=st, stop=sp)
                return ins
            return S.add("pe", fn, reads, writes)

        def transposes(lst, reads, writes):
            def fn(e):
                ins = None
                for (o, i) in lst:
                    ins = e.transpose(out=o, in_=i, identity=IDB[:])
                return ins
            return S.add("pe", fn, list(reads) + [bIDB], writes)

        def rsqrt_act(out, in_, reads, writes, bias_ap, scale=1.0):
            act(out, in_, AF.Ln, reads, writes, bias=bias_ap, scale=scale)
            act(out, out, AF.Exp, writes, writes, scale=-0.5)

        dma("sp", SM[:], smallp, writes=[bSM])
        dma("sp", ON32[:], ones32, writes=[bON])
        dma("pool", IDB[:], ident, writes=[bIDB])
        dma("pool", BDB[:], bdiag, writes=[bBDB])
        W0, bW0 = cz.get("W0", [8, 3584])
        w0v = w_in_0.rearrange("(k p) n -> p k n", p=128)
        bW0g = [Buf(f"W0g{g}") for g in range(7)]
        w0_sems = set()
        for g in (0, 1, 2, 5, 6, 3, 4):
            r_ = dma("pool", W0[:, :, g * 512:(g + 1) * 512], w0v[:, :, g * 512:(g + 1) * 512], writes=[bW0g[g]])
            w0_sems.add(r_[1])
        S.add("dve", lambda e: e.memset(PRM[:, 0:1], EPS), writes=[bPRM])
        S.add("dve", lambda e: e.memset(PRM[:, 1:2], 64 * EPS), writes=[bPRM])
        cpy(PRM[:, 2:3], SM[:, QN:QN + 1], [bSM], [bPRM])
        tsc(PRM[:, 3:4], SM[:, KN:KN + 1], 8.0, None, ALU.mult, None, [bSM], [bPRM])
        tsc(PRM[:, 8:16], SM[:, PSC:PSC + 8], 0.5, None, ALU.mult, None, [bSM], [bPRM])
        tsc(PRM[:, 16:80], SM[:, KNR:KNR + 64], 8.0, None, ALU.mult, None, [bSM], [bPRM])
        tanh_c, btc = TC[:], Buf('TC')
        act(tanh_c, SM[:, CVEC:CVEC + 16], AF.Tanh, [bSM], [btc], scale=0.5)
        stt(SC[:], tanh_c, 1.0, SM[:, CVEC:CVEC + 16], ALU.add, ALU.mult, [btc, bSM], [bSC])
        tsc(SC[:], SC[:], 0.5, None, ALU.mult, None, [bSC], [bSC])
        cf.reset()
        cb.reset()
        bMODl = [Buf("MOD0"), Buf("MOD1")]
        bAMl = [Buf("AM0"), Buf("AM1")]
        bGl = [Buf("G0"), Buf("G1")]
        WA = [cb.get(f"WAb{i}", [8, 256]) for i in range(4)]
        WF = [cf.get(f"WAf{i}", [8, 256]) for i in range(4)]
        SCB, bSCB = cz.get("SCBb", [16, 128])
        SCh, bSCh = cz.get("SCh", [32])
        WAq = [cz.get(f"WAq{i}", [8, 128]) for i in range(2)]
        cpy(SCh[:, 0:16], SC[:, 0:16], [bSC], [bSCh])
        cpy(SCB, SC[:, 0:16].unsqueeze(2).to_broadcast([128, 16, 128]), [bSC], [bSCB])

        def finish_mod(l, pfm, bpfm):
            ba = BA0 if l == 0 else BA1
            ng = NG0 if l == 0 else NG1
            tt(MOD[:, l, :].rearrange("p (a b) -> p a b", b=2), pfm[:, 0:32].rearrange("p (a b) -> p a b", b=2),
               SM[:, ba:ba + 16].unsqueeze(2).to_broadcast([128, 16, 2]), ALU.add, [bpfm, bSM], [bMODl[l]])
            mv = MOD[:, l, :].rearrange("p (a b) -> p a b", b=2)
            for ci in range(2):
                stt(AM[:, l * 2 + ci, :], mv[:, 8:16, ci], 1.0, SM[:, ng:ng + 8], ALU.add, ALU.mult,
                    [bMODl[l], bSM], [bAMl[l]])

        wav0 = w_ada[0].rearrange("(k p) n -> p k n", p=128)
        pfm, bpfm = PSF[4], bPSF[4]
        for g in range(4):
            dma("sp", WF[g][0], wav0[:, :, g * 256:(g + 1) * 256], writes=[WF[g][1]])
        for g in range(8):
            wa, bwa = WA[g % 4]
            wf, bwf = WF[g % 4]
            if g % 2 == 0:
                cpy(wa, wf, [bwf], [bwa])
            else:
                act(wa, wf, AF.Copy, [bwf], [bwa])
            if g + 4 < 8:
                dma("sp", wf, wav0[:, :, (g + 4) * 256:(g + 5) * 256], writes=[bwf])
            lst = []
            for blk in range(2):
                o = pfm[:, (g * 2 + blk) * 2:(g * 2 + blk) * 2 + 2]
                for k in range(8):
                    lst.append((o, wa[:, k, blk * 128:(blk + 1) * 128], SCh[:, 2 * k:2 * k + 2], k == 0, k == 7))
            mms(lst, [bwa, bSCh], [bpfm])
        finish_mod(0, pfm, bpfm)

        def modgen(WFq, BGq):
            pieces = [(0, "g", j) for j in range(8)] + [(1, "f", j) for j in range(16)] + [(1, "g", j) for j in range(8)]
            pf1, bpf1 = PSF[5], bPSF[5]

            def issue(i):
                l, kind, j = pieces[i]
                col0 = (2048 if kind == "g" else 0) + j * 128
                wv = w_ada[l].rearrange("(k p) n -> p k n", p=128)
                dma("sp", WFq[i % 2][0], wv[:, :, col0:col0 + 128], writes=[WFq[i % 2][1]])
                if kind == "g":
                    dma("sp", BGq[i % 2][0], bg[:, l * 1024 + j * 128:l * 1024 + (j + 1) * 128], writes=[BGq[i % 2][1]])

            issue(0)
            yield
            for i, (l, kind, j) in enumerate(pieces):
                if i + 1 < len(pieces):
                    issue(i + 1)
                    yield
                wf, bwf = WFq[i % 2]
                wa, bwa = WAq[i % 2]
                if i % 2 == 0:
                    cpy(wa, wf, [bwf], [bwa])
                else:
                    act(wa, wf, AF.Copy, [bwf], [bwa])
                yield
                if kind == "f":
                    lst = [(pf1[:, j * 2:j * 2 + 2], wa[:, k, :], SCh[:, 2 * k:2 * k + 2], k == 0, k == 7) for k in range(8)]
                    mms(lst, [bwa, bSCh], [bpf1])
                    if j == 15:
                        finish_mod(1, pf1, bpf1)
                else:
                    pg, bpg = bankf()
                    lst = []
                    for ci in range(2):
                        for k in range(8):
                            lst.append((pg[:, ci * 128:(ci + 1) * 128], SCB[:, 2 * k + ci, :], wa[:, k, :], k == 0, k == 7))
                    mms(lst, [bwa, bSCB], [bpg])
                    for ci in range(2):
                        tt(G[:, l * 2 + ci, j * 128:(j + 1) * 128], pg[:, ci * 128:(ci + 1) * 128], BGq[i % 2][0],
                           ALU.add, [bpg, BGq[i % 2][1]], [bGl[l]])
                yield

        def Asc(l, ci, k):
            return AM[:, l * 2 + ci, k:k + 1]

        def Bsc(l, ci, k):
            return MOD[:, l, 2 * k + ci:2 * k + ci + 1]

        S.barrier(skip_sems=w0_sems)

        def norm_to_hT(Xt, bX, xn, bxn, junk, bjunk, hT_dst, bhT, l, ci, evac_act=False, rs_pre=None, mid=None):
            if rs_pre is not None:
                rs, bs = rs_pre
            else:
                st_, bs = stat()
                ss, rs = st_[:, 0:1], st_[:, 1:2]
                act(junk, Xt, AF.Square, [bX], [bjunk, bs], accum=ss)
                act(rs, ss, AF.Ln, [bs, bPRM], [bs], bias=PRM[:, 0:1], scale=1.0 / 1024)
                act(rs, rs, AF.Exp, [bs], [bs], scale=-0.5)
            tsc(xn, Xt, rs, None, ALU.mult, None, [bX, bs], [bxn])
            for hf in range(2):
                pb, bpb = PSB[hf], bPSB[hf]
                transposes([(pb[:, j * 128:(j + 1) * 128], xn[:, (hf * 4 + j) * 128:(hf * 4 + j + 1) * 128])
                            for j in range(4)], [bxn], [bpb])
            if mid is not None:
                mid()
            for j in range(4):
                for hf in range(2):
                    k = hf * 4 + j
                    pb, bpb = PSB[hf], bPSB[hf]
                    if hf == 0:
                        act(hT_dst[:, k, :], pb[:, j * 128:(j + 1) * 128], AF.Identity, [bpb, bAMl[l], bMODl[l]], [bhT[0]],
                            scale=Asc(l, ci, k), bias=Bsc(l, ci, k))
                    else:
                        tsc(hT_dst[:, k, :], pb[:, j * 128:(j + 1) * 128], Asc(l, ci, k), Bsc(l, ci, k), ALU.mult, ALU.add,
                            [bpb, bAMl[l], bMODl[l]], [bhT[1]])

        def run_streams(gens, width, stagger=0, bgen=None):
            active = []
            it = iter(gens)
            if stagger:
                g0 = next(it, None)
                if g0 is not None:
                    active.append(g0)
                    for _ in range(stagger):
                        try:
                            next(g0)
                        except StopIteration:
                            active.remove(g0)
                            break
            while True:
                while len(active) < width:
                    g = next(it, None)
                    if g is None:
                        break
                    active.append(g)
                if not active:
                    break
                for g in list(active):
                    try:
                        next(g)
                    except StopIteration:
                        active.remove(g)
                if bgen is not None:
                    try:
                        next(bgen)
                    except StopIteration:
                        bgen = None
            if bgen is not None:
                for _ in bgen:
                    pass

        cb.reset()
        cf.reset()
        psf_n[0] = 5
        X_ = [cf.get(f"X{i}", [1024]) for i in range(4)]
        TMP = [cf.get(f"TMP{i}", [512]) for i in range(4)]
        K32 = [cf.get(f"K32{i}", [512]) for i in range(2)]
        V32 = [cf.get(f"V32{i}", [512]) for i in range(2)]
        WFq = [cf.get(f"WFq{i}", [8, 128]) for i in range(2)]
        BGq = [cf.get(f"BGq{i}", [128]) for i in range(2)]
        JUNK, bJUNK = cb.get("junk", [1024])
        XN = [cb.get(f"xn{i}", [1024]) for i in range(2)]
        HT = [cb.get(f"HT{i}", [8, 512]) for i in range(2)]
        HTB = [(Buf("hlo"), Buf("hhi")) for i in range(2)]
        UTs_ = [cb.get("UTs0", [4, 512]), cz.get("UTs1", [4, 512])]
        SGAs_ = [cb.get("SGAs0", [4, 512]), cz.get("SGAs1", [4, 512])]
        QTs_ = [cb.get("QTs0", [4, 512]), cz.get("QTs1", [4, 512])]
        KTs_ = [cb.get("KTs0", [4, 512]), cz.get("KTs1", [4, 512])]
        VST = [cb.get(f"VST{i}", [4, 8, 65]) for i in range(2)]
        SGBS = [cb.get(f"SGBS{i}", [4, 512]) for i in range(2)]
        SQ = [cb.get(f"SQ{i}", [512]) for i in range(2)]
        ZERO, bZERO = cb.get("zero", [4, 16])
        S.add("dve", lambda e: e.memset(ZERO, 0.0), writes=[bZERO])
        for i in range(2):
            S.add("dve", lambda e, i=i: e.memset(VST[i][0][:, :, :, 64:65], 1.0), writes=[VST[i][1]])
        sUv = sU.rearrange("c p t -> p c t")
        for b in range(4):
            dma("pool", sUv[:, :, b * 288:b * 288 + 16], ZERO, [bZERO], [bU])
            dma("pool", sUv[:, :, b * 288 + 272:b * 288 + 288], ZERO, [bZERO], [bU])
        dma("pool", sUv[:, :, UB:UB + 16], ZERO, [bZERO], [bU])

        supertiles = [(0, list(range(0, 4))), (0, list(range(4, 8)))]
        supertiles += [(1, list(range(8 + a, 8 + min(a + 4, NTS)))) for a in range(0, NTS, 4)]
        if build_program.phases < 1:
            supertiles = []
        supertiles = supertiles[:build_program.nst]

        xloaded = set()

        def stA(sti, ci, tiles):
            sl = sti % 2
            nt = len(tiles)
            N = nt * 128
            hT, bhT0 = HT[sl]
            bhT = HTB[sl]
            vst, bvst = VST[sl]
            sgbs, bsgbs = SGBS[sl]
            UTs, bUTs = UTs_[sl]
            SGAs, bSGAs = SGAs_[sl]
            QTs, bQTs = QTs_[sl]
            KTs, bKTs = KTs_[sl]
            def xload(sti_, jj_):
                ci_, tiles_ = supertiles[sti_]
                tg_ = tiles_[jj_]
                Xt_, bX_ = X_[(sti_ % 2) * 2 + jj_ % 2]
                src = xp[tg_ * 128:(tg_ + 1) * 128, :] if ci_ == 0 else xs[(tg_ - 8) * 128:(tg_ - 7) * 128, :]
                dma("sp", Xt_, src, writes=[bX_])
                xloaded.add((sti_, jj_))

            for jj in range(min(2, nt)):
                if (sti, jj) not in xloaded:
                    xload(sti, jj)
            rsn = {}

            def stats_of(j_):
                st_, bs_ = stat()
                Xj, bXj = X_[sl * 2 + j_ % 2]
                act(JUNK, Xj, AF.Square, [bXj], [Buf("j"), bs_], accum=st_[:, 0:1])
                act(st_[:, 1:2], st_[:, 0:1], AF.Ln, [bs_, bPRM], [bs_], bias=PRM[:, 0:1], scale=1.0 / 1024)
                act(st_[:, 1:2], st_[:, 1:2], AF.Exp, [bs_], [bs_], scale=-0.5)
                rsn[j_] = (st_[:, 1:2], bs_)

            stats_of(0)
            for jj, tg in enumerate(tiles):
                Xt, bX = X_[sl * 2 + jj % 2]
                xn, bxn = XN[sl]
                nxt_stats = (lambda j_=jj + 1: stats_of(j_)) if jj + 1 < nt else None
                norm_to_hT(Xt, bX, xn, bxn, JUNK, Buf("j"), hT[:, :, jj * 128:(jj + 1) * 128], bhT, 0, ci,
                           rs_pre=rsn[jj], mid=nxt_stats)
                if jj + 2 < nt:
                    xload(sti, jj + 2)
                yield
            if sti + 2 < len(supertiles):
                for jj in range(min(2, len(supertiles[sti + 2][1]))):
                    xload(sti + 2, jj)

            last = (ci == 1 and tiles[0] - 8 == 16)
            Nab, Nga, Nq, ngb = (144, 128, 128, 1) if last else (N, N, N, nt)

            def fm_group(col0, n):
                pb_, bpb_ = bankf()
                lst = [(pb_[:, :n], W0[:, k, col0:col0 + 128], hT[:, k, :n], k == 0, k == 7) for k in range(8)]
                mms(lst, [bW0g[col0 // 512], *bhT], [bpb_])
                return pb_, bpb_

            for c in range(4):
                pa, bpa = fm_group(c * 128, Nab)
                pbb, bpbb = fm_group(512 + c * 128, Nab)
                t0, bt0 = TMP[sl * 2]
                act(t0[:, :Nab], pbb[:, :Nab], AF.Tanh, [bpbb], [bt0], scale=0.5)
                stt(UTs[:, c, :Nab], t0[:, :Nab], 1.0, pa[:, :Nab], ALU.add, ALU.mult, [bt0, bpa], [bUTs])
                yield
            for c in range(4):
                pg_, bpg_ = fm_group(1024 + c * 128, Nga)
                t0, bt0 = TMP[sl * 2 + 1]
                act(t0[:, :Nga], pg_[:, :Nga], AF.Tanh, [bpg_], [bt0], scale=0.5)
                stt(SGAs[:, c, :Nga], t0[:, :Nga], 1.0, pg_[:, :Nga], ALU.add, ALU.mult, [bt0, bpg_], [bSGAs])
                yield
            for jj, tg in enumerate(tiles):
                hs = hT[:, :, jj * 128:(jj + 1) * 128]

                def tm_group(col0):
                    pb_, bpb_ = bankf()
                    lst = [(pb_[:, :], hs[:, k, :], W0[:, k, col0:col0 + 512], k == 0, k == 7) for k in range(8)]
                    mms(lst, [bW0g[col0 // 512], *bhT], [bpb_])
                    return pb_, bpb_

                pv, bpv = tm_group(2560)
                cpy(vst[:, jj, :, 0:64], pv[:, :].rearrange("p (h d) -> p h d", h=8), [bpv], [bvst])
                if ci == 0:
                    v32, bv32 = V32[sl]
                    cpy(v32, pv[:, :], [bpv], [bv32])
                    dma("pool", vo[tg * 128:(tg + 1) * 128, :], v32, [bv32], [])
                yield
                if jj >= ngb:
                    continue
                pgb, bpgb = tm_group(3072)
                t0, bt0 = TMP[sl * 2 + 1]
                act(t0, pgb[:, :], AF.Tanh, [bpgb], [bt0], scale=0.5)
                stt(sgbs[:, jj, :], t0, 1.0, pgb[:, :], ALU.add, ALU.mult, [bt0, bpgb], [bsgbs])
                yield
            def prompt_k(jj):
                tg = tiles[jj]
                hs = hT[:, :, jj * 128:(jj + 1) * 128]
                pk, bpk = bankf()
                lst = [(pk[:, :], hs[:, k, :], W0[:, k, 2048:2560], k == 0, k == 7) for k in range(8)]
                mms(lst, [bW0g[4], *bhT], [bpk])
                k32, bk32 = K32[sl]
                t1, bt1 = TMP[sl * 2 + 1]
                act(t1, pk[:, :], AF.Square, [bpk], [bt1])
                kss, bs = stat()
                S.add("dve", lambda e, t1=t1, kss=kss: e.tensor_reduce(
                    out=kss, in_=t1.rearrange("p (h d) -> p h d", h=8), axis=mybir.AxisListType.X, op=ALU.add),
                    [bt1], [bs])
                act(kss, kss, AF.Ln, [bs, bPRM], [bs], bias=PRM[:, 1:2])
                act(kss, kss, AF.Exp, [bs], [bs], scale=-0.5)
                tt(k32.rearrange("p (h d) -> p h d", h=8), pk[:, :].rearrange("p (h d) -> p h d", h=8),
                   kss.unsqueeze(2).to_broadcast([128, 8, 64]), ALU.mult, [bpk, bs], [bk32])
                tt(k32.rearrange("p (h d) -> p h d", h=8), k32.rearrange("p (h d) -> p h d", h=8),
                   PRM[:, 16:80].unsqueeze(1).to_broadcast([128, 8, 64]), ALU.mult, [bk32, bPRM], [bk32])
                dma("pool", ko[tg * 128:(tg + 1) * 128, :], k32, [bk32], [])

            for (col, dst, bdst, scl, nn) in ((1536, QTs, bQTs, PRM[:, 2:3], Nq), (2048, KTs, bKTs, PRM[:, 3:4], N)):
                for c in range(4):
                    pq, bpq = fm_group(col + c * 128, nn)
                    sq, bsq = SQ[sl]
                    act(sq[:, :nn], pq[:, :nn], AF.Square, [bpq], [bsq])
                    yield
                    p2, bp2 = bankf()
                    mms([(p2[:, :nn], BDB[:], sq[:, :nn], True, True)], [bBDB, bsq], [bp2])
                    t0, bt0 = TMP[sl * 2]
                    act(t0[:, :nn], p2[:, :nn], AF.Ln, [bp2, bPRM], [bt0], bias=PRM[:, 1:2])
                    act(t0[:, :nn], t0[:, :nn], AF.Exp, [bt0], [bt0], scale=-0.5)
                    stt(dst[:, c, :nn], pq[:, :nn], scl, t0[:, :nn], ALU.mult, ALU.mult, [bpq, bt0, bPRM], [bdst])
                    if ci == 0 and col == 1536 and c < nt:
                        prompt_k(c)
                    yield
            if ci == 0:
                t0c = tiles[0] * 128
                for bb in range(2):
                    b = tiles[0] // 2 + bb
                    dma("pool", sUv[:, :, b * 288 + 16:b * 288 + 272], UTs[:, :, bb * 256:(bb + 1) * 256], [bUTs], [bU])
            else:
                t0c = 1024 + (tiles[0] - 8) * 128
                u0 = UB + 16 + (tiles[0] - 8) * 128
                dma("pool", sUv[:, :, u0:u0 + Nab], UTs[:, :, :Nab], [bUTs], [bU])
            dma("pool", sGA.rearrange("c p t -> p c t")[:, :, t0c:t0c + Nga], SGAs[:, :, :Nga], [bSGAs], [bGA])
            dma("pool", sQ.rearrange("c p t -> p c t")[:, :, t0c:t0c + Nq], QTs[:, :, :Nq], [bQTs], [bQ])
            dma("pool", sK.rearrange("c p t -> p c t")[:, :, t0c:t0c + N], KTs[:, :, :N], [bKTs], [bK])
            g0 = tiles[0]
            dma("pool", sV.rearrange("j p f -> p j f")[:, g0:g0 + nt, :],
                vst[:, 0:nt, :, :].rearrange("p j h d -> p j (h d)"), [bvst], [bV])
            dma("pool", sGB.rearrange("j p f -> p j f")[:, g0:g0 + ngb, :], sgbs[:, 0:ngb, :], [bsgbs], [bGB])
            yield

        run_streams((stA(sti, ci, tiles) for sti, (ci, tiles) in enumerate(supertiles)), 2, stagger=0,
                    bgen=modgen(WFq, BGq))
        psf_n[0] = 4

        S.barrier()
        PHASES_DONE = build_program.phases

        if PHASES_DONE >= 2:
            cz.reset()
            cb.reset()
            cf.reset()
            WO0, bWO0 = cz.get("WO0", [8, 1024])
            ET, bET = cz.get("ET", [8, 13, 128])
            CK, bCK = cz.get("CK", [4, 512])
            CV, bCV = cz.get("CV", [4, 8, 65])
            DG, bDG = cz.get("DG", [124, 128])
            dma("pool", WO0, w_out_0.rearrange("(k p) n -> p k n", p=128), writes=[bWO0])
            dma("pool", CK, ckT.rearrange("(c p) k -> p c k", p=128), writes=[bCK])
            S.add("dve", lambda e: e.memset(CV[:, :, :, 64:65], 1.0), writes=[bCV])
            for j in range(4):
                dma("pool", CV[:, j, :, 0:64], cvv[j * 128:(j + 1) * 128, :].rearrange("p (h d) -> p h d", h=8), writes=[bCV])
            CWH, bCWH = CWHT[:], Buf("CWH")
            tsc(CWH, SM[:, CW:CW + 124], 0.5, None, ALU.mult, None, [bSM], [bCWH])
            bDGc = [Buf(f"DG{c}") for c in range(4)]
            for c in range(4):
                tt(DG[:, c * 31:(c + 1) * 31, :], IDB[:].unsqueeze(1).to_broadcast([128, 31, 128]),
                   CWH[:, c * 31:(c + 1) * 31].unsqueeze(2).to_broadcast([128, 31, 128]),
                   ALU.mult, [bIDB, bCWH], [bDGc[c]], eng=DG_ENG)
            al = [Buf("CY"), Buf("CY2"), Buf("MEAN"), Buf("RSTD")]
            cf.reset()
            CY, _ = cf.get("CY", [4, 512])
            CY2, _ = cf.get("CY2", [4, 512])
            MEAN, _ = cf.get("MEAN", [512])
            RSTD, _ = cf.get("RSTD", [512])
            bCY, bCY2, bMEAN, bRSTD = al
            TN = [cf.get(f"TN{i}", [512]) for i in range(2)]
            XR = [cf.get(f"XR{i}", [1024]) for i in range(2)]
            Y0 = [cf.get(f"Y0{i}", [1024]) for i in range(2)]
            OT, bOT = cf.get("OT", [512])
            UP, bUP = cb.get("UP", [4, 576])
            SGA, bSGA = cb.get("SGA", [4, 512])
            ZT_ = [cb.get(f"ZT{i}", [8, 512]) for i in range(2)]
            QT, bQT = cb.get("QT", [4, 512])
            KW, bKW = cb.get("KW", [4, 1024])
            VW, bVW = cb.get("VW", [8, 8, 65])
            SGB, bSGB = cb.get("SGB", [4, 512])
            PT = [cb.get(f"PT{i}", [9, 128]) for i in range(3)]
            YA, bYA = cb.get("YA", [512])
            SN, bSN = cb.get("SN", [512])

            stB = [(0, [0, 1, 2, 3]), (0, [4, 5, 6, 7])] + [(1, list(range(a, min(a + 4, NTB)))) for a in range(0, NTB, 4)]
            psf_n[0] = 4
            rot[0] = [0, 1, 2, 6]
            psb_n[0] = 1
            pt_i = [0]
            xr_i = [0]

            def Xgen(ci, tiles, zi):
                ZT, bZT = ZT_[zi]
                nt = len(tiles)
                N = nt * 128
                if ci == 0:
                    tcol = tiles[0] * 128
                    seqs = [(0, 256, 0), (256, 256, 288)]
                    b0 = tiles[0] // 2
                    dma("sp", UP[:, :, 0:576], sUv[:, :, b0 * 288:b0 * 288 + 576], [bU], [bUP])
                else:
                    tcol = 1024 + tiles[0] * 128
                    seqs = [(0, N, 0)]
                    u0 = UB + tiles[0] * 128
                    dma("sp", UP[:, :, 0:N + 32], sUv[:, :, u0:u0 + N + 32], [bU], [bUP])
                dma("sp", SGA[:, :, :N], sGA.rearrange("c p t -> p c t")[:, :, tcol:tcol + N], [bGA], [bSGA])
                for c in range(4):
                    pc, bpc = PSF[3], bPSF[3]
                    lst = []
                    for (oc, n, uo) in seqs:
                        for k in range(31):
                            lst.append((pc[:, oc:oc + n], DG[:, c * 31 + k, :], UP[:, c, uo + k + 1:uo + k + 1 + n],
                                        k == 0, k == 30))
                    npc = 4
                    per = (len(lst) + npc - 1) // npc
                    for pi in range(npc):
                        mms(lst[pi * per:(pi + 1) * per], [bDGc[c], bUP], [bpc])
                        if pi == npc - 1:
                            act(CY[:, c, :N], pc[:, :N], AF.Identity, [bpc, bSM], [bCY], bias=SM[:, CB + c:CB + c + 1])
                            act(CY2[:, c, :N], pc[:, :N], AF.Square, [bpc, bSM], [bCY2], bias=SM[:, CB + c:CB + c + 1])
                        yield
                pm, bpm = bankf()
                mms([(pm[:, :N], ON32[:], CY[:, c, :N], c == 0, c == 3) for c in range(4)], [bON, bCY], [bpm])
                cpy(MEAN[:, :N], pm[:, :N], [bpm], [bMEAN])
                yield
                pq2, bpq2 = bankf()
                mms([(pq2[:, :N], ON32[:], CY2[:, c, :N], c == 0, c == 3) for c in range(4)], [bON, bCY2], [bpq2])
                tt(RSTD[:, :N], MEAN[:, :N], MEAN[:, :N], ALU.mult, [bMEAN], [bRSTD])
                tt(RSTD[:, :N], pq2[:, :N], RSTD[:, :N], ALU.subtract, [bpq2, bRSTD], [bRSTD])
                act(RSTD[:, :N], RSTD[:, :N], AF.Ln, [bRSTD, bPRM], [bRSTD], bias=PRM[:, 0:1])
                act(RSTD[:, :N], RSTD[:, :N], AF.Exp, [bRSTD], [bRSTD], scale=-0.5)
                yield
                for c in range(4):
                    tn, btn = TN[c % 2]
                    tt(tn[:, :N], CY[:, c, :N], MEAN[:, :N], ALU.subtract, [bCY, bMEAN], [btn])
                    yield
                    tt(tn[:, :N], tn[:, :N], RSTD[:, :N], ALU.mult, [btn, bRSTD], [btn])
                    yield
                    tsc(tn[:, :N], tn[:, :N], SM[:, LG + c:LG + c + 1], SM[:, LB + c:LB + c + 1], ALU.mult, ALU.add,
                        [btn, bSM], [btn])
                    act(SN[:, :N], tn[:, :N], AF.Tanh, [btn], [bSN], scale=0.5)
                    yield
                    stt(tn[:, :N], SN[:, :N], 1.0, tn[:, :N], ALU.add, ALU.mult, [bSN, btn], [btn])
                    yield
                    stt(ZT[:, c, :N], tn[:, :N], 0.25, SGA[:, c, :N], ALU.mult, ALU.mult, [btn, bSGA], [bZT])
                    yield

            def Yparams(ci, tiles):
                nt = len(tiles)
                if ci == 0:
                    tcol = tiles[0] * 128
                    gt0 = tiles[0]
                    kt0, nkt = tiles[0], 4
                else:
                    tcol = 1024 + tiles[0] * 128
                    gt0 = 8 + tiles[0]
                    kt0 = max(tiles[0] - 2, 0)
                    nkt = min(max(tiles[-1] + 2, 3) + 1, NTS) - kt0
                return nt, nt * 128, tcol, gt0, kt0, nkt

            def Yloads(ci, tiles):
                nt, N, tcol, gt0, kt0, nkt = Yparams(ci, tiles)
                dma("sp", QT[:, :, :N], sQ.rearrange("c p t -> p c t")[:, :, tcol:tcol + N], [bQ], [bQT])
                kcol = (kt0 * 128) if ci == 0 else (1024 + kt0 * 128)
                dma("sp", KW[:, :, :nkt * 128], sK.rearrange("c p t -> p c t")[:, :, kcol:kcol + nkt * 128], [bK], [bKW])
                gk0 = kt0 if ci == 0 else 8 + kt0
                dma("sp", VW[:, 0:nkt, :, :].rearrange("p j h d -> p j (h d)"),
                    sV.rearrange("j p f -> p j f")[:, gk0:gk0 + nkt, :], [bV], [bVW])
                dma("sp", SGB[:, 0:nt, :], sGB.rearrange("j p f -> p j f")[:, gt0:gt0 + nt, :], [bGB], [bSGB])

            def Ygen(ci, tiles, zi, nxt=None):
                ZT, bZT = ZT_[zi]
                nt, N, tcol, gt0, kt0, nkt = Yparams(ci, tiles)
                tinfo = {}
                for jj, t in enumerate(tiles):
                    if ci == 0:
                        bt = (t // 2) * 2
                        wt = [bt - tiles[0], bt - tiles[0] + 1]
                        pat0 = None
                        nctx = 0
                    else:
                        kts = key_tiles(t)
                        wt = [j - kt0 for j in kts]
                        pat0 = pat_index(t, kts[0])
                        nctx = 4
                    tinfo[jj] = (wt, pat0, nctx)
                ob = [(PSF[4], bPSF[4]), (PSF[5], bPSF[5])]
                pts = {}

                def stage1(jj, h):
                    wt, pat0, nctx = tinfo[jj]
                    nw = len(wt)
                    c, hh = h // 2, h % 2
                    pr = slice(64 * hh, 64 * hh + 64)
                    pt, bpt = PT[pt_i[0] % 3]
                    pt_i[0] += 1
                    pts[(jj, h)] = (pt, bpt)
                    qs = QT[pr, c, jj * 128:(jj + 1) * 128]
                    groups = [wt[0:4]] + ([wt[4:]] if nw > 4 else [])
                    so = 0
                    for grp in groups:
                        ps_, bps_ = bankf()
                        lst = [(ps_[:, i * 128:(i + 1) * 128], KW[pr, c, s_ * 128:(s_ + 1) * 128], qs, True, True)
                               for i, s_ in enumerate(grp)]
                        mms(lst, [bKW, bQT], [bps_])
                        act(pt[:, so:so + len(grp), :].rearrange("p a b -> p (a b)"), ps_[:, :len(grp) * 128],
                            AF.Exp, [bps_], [bpt])
                        so += len(grp)
                    if pat0 is not None:
                        tt(pt[:, 0:nw, :], pt[:, 0:nw, :], ET[:, h, pat0:pat0 + nw, :], ALU.mult, [bpt, bETh[h]], [bpt])
                    if nctx:
                        ps_, bps_ = bankf()
                        lst = [(ps_[:, i * 128:(i + 1) * 128], CK[pr, c, i * 128:(i + 1) * 128], qs, True, True)
                               for i in range(4)]
                        mms(lst, [bCK, bQT], [bps_])
                        act(pt[:, nw:nw + 4, :].rearrange("p a b -> p (a b)"), ps_[:, :], AF.Exp, [bps_], [bpt])

                def stage2(jj, h):
                    wt, pat0, nctx = tinfo[jj]
                    nw = len(wt)
                    tot = nw + nctx
                    pt, bpt = pts[(jj, h)]
                    po, bpo = ob[h // 4]
                    oo = (h % 4) * 65
                    lst = []
                    for i, s_ in enumerate(wt):
                        lst.append((po[:, oo:oo + 65], pt[:, i, :], VW[:, s_, h, :], i == 0, i == tot - 1))
                    for i in range(nctx):
                        lst.append((po[:, oo:oo + 65], pt[:, nw + i, :], CV[:, i, h, :], False, nw + i == tot - 1))
                    mms(lst, [bpt, bVW, bCV], [bpo])

                def tail1(jj):
                    for hb in range(2):
                        po, bpo = ob[hb]
                        pov = po[:, 0:260].rearrange("p (h d) -> p h d", h=4)
                        st_, bs = stat()
                        rd = st_[:, 0:4]
                        S.add("dve", lambda e, rd=rd, pov=pov: e.reciprocal(out=rd, in_=pov[:, :, 64]), [bpo], [bs])
                        otv = OT[:, hb * 256:(hb + 1) * 256].rearrange("p (h d) -> p h d", h=4)
                        stt(otv, pov[:, :, 0:64], 0.5, rd.unsqueeze(2).to_broadcast([128, 4, 64]), ALU.mult, ALU.mult,
                            [bpo, bs], [bOT])
                    tt(YA, OT, SGB[:, jj, :], ALU.mult, [bOT, bSGB], [bYA])

                def tail2(jj):
                    pb, bpb = bankb()
                    transposes([(pb[:, k * 128:(k + 1) * 128], YA[:, k * 128:(k + 1) * 128]) for k in range(4)],
                               [bYA], [bpb])
                    cpy(ZT[:, 4:8, jj * 128:(jj + 1) * 128], pb[:, 0:512].rearrange("p (a b) -> p a b", a=4),
                        [bpb], [bZT])

                items = [(jj, h) for jj in range(nt) for h in range(8)]
                stage1(*items[0])
                stage1(*items[1])
                pend = None
                for i_, (jj, h) in enumerate(items):
                    if i_ + 2 < len(items):
                        stage1(*items[i_ + 2])
                    stage2(jj, h)
                    if pend is not None and h == 1:
                        tail2(pend)
                        pend = None
                    if h == 7:
                        tail1(jj)
                        pend = jj
                    yield
                if pend is not None:
                    tail2(pend)
                if nxt is not None:
                    Yloads(*nxt)
                yield
                for jj, t in enumerate(tiles):
                    xr, bxr = XR[xr_i[0] % 2]
                    y0, by0 = Y0[xr_i[0] % 2]
                    xr_i[0] += 1
                    src = xp[t * 128:(t + 1) * 128, :] if ci == 0 else xs[t * 128:(t + 1) * 128, :]
                    dma("sp", xr, src, writes=[bxr])
                    for n in range(2):
                        po, bpo = bankf()
                        lst = [(po[:, :], ZT[:, k, jj * 128:(jj + 1) * 128], WO0[:, k, n * 512:(n + 1) * 512], k == 0, k == 7)
                               for k in range(8)]
                        mms(lst, [bZT, bWO0], [bpo])
                        tt(y0[:, n * 512:(n + 1) * 512], po[:, :], G[:, ci, n * 512:(n + 1) * 512], ALU.mult, [bpo, bGl[0]], [by0])
                        yield
                    tt(y0, y0, xr, ALU.add, [by0, bxr], [by0])
                    gt = t if ci == 0 else 8 + t
                    dma("pool", sY0[gt], y0, [by0], [bY0])
                    st_, bs = stat()
                    act(xr, y0, AF.Square, [by0], [bxr, bs], accum=st_[:, 0:1])
                    act(RS1[:, gt:gt + 1], st_[:, 0:1], AF.Ln, [bs, bPRM], [bRS1[gt]], bias=PRM[:, 0:1], scale=1.0 / 1024)
                    act(RS1[:, gt:gt + 1], RS1[:, gt:gt + 1], AF.Exp, [bRS1[gt]], [bRS1[gt]], scale=-0.5)

            def zip2(g1, g2):
                gs = [g for g in (g1, g2) if g is not None]
                while gs:
                    for g in list(gs):
                        try:
                            next(g)
                        except StopIteration:
                            gs.remove(g)

            bETh = [Buf(f"ET{h}") for h in range(8)]

            def build_ET():
                for h in range(8):
                    dma("pool", ET[:, h, :, :].rearrange("p a b -> p (a b)"), ebias[:, h * 1664:(h + 1) * 1664],
                        writes=[bETh[h]])
                yield
                for h in range(8):
                    eh = ET[:, h, :, :].rearrange("p a b -> p (a b)")
                    act(eh, eh, AF.Exp, [bETh[h]], [bETh[h]])
                    yield

            def wzip(gws):
                alive = [True] * len(gws)
                while alive[0]:
                    for gi, (g, reps) in enumerate(gws):
                        for _ in range(reps):
                            if alive[gi]:
                                try:
                                    next(g)
                                except StopIteration:
                                    alive[gi] = False
                for gi, (g, reps) in enumerate(gws):
                    if alive[gi]:
                        for _ in g:
                            pass

            W1OFF = 25632
            W1 = ZB[:, W1OFF:W1OFF + 16384].rearrange("p (k n) -> p k n", k=8)
            bW1 = Buf("W1")
            w1v = w_in_1.rearrange("(k p) n -> p k n", p=128)

            def Xchain():
                for si, (ci, tiles) in enumerate(stB):
                    yield from Xgen(ci, tiles, si % 2)
                for k in range(0, 8, 2):
                    dma("pool", W1[:, k:k + 2, :], w1v[:, k:k + 2, :], writes=[bW1] + bDGc)
                yield

            def Ychain():
                for si, (ci, tiles) in enumerate(stB):
                    yield from Ygen(ci, tiles, si % 2, nxt=(stB[si + 1] if si + 1 < len(stB) else None))

            Yloads(*stB[0])
            xc, yc, ec = Xchain(), Ychain(), build_ET()
            alive = {"x": True, "y": True, "e": True}

            def stepg(name, g):
                if alive[name]:
                    try:
                        next(g)
                    except StopIteration:
                        alive[name] = False

            for i_ in range(16):
                stepg("x", xc)
                if i_ % 4 == 3:
                    stepg("e", ec)
            while alive["x"] or alive["y"] or alive["e"]:
                stepg("y", yc)
                stepg("x", xc)
                stepg("e", ec)
            psf_n[0] = 4
            rot[0] = [0, 1, 2, 3, 4, 5]
            psb_n[0] = 2
            S.barrier()

        if PHASES_DONE >= 3:
            cz.reset()
            cb.reset()
            cf.reset()
            PW, bPW = cz.get("PW", [8, 256])
            WO1, bWO1 = cz.get("WO1", [8, 1024])
            BND, bBND = cz.get("BND", [32, 128])
            assert cz.off <= W1OFF
            dma("pool", PW, pool_w.rearrange("g (kc p) e -> p (g kc) e", p=128), writes=[bPW])
            dma("pool", WO1, w_out_1.rearrange("(k p) n -> p k n", p=128), writes=[bWO1])
            dma("pool", BND, band.rearrange("p (a b) -> p a b", a=32), writes=[bBND])
            JUNK, bJUNK = cb.get("junk", [1024])
            NSTR = 2
            NYR = 4
            YR = [[cf.get(f"YR{q}_{i}", [1024]) for i in range(NYR)] for q in range(NSTR)]
            TG = [[cb.get(f"TG{q}_{i}", [512]) for i in range(2)] for q in range(NSTR)]
            TO = [[cf.get(f"TO{q}_{i}", [512]) for i in range(2)] for q in range(NSTR)]
            XN = [cb.get(f"xn{q}", [1024]) for q in range(NSTR)]
            H1 = [cb.get(f"H1{q}", [8, 128]) for q in range(NSTR)]
            H1B = [(Buf("h1lo"), Buf("h1hi")) for q in range(NSTR)]
            U1 = [[cb.get(f"U1{q}_{i}", [1024]) for i in range(4)] for q in range(NSTR)]
            SG1 = [[cb.get(f"SG1{q}_{i}", [8, 128]) for i in range(3)] for q in range(NSTR)]
            DT = [cb.get(f"DT{q}", [8, 128]) for q in range(NSTR)]
            Z1 = [cb.get(f"Z1{q}", [8, 128]) for q in range(NSTR)]

            def zipn(gs):
                gs = [g for g in gs if g is not None]
                while gs:
                    for g in list(gs):
                        try:
                            next(g)
                        except StopIteration:
                            gs.remove(g)
                    yield

            def seq_stream(q, seqs):
                cnt = [0]
                for (ci, tl, nout) in seqs:
                    info = {}

                    def C1(t, ci=ci, info=info):
                        i = cnt[0]
                        cnt[0] += 1
                        info[t] = i
                        yr, byr = YR[q][i % NYR]
                        xn, bxn = XN[q]
                        h1, _ = H1[q]
                        bh1 = H1B[q]
                        u1, bu1 = U1[q][i % 4]
                        sg1, bsg1 = SG1[q][i % 3]
                        gt = t if ci == 0 else 8 + t
                        dma("sp", yr, sY0[gt], [bY0], [byr])
                        norm_to_hT(yr, byr, xn, bxn, JUNK, Buf("j"), h1, bh1, 1, ci, evac_act=True,
                                   rs_pre=(RS1[:, gt:gt + 1], bRS1[gt]))
                        yield
                        for n in range(2):
                            po, bpo = bankf()
                            lst = [(po[:, :], h1[:, k, :], W1[:, k, n * 512:(n + 1) * 512], k == 0, k == 7) for k in range(8)]
                            mms(lst, [*bh1, bW1], [bpo])
                            act(u1[:, n * 512:(n + 1) * 512], po[:, :], AF.Copy, [bpo], [bu1])
                            yield
                        for hf in range(2):
                            po, bpo = bankf()
                            lst = []
                            for bq in range(4):
                                col = 1024 + (hf * 4 + bq) * 128
                                for k in range(8):
                                    lst.append((po[:, bq * 128:(bq + 1) * 128], W1[:, k, col:col + 128], h1[:, k, :], k == 0, k == 7))
                            mms(lst, [*bh1, bW1], [bpo])
                            tg, btg = TG[q][hf]
                            act(tg, po[:, :], AF.Tanh, [bpo], [btg], scale=0.5)
                            stt(sg1[:, hf * 4:(hf + 1) * 4, :].rearrange("p a b -> p (a b)"), tg, 1.0, po[:, :], ALU.add, ALU.mult,
                                [btg, bpo], [bsg1])
                            yield

                    def C2(t, first, last_true, ci=ci, info=info):
                        i = info[t]
                        yr, byr = YR[q][i % NYR]
                        sg1, bsg1 = SG1[q][i % 3]
                        dt, bdt = DT[q]
                        z1, bz1 = Z1[q]
                        nbs = []
                        base = 0 if ci == 0 else 16
                        if not first:
                            nbs.append(((i - 1) % 4, base + 0))
                        nbs.append((i % 4, base + (4 if first else 8)))
                        if not last_true:
                            nbs.append(((i + 1) % 4, base + 12))
                        for hf in range(2):
                            po, bpo = bankf()
                            lst = []
                            for bq in range(4):
                                cc = hf * 4 + bq
                                g = cc // 2
                                for j_, (sl_, bi) in enumerate(nbs):
                                    lst.append((po[:, bq * 128:(bq + 1) * 128], U1[q][sl_][0][:, cc * 128:(cc + 1) * 128],
                                                BND[:, bi + g, :], j_ == 0, j_ == len(nbs) - 1))
                            rd = [U1[q][sl_][1] for (sl_, bi) in nbs]
                            mms(lst, rd + [bBND], [bpo])
                            act(dt[:, hf * 4:(hf + 1) * 4, :].rearrange("p a b -> p (a b)"), po[:, :], AF.Copy, [bpo], [bdt])
                            yield
                        for hf in range(2):
                            po, bpo = bankf()
                            lst = []
                            for bq in range(4):
                                ob_ = hf * 4 + bq
                                g, eb = ob_ // 2, ob_ % 2
                                for kc in range(2):
                                    lst.append((po[:, bq * 128:(bq + 1) * 128], PW[:, g * 2 + kc, eb * 128:(eb + 1) * 128],
                                                dt[:, g * 2 + kc, :], kc == 0, kc == 1))
                            mms(lst, [bPW, bdt], [bpo])
                            for bq in range(4):
                                ob_ = hf * 4 + bq
                                stt(z1[:, ob_, :], po[:, bq * 128:(bq + 1) * 128], PRM[:, 8 + ob_:9 + ob_], sg1[:, ob_, :],
                                    ALU.mult, ALU.mult, [bpo, bPRM, bsg1], [bz1])
                            yield
                        for n in range(2):
                            to, bto = TO[q][n]
                            po, bpo = bankf()
                            lst = [(po[:, :], z1[:, k, :], WO1[:, k, n * 512:(n + 1) * 512], k == 0, k == 7) for k in range(8)]
                            mms(lst, [bz1, bWO1], [bpo])
                            tt(to, po[:, :], G[:, 2 + ci, n * 512:(n + 1) * 512], ALU.mult, [bpo, bGl[1]], [bto])
                            tt(yr[:, n * 512:(n + 1) * 512], to, yr[:, n * 512:(n + 1) * 512], ALU.add, [bto, byr], [byr],
                               eng="pool")
                            yield
                        dst = yp[t * 128:(t + 1) * 128, :] if ci == 0 else ys[t * 128:(t + 1) * 128, :]
                        dma("pool", dst, yr, [byr], [])

                    yield from zipn([C1(tl[0])])
                    if len(tl) > 1:
                        yield from zipn([C1(tl[1])])
                    for i_, t in enumerate(tl):
                        g1 = C1(tl[i_ + 2]) if i_ + 2 < len(tl) else None
                        g2 = C2(t, i_ == 0, (ci == 0 and i_ == len(tl) - 1)) if i_ < nout else None
                        yield from zipn([g1, g2])

            seqP = [(0, [2 * b, 2 * b + 1], 2) for b in range(4)]
            seqS = [(1, list(range(NTB)), 16)]
            gS, gP = seq_stream(0, seqS), seq_stream(1, seqP)
            alive = [True, True]
            while any(alive):
                for gi, (g, reps) in enumerate(((gS, 1), (gP, 1))):
                    for _ in range(reps):
                        if alive[gi]:
                            try:
                                next(g)
                            except StopIteration:
                                alive[gi] = False

        S.finish()
        S.emit(block, sems, dsems)
    return nc


build_program.phases = 3
build_program.nst = 99
DG_ENG = "dve"


def _att_tables(rpb, flip):
    pats = [(10, 10 + d) for d in (-2, -1, 0, 1, 2)] + [(0, j) for j in range(4)] + [(1, j) for j in range(4)]
    idx = np.arange(128)
    bias = np.zeros((8, 13, 128, 128), np.float32)
    mask = np.zeros((13, 128, 128), np.float32)
    for pi, (t, j) in enumerate(pats):
        ik = 2 * j + idx // 64
        ckl = idx % 64
        iq = 2 * t + idx // 64
        cql = idx % 64
        if flip:
            rk, ck_, rq, cq = 63 - ik, 63 - ckl, 63 - iq, 63 - cql
        else:
            rk, ck_, rq, cq = ik, ckl, iq, cql
        RK, RQ = rk[:, None], rq[None, :]
        CK_, CQ = ck_[:, None], cq[None, :]
        start = np.clip(RQ - 4, 0, 56)
        rowok = (RK >= start) & (RK < start + 8)
        qcs = np.clip(CQ - 8, 0, 48)
        colok = (CK_ >= qcs) & (CK_ < qcs + 16)
        dr = np.clip(RK - RQ + 7, 0, 14)
        dc = np.clip(CK_ - CQ + 15, 0, 30)
        mask[pi] = (rowok & colok).astype(np.float32)
        dr = np.broadcast_to(dr, (128, 128))
        dc = np.broadcast_to(dc, (128, 128))
        bias[:, pi] = rpb[:, dr, dc]
    bias = np.where(mask[None] > 0, bias, np.float32(-30000.0)).astype(np.float32)
    eb = np.ascontiguousarray(bias.transpose(2, 0, 1, 3)).reshape(128, 8 * 13 * 128)
    em = np.ascontiguousarray(mask.transpose(1, 0, 2)).reshape(128, 13 * 128)
    return eb, em


def _band_tables2(flip):
    out = np.zeros((32, 128, 128), np.float32)
    wins = (2, 4, 8, 16)
    for sset in range(2):
        fl = flip
        for g, w in enumerate(wins):
            lo, hi = -(w // 2), w - w // 2
            if fl:
                lo, hi = -(w - w // 2) + 1, w // 2 + 1
            for role in range(4):
                M = np.zeros((128, 128), np.float32)
                for t in range(128):
                    a, b = t + lo, t + hi
                    ca, cbb = a, b
                    if role == 1:
                        ca = max(a, 0)
                    if role == 2 and sset == 0:
                        cbb = min(b, 128)
                    if role in (0, 3):
                        cnt = w
                    else:
                        cnt = cbb - ca
                    for s in range(ca, cbb):
                        if role == 0 and s < 0:
                            M[s + 128, t] += 1.0 / cnt
                        elif role == 3 and s >= 128:
                            M[s - 128, t] += 1.0 / cnt
                        elif role in (1, 2) and 0 <= s < 128:
                            M[s, t] += 1.0 / cnt
                    if role in (1, 2):
                        M[t, t] -= 1.0
                out[sset * 16 + role * 4 + g] = M
    return np.ascontiguousarray(out.transpose(1, 0, 2)).reshape(128, 32 * 128)


_CACHE = {}


def kernel(x_prompt, x_sample, cache_k_0, cache_v_0, c, c_ctx,
           norm_g_0, w_ada_0, b_ada_0, w_in_0, conv_w_0, conv_b_0, conv_ln_g_0, conv_ln_b_0,
           q_norm_0, k_norm_0, rpb_0, w_out_0,
           norm_g_1, w_ada_1, b_ada_1, w_in_1, pool_w_1, pool_scale_1, w_out_1):
    f = lambda a: np.ascontiguousarray(np.asarray(a, dtype=np.float32))
    x_prompt, x_sample, cache_k_0, cache_v_0, c, c_ctx = map(f, (x_prompt, x_sample, cache_k_0, cache_v_0, c, c_ctx))
    if "nc" not in _CACHE:
        _CACHE["nc"] = build_program()
    nc = _CACHE["nc"]
    fm = lambda v, nb: f(v).reshape(nb, 128).T
    shared = dict(w_ada_0=f(w_ada_0), w_ada_1=f(w_ada_1), w_in_0=f(w_in_0), w_out_0=f(w_out_0), w_in_1=f(w_in_1),
                  pool_w=f(pool_w_1), w_out_1=f(w_out_1), ident=np.eye(128, dtype=np.float32),
                  bdiag=np.kron(np.eye(2, dtype=np.float32), np.ones((64, 64), np.float32)),
                  ones32=np.full((128, 128), 1.0 / 512, np.float32))
    bgr = np.concatenate([np.broadcast_to(f(b_ada_0)[2048:], (128, 1024)),
                          np.broadcast_to(f(b_ada_1)[2048:], (128, 1024))], axis=1)
    shared["bg"] = np.ascontiguousarray(bgr)
    tabs = {fl: (_att_tables(f(rpb_0), fl), _band_tables2(fl)) for fl in (False, True)}
    in_maps = []
    for i in range(8):
        b, half = i // 2, i % 2
        flip = half == 1
        sm = np.zeros((128, NSM), np.float32)
        sm[:, NG0:NG0 + 8] = fm(norm_g_0, 8)
        sm[:, NG1:NG1 + 8] = fm(norm_g_1, 8)
        sm[:, BA0:BA0 + 24] = fm(b_ada_0, 24)
        sm[:, BA1:BA1 + 24] = fm(b_ada_1, 24)
        cw = f(conv_w_0)[::-1] if flip else f(conv_w_0)
        sm[:, CW:CW + 124] = cw.T.reshape(4, 128, 31).transpose(1, 0, 2).reshape(128, 124)
        sm[:, CB:CB + 4] = fm(conv_b_0, 4)
        sm[:, LG:LG + 4] = fm(conv_ln_g_0, 4)
        sm[:, LB:LB + 4] = fm(conv_ln_b_0, 4)
        sm[:, QN] = np.tile(f(q_norm_0), 2)
        sm[:, KN] = np.tile(f(k_norm_0), 2)
        sm[:, PSC:PSC + 8] = fm(pool_scale_1, 8)
        cv2 = np.stack([fm(c_ctx, 8), fm(c[b], 8)], axis=2)
        sm[:, CVEC:CVEC + 16] = cv2.reshape(128, 16)
        sm[:, KNR:KNR + 64] = np.broadcast_to(f(k_norm_0), (128, 64))
        xsb = x_sample[b][::-1] if flip else x_sample[b]
        (eb, em), bnd = tabs[flip]
        m = dict(shared)
        xpb = x_prompt[4 * i:4 * i + 4][:, ::-1] if flip else x_prompt[4 * i:4 * i + 4]
        m.update(xp=np.ascontiguousarray(xpb).reshape(1024, 1024),
                 xs=np.ascontiguousarray(xsb[:TS]),
                 ckT=np.ascontiguousarray(cache_k_0[b].reshape(512, 512).T),
                 cvv=np.ascontiguousarray(cache_v_0[b].reshape(512, 512)),
                 smallp=sm, ebias=eb, emask=em, band=bnd)
        in_maps.append(m)
    if _CACHE.get("in_maps_only"):
        return in_maps
    res = run_bass_kernel_spmd(nc, in_maps, core_ids=list(range(8)))
    y_prompt = np.zeros((32, 256, 1024), np.float32)
    y_sample = np.zeros((4, 4096, 1024), np.float32)
    k_ctx = np.zeros((32, 256, 8, 64), np.float32)
    v_ctx = np.zeros((32, 256, 8, 64), np.float32)
    for i in range(8):
        r = res.results[i]
        b, half = i // 2, i % 2
        sl = slice(None, None, -1) if half == 1 else slice(None)
        y_prompt[4 * i:4 * i + 4] = r["yp"].reshape(4, 256, 1024)[:, sl]
        k_ctx[4 * i:4 * i + 4] = r["ko"].reshape(4, 256, 8, 64)[:, sl]
        v_ctx[4 * i:4 * i + 4] = r["vo"].reshape(4, 256, 8, 64)[:, sl]
        if half == 0:
            y_sample[b, :2048] = r["ys"]
        else:
            y_sample[b, 2048:] = r["ys"][::-1]
    return (y_prompt, y_sample, k_ctx, v_ctx)
```

```python
import numpy as np
from contextlib import ExitStack
import concourse.bass as bass
import concourse.mybir as mybir
from concourse.bass_utils import run_bass_kernel_spmd

F32 = mybir.dt.float32
BF16 = mybir.dt.bfloat16
ALU = mybir.AluOpType
AF = mybir.ActivationFunctionType
EPS = 1e-6
NTS = 19
NTB = 17
TS = NTS * 128
TTOT = 1024 + TS
UB = 4 * 288
ULEN = UB + 16 + TS + 16
NG0, NG1, BA0, BA1, CW, CB, LG, LB, QN, KN, PSC, CVEC, KNR, NSM = 0, 8, 16, 40, 64, 188, 192, 196, 200, 201, 202, 210, 226, 290


class Buf:
    __slots__ = ("name", "lw", "rs", "excl")

    def __init__(self, name, excl=False):
        self.name = name
        self.excl = excl
        self.lw = None
        self.rs = []


class Sched:
    ENG = ["pe", "act", "dve", "pool", "sp"]

    def __init__(self, ndma=24):
        self.ops = {e: [] for e in self.ENG}
        self.ndma = ndma
        self.dma_count = [0] * ndma
        self.dma_next = [0, 0]
        self.bar = {e: [] for e in self.ENG}

    def add(self, eng, fn, reads=(), writes=(), dma=False):
        deps = set(self.bar[eng])
        self.bar[eng] = []
        idx = len(self.ops[eng])
        if dma:
            half = self.ndma // 2
            qi = 0 if eng == "sp" else 1
            k = qi * half + self.dma_next[qi]
            self.dma_next[qi] = (self.dma_next[qi] + 1) % half
            if self.dma_count[k] > 0:
                deps.add(("dma", k, 16 * self.dma_count[k]))
            self.dma_count[k] += 1
            ref = ("dma", k, 16 * self.dma_count[k])
        else:
            ref = ("op", eng, idx)
        excl_reads = [b for b in reads if b.excl]
        if excl_reads:
            reads = [b for b in reads if not b.excl]
            writes = list(writes) + excl_reads
        for b in reads:
            if b.lw is not None:
                deps.add(b.lw)
        for b in writes:
            if b.lw is not None:
                deps.add(b.lw)
            deps.update(b.rs)
        for b in reads:
            b.rs.append(ref)
        for b in writes:
            b.lw = ref
            b.rs = []
        deps.discard(ref)
        if eng == "pe":
            deps = {d for d in deps if not (d[0] == "op" and d[1] == "pe")}
        self.ops[eng].append(dict(fn=fn, deps=deps, ref=ref, dma=dma))
        return ref

    def barrier(self, skip_sems=()):
        refs = []
        for e in self.ENG:
            for op in reversed(self.ops[e]):
                if op["fn"] is not None and not op["dma"]:
                    refs.append(op["ref"])
                    break
        for k in range(self.ndma):
            if self.dma_count[k] > 0 and k not in skip_sems:
                refs.append(("dma", k, 16 * self.dma_count[k]))
        for e in self.ENG:
            self.bar[e] = list(refs)

    def finish(self):
        self.barrier()
        for e in self.ENG:
            self.ops[e].append(dict(fn=None, deps=set(self.bar[e]), ref=None, dma=False))
            self.bar[e] = []

    def emit(self, block, sems, dma_sems):
        need = {e: set() for e in self.ENG}
        for e in self.ENG:
            for op in self.ops[e]:
                for d in op["deps"]:
                    if d[0] == "op":
                        need[d[1]].add(d[2])
        count = {}
        for e in self.ENG:
            count[e] = {}
            c = 0
            for idx in sorted(need[e]):
                c += 1
                count[e][idx] = c

        def run(e):
            def f(eng):
                waited = {}
                for idx, op in enumerate(self.ops[e]):
                    for d in sorted(op["deps"], key=str):
                        if d[0] == "op":
                            key = ("op", d[1])
                            val = count[d[1]][d[2]]
                            sem = sems[d[1]]
                        else:
                            key = ("dma", d[1])
                            val = d[2]
                            sem = dma_sems[d[1]]
                        if waited.get(key, 0) >= val:
                            continue
                        eng.wait_ge(sem, val)
                        waited[key] = val
                    if op["fn"] is None:
                        continue
                    ins = op["fn"](eng)
                    if op["dma"]:
                        ins.then_inc(dma_sems[op["ref"][1]], 16)
                    elif idx in count[e]:
                        ins.then_inc(sems[e], 1)
            return f

        block.tensor(run("pe"))
        block.scalar(run("act"))
        block.vector(run("dve"))
        block.gpsimd(run("pool"))
        block.sync(run("sp"))


def pat_index(t, j):
    if t >= 2:
        return j - t + 2
    return 5 + 4 * t + j


def key_tiles(t):
    return list(range(max(t - 2, 0), max(t + 2, 3) + 1))


def build_program():
    nc = bass.Bass("TRN2", target_bir_lowering=False)
    S = Sched()

    def din(name, shape):
        return nc.dram_tensor(name, list(shape), F32, kind="ExternalInput").ap()

    def dout(name, shape):
        return nc.dram_tensor(name, list(shape), F32, kind="ExternalOutput").ap()

    xp = din("xp", [1024, 1024])
    xs = din("xs", [TS, 1024])
    ckT = din("ckT", [512, 512])
    cvv = din("cvv", [512, 512])
    smallp = din("smallp", [128, NSM])
    bg = din("bg", [128, 2048])
    w_ada = [din("w_ada_0", [1024, 3072]), din("w_ada_1", [1024, 3072])]
    w_in_0 = din("w_in_0", [1024, 3584])
    w_out_0 = din("w_out_0", [1024, 1024])
    w_in_1 = din("w_in_1", [1024, 2048])
    pool_w = din("pool_w", [4, 256, 256])
    w_out_1 = din("w_out_1", [1024, 1024])
    ebias = din("ebias", [128, 8 * 13 * 128])
    emask = din("emask", [128, 13 * 128])
    band = din("band", [128, 32 * 128])
    ident = din("ident", [128, 128])
    bdiag = din("bdiag", [128, 128])
    ones32 = din("ones32", [128, 128])
    yp = dout("yp", [1024, 1024])
    ys = dout("ys", [2048, 1024])
    ko = dout("ko", [1024, 512])
    vo = dout("vo", [1024, 512])
    sU = nc.dram_tensor("sU", [4, 128, ULEN], BF16).ap()
    sGA = nc.dram_tensor("sGA", [4, 128, TTOT], BF16).ap()
    sQ = nc.dram_tensor("sQ", [4, 128, TTOT], BF16).ap()
    sK = nc.dram_tensor("sK", [4, 128, TTOT], BF16).ap()
    sV = nc.dram_tensor("sV", [27, 128, 520], BF16).ap()
    sGB = nc.dram_tensor("sGB", [27, 128, 512], BF16).ap()
    sY0 = nc.dram_tensor("sY0", [25, 128, 1024], F32).ap()
    bU, bGA, bQ, bK, bV, bGB, bY0 = (Buf(n) for n in ["sU", "sGA", "sQ", "sK", "sV", "sGB", "sY0"])

    with ExitStack() as es:
        ZB = es.enter_context(nc.sbuf_tensor("ZB", [128, 43008], BF16))
        WKB = es.enter_context(nc.sbuf_tensor("WKB", [128, 29440], BF16))
        WK = es.enter_context(nc.sbuf_tensor("WK", [128, 10752], F32))
        SM = es.enter_context(nc.sbuf_tensor("SM", [128, NSM], F32))
        G = es.enter_context(nc.sbuf_tensor("G", [128, 4, 1024], F32))
        MOD = es.enter_context(nc.sbuf_tensor("MOD", [128, 2, 32], F32))
        AM = es.enter_context(nc.sbuf_tensor("AM", [128, 4, 8], F32))
        IDB = es.enter_context(nc.sbuf_tensor("IDB", [128, 128], BF16))
        BDB = es.enter_context(nc.sbuf_tensor("BDB", [128, 128], BF16))
        ON32 = es.enter_context(nc.sbuf_tensor("ON32", [128, 128], F32))
        SC = es.enter_context(nc.sbuf_tensor("SC", [128, 16], F32))
        ST = es.enter_context(nc.sbuf_tensor("ST", [128, 64], F32))
        PRM = es.enter_context(nc.sbuf_tensor("PRM", [128, 80], F32))
        CWHT = es.enter_context(nc.sbuf_tensor("CWHT", [128, 124], F32))
        RS1 = es.enter_context(nc.sbuf_tensor("RS1", [128, 32], F32))
        bRS1 = [Buf(f"rs1_{i}") for i in range(32)]
        PSF = [es.enter_context(nc.psum_tensor(f"psf{i}", [128, 512], F32)) for i in range(7)]
        PSB0 = es.enter_context(nc.psum_tensor("psb0", [128, 1024], BF16))
        bPSF = [Buf(f"psf{i}", True) for i in range(7)]
        PSB = [PSB0, PSF[6][:, :].bitcast(BF16)]
        bPSB = [Buf("psb0", True), bPSF[6]]
        sems = {e: es.enter_context(nc.semaphore("s_" + e)) for e in S.ENG}
        dsems = [es.enter_context(nc.semaphore(f"d{i}")) for i in range(S.ndma)]
        block = es.enter_context(nc.Block())

        bSM, bG, bMOD, bAM, bIDB, bBDB, bON, bSC, bPRM = (Buf(n) for n in
                                                             ["SM", "G", "MOD", "AM", "IDB", "BDB", "ON", "SC", "PRM"])
        st_ctr = [0]
        stat_bufs = [Buf(f"stc{i}") for i in range(8)]
        TC = es.enter_context(nc.sbuf_tensor("TC", [128, 16], F32))

        def stat():
            s_ = st_ctr[0] % 8
            st_ctr[0] += 1
            return ST[:, 8 * s_:8 * s_ + 8], stat_bufs[s_]

        psf_i = [0]
        psb_i = [0]
        psf_n = [4]
        psb_n = [2]
        rot = [[0, 1, 2, 3, 4, 5]]

        def bankf():
            i = psf_i[0]
            i = i % psf_n[0]
            psf_i[0] = (i + 1) % psf_n[0]
            i = rot[0][i]
            return PSF[i], bPSF[i]

        def bankb():
            i = psb_i[0] % psb_n[0]
            psb_i[0] = (i + 1) % psb_n[0]
            return PSB[i], bPSB[i]

        class Carver:
            def __init__(self, t, size):
                self.t, self.size, self.off = t, size, 0

            def reset(self):
                self.off = 0

            def get(self, name, shape):
                n = int(np.prod(shape))
                assert self.off + n <= self.size, (name, self.off, n, self.size)
                v = self.t[:, self.off:self.off + n]
                self.off += n
                if len(shape) == 2:
                    v = v.rearrange("p (a b) -> p a b", a=shape[0])
                elif len(shape) == 3:
                    v = v.rearrange("p (a b c) -> p a b c", a=shape[0], b=shape[1])
                return v, Buf(name)

        cz, cb, cf = Carver(ZB, 43008), Carver(WKB, 29440), Carver(WK, 10752)

        def dma(q, out, in_, reads=(), writes=()):
            return S.add(q, lambda e: e.dma_start(out=out, in_=in_), reads, writes, dma=True)

        def act(out, in_, func, reads, writes, bias=None, scale=None, accum=None):
            def fn(e):
                kw = {}
                if bias is not None:
                    kw["bias"] = bias
                if scale is not None:
                    kw["scale"] = scale
                if accum is not None:
                    kw["accum_out"] = accum
                return e.activation(out=out, in_=in_, func=func, **kw)
            return S.add("act", fn, reads, writes)

        def tt(out, in0, in1, op, reads, writes, eng="dve"):
            return S.add(eng, lambda e: e.tensor_tensor(out=out, in0=in0, in1=in1, op=op), reads, writes)

        def tsc(out, in0, s1, s2, op0, op1, reads, writes, eng="dve"):
            if s2 is None:
                return S.add(eng, lambda e: e.tensor_scalar(out=out, in0=in0, scalar1=s1, scalar2=None, op0=op0),
                             reads, writes)
            return S.add(eng, lambda e: e.tensor_scalar(out=out, in0=in0, scalar1=s1, scalar2=s2, op0=op0, op1=op1),
                         reads, writes)

        def stt(out, in0, scalar, in1, op0, op1, reads, writes, eng="dve"):
            return S.add(eng, lambda e: e.scalar_tensor_tensor(out=out, in0=in0, scalar=scalar, in1=in1,
                                                               op0=op0, op1=op1), reads, writes)

        def cpy(out, in_, reads, writes, eng="dve"):
            return S.add(eng, lambda e: e.tensor_copy(out=out, in_=in_), reads, writes)

        def mms(lst, reads, writes):
            def fn(e):
                ins = None
                for (o, l, r, st, sp) in lst:
                    ins = e.matmul(o, lhsT=l, rhs=r, start=st, stop=sp)
                return ins
            return S.add("pe", fn, reads, writes)

        def transposes(lst, reads, writes):
            def fn(e):
                ins = None
                for (o, i) in lst:
                    ins = e.transpose(out=o, in_=i, identity=IDB[:])
                return ins
            return S.add("pe", fn, list(reads) + [bIDB], writes)

        def rsqrt_act(out, in_, reads, writes, bias_ap, scale=1.0):
            act(out, in_, AF.Ln, reads, writes, bias=bias_ap, scale=scale)
            act(out, out, AF.Exp, writes, writes, scale=-0.5)

        dma("sp", SM[:], smallp, writes=[bSM])
        dma("sp", ON32[:], ones32, writes=[bON])
        dma("pool", IDB[:], ident, writes=[bIDB])
        dma("pool", BDB[:], bdiag, writes=[bBDB])
        W0, bW0 = cz.get("W0", [8, 3584])
        w0v = w_in_0.rearrange("(k p) n -> p k n", p=128)
        bW0g = [Buf(f"W0g{g}") for g in range(7)]
        w0_sems = set()
        for g in (0, 1, 2, 5, 6, 3, 4):
            r_ = dma("pool", W0[:, :, g * 512:(g + 1) * 512], w0v[:, :, g * 512:(g + 1) * 512], writes=[bW0g[g]])
            w0_sems.add(r_[1])
        S.add("dve", lambda e: e.memset(PRM[:, 0:1], EPS), writes=[bPRM])
        S.add("dve", lambda e: e.memset(PRM[:, 1:2], 64 * EPS), writes=[bPRM])
        cpy(PRM[:, 2:3], SM[:, QN:QN + 1], [bSM], [bPRM])
        tsc(PRM[:, 3:4], SM[:, KN:KN + 1], 8.0, None, ALU.mult, None, [bSM], [bPRM])
        tsc(PRM[:, 8:16], SM[:, PSC:PSC + 8], 0.5, None, ALU.mult, None, [bSM], [bPRM])
        tsc(PRM[:, 16:80], SM[:, KNR:KNR + 64], 8.0, None, ALU.mult, None, [bSM], [bPRM])
        tanh_c, btc = TC[:], Buf('TC')
        act(tanh_c, SM[:, CVEC:CVEC + 16], AF.Tanh, [bSM], [btc], scale=0.5)
        stt(SC[:], tanh_c, 1.0, SM[:, CVEC:CVEC + 16], ALU.add, ALU.mult, [btc, bSM], [bSC])
        tsc(SC[:], SC[:], 0.5, None, ALU.mult, None, [bSC], [bSC])
        cf.reset()
        cb.reset()
        bMODl = [Buf("MOD0"), Buf("MOD1")]
        bAMl = [Buf("AM0"), Buf("AM1")]
        bGl = [Buf("G0"), Buf("G1")]
        WA = [cb.get(f"WAb{i}", [8, 256]) for i in range(4)]
        WF = [cf.get(f"WAf{i}", [8, 256]) for i in range(4)]
        SCB, bSCB = cz.get("SCBb", [16, 128])
        SCh, bSCh = cz.get("SCh", [32])
        WAq = [cz.get(f"WAq{i}", [8, 128]) for i in range(2)]
        cpy(SCh[:, 0:16], SC[:, 0:16], [bSC], [bSCh])
        cpy(SCB, SC[:, 0:16].unsqueeze(2).to_broadcast([128, 16, 128]), [bSC], [bSCB])

        def finish_mod(l, pfm, bpfm):
            ba = BA0 if l == 0 else BA1
            ng = NG0 if l == 0 else NG1
            tt(MOD[:, l, :].rearrange("p (a b) -> p a b", b=2), pfm[:, 0:32].rearrange("p (a b) -> p a b", b=2),
               SM[:, ba:ba + 16].unsqueeze(2).to_broadcast([128, 16, 2]), ALU.add, [bpfm, bSM], [bMODl[l]])
            mv = MOD[:, l, :].rearrange("p (a b) -> p a b", b=2)
            for ci in range(2):
                stt(AM[:, l * 2 + ci, :], mv[:, 8:16, ci], 1.0, SM[:, ng:ng + 8], ALU.add, ALU.mult,
                    [bMODl[l], bSM], [bAMl[l]])

        wav0 = w_ada[0].rearrange("(k p) n -> p k n", p=128)
        pfm, bpfm = PSF[4], bPSF[4]
        for g in range(4):
            dma("sp", WF[g][0], wav0[:, :, g * 256:(g + 1) * 256], writes=[WF[g][1]])
        for g in range(8):
            wa, bwa = WA[g % 4]
            wf, bwf = WF[g % 4]
            if g % 2 == 0:
                cpy(wa, wf, [bwf], [bwa])
            else:
                act(wa, wf, AF.Copy, [bwf], [bwa])
            if g + 4 < 8:
                dma("sp", wf, wav0[:, :, (g + 4) * 256:(g + 5) * 256], writes=[bwf])
            lst = []
            for blk in range(2):
                o = pfm[:, (g * 2 + blk) * 2:(g * 2 + blk) * 2 + 2]
                for k in range(8):
                    lst.append((o, wa[:, k, blk * 128:(blk + 1) * 128], SCh[:, 2 * k:2 * k + 2], k == 0, k == 7))
            mms(lst, [bwa, bSCh], [bpfm])
        finish_mod(0, pfm, bpfm)

        def modgen(WFq, BGq):
            pieces = [(0, "g", j) for j in range(8)] + [(1, "f", j) for j in range(16)] + [(1, "g", j) for j in range(8)]
            pf1, bpf1 = PSF[5], bPSF[5]

            def issue(i):
                l, kind, j = pieces[i]
                col0 = (2048 if kind == "g" else 0) + j * 128
                wv = w_ada[l].rearrange("(k p) n -> p k n", p=128)
                dma("sp", WFq[i % 2][0], wv[:, :, col0:col0 + 128], writes=[WFq[i % 2][1]])
                if kind == "g":
                    dma("sp", BGq[i % 2][0], bg[:, l * 1024 + j * 128:l * 1024 + (j + 1) * 128], writes=[BGq[i % 2][1]])

            issue(0)
            yield
            for i, (l, kind, j) in enumerate(pieces):
                if i + 1 < len(pieces):
                    issue(i + 1)
                    yield
                wf, bwf = WFq[i % 2]
                wa, bwa = WAq[i % 2]
                if i % 2 == 0:
                    cpy(wa, wf, [bwf], [bwa])
                else:
                    act(wa, wf, AF.Copy, [bwf], [bwa])
                yield
                if kind == "f":
                    lst = [(pf1[:, j * 2:j * 2 + 2], wa[:, k, :], SCh[:, 2 * k:2 * k + 2], k == 0, k == 7) for k in range(8)]
                    mms(lst, [bwa, bSCh], [bpf1])
                    if j == 15:
                        finish_mod(1, pf1, bpf1)
                else:
                    pg, bpg = bankf()
                    lst = []
                    for ci in range(2):
                        for k in range(8):
                            lst.append((pg[:, ci * 128:(ci + 1) * 128], SCB[:, 2 * k + ci, :], wa[:, k, :], k == 0, k == 7))
                    mms(lst, [bwa, bSCB], [bpg])
                    for ci in range(2):
                        tt(G[:, l * 2 + ci, j * 128:(j + 1) * 128], pg[:, ci * 128:(ci + 1) * 128], BGq[i % 2][0],
                           ALU.add, [bpg, BGq[i % 2][1]], [bGl[l]])
                yield

        def Asc(l, ci, k):
            return AM[:, l * 2 + ci, k:k + 1]

        def Bsc(l, ci, k):
            return MOD[:, l, 2 * k + ci:2 * k + ci + 1]

        S.barrier(skip_sems=w0_sems)

        def norm_to_hT(Xt, bX, xn, bxn, junk, bjunk, hT_dst, bhT, l, ci, evac_act=False, rs_pre=None, mid=None):
            if rs_pre is not None:
                rs, bs = rs_pre
            else:
                st_, bs = stat()
                ss, rs = st_[:, 0:1], st_[:, 1:2]
                act(junk, Xt, AF.Square, [bX], [bjunk, bs], accum=ss)
                act(rs, ss, AF.Ln, [bs, bPRM], [bs], bias=PRM[:, 0:1], scale=1.0 / 1024)
                act(rs, rs, AF.Exp, [bs], [bs], scale=-0.5)
            tsc(xn, Xt, rs, None, ALU.mult, None, [bX, bs], [bxn])
            for hf in range(2):
                pb, bpb = PSB[hf], bPSB[hf]
                transposes([(pb[:, j * 128:(j + 1) * 128], xn[:, (hf * 4 + j) * 128:(hf * 4 + j + 1) * 128])
                            for j in range(4)], [bxn], [bpb])
            if mid is not None:
                mid()
            for j in range(4):
                for hf in range(2):
                    k = hf * 4 + j
                    pb, bpb = PSB[hf], bPSB[hf]
                    if hf == 0:
                        act(hT_dst[:, k, :], pb[:, j * 128:(j + 1) * 128], AF.Identity, [bpb, bAMl[l], bMODl[l]], [bhT[0]],
                            scale=Asc(l, ci, k), bias=Bsc(l, ci, k))
                    else:
                        tsc(hT_dst[:, k, :], pb[:, j * 128:(j + 1) * 128], Asc(l, ci, k), Bsc(l, ci, k), ALU.mult, ALU.add,
                            [bpb, bAMl[l], bMODl[l]], [bhT[1]])

        def run_streams(gens, width, stagger=0, bgen=None):
            active = []
            it = iter(gens)
            if stagger:
                g0 = next(it, None)
                if g0 is not None:
                    active.append(g0)
                    for _ in range(stagger):
                        try:
                            next(g0)
                        except StopIteration:
                            active.remove(g0)
                            break
            while True:
                while len(active) < width:
                    g = next(it, None)
                    if g is None:
                        break
                    active.append(g)
                if not active:
                    break
                for g in list(active):
                    try:
                        next(g)
                    except StopIteration:
                        active.remove(g)
                if bgen is not None:
                    try:
                        next(bgen)
                    except StopIteration:
                        bgen = None
            if bgen is not None:
                for _ in bgen:
                    pass

        cb.reset()
        cf.reset()
        psf_n[0] = 5
        X_ = [cf.get(f"X{i}", [1024]) for i in range(4)]
        TMP = [cf.get(f"TMP{i}", [512]) for i in range(4)]
        K32 = [cf.get(f"K32{i}", [512]) for i in range(2)]
        V32 = [cf.get(f"V32{i}", [512]) for i in range(2)]
        WFq = [cf.get(f"WFq{i}", [8, 128]) for i in range(2)]
        BGq = [cf.get(f"BGq{i}", [128]) for i in range(2)]
        JUNK, bJUNK = cb.get("junk", [1024])
        XN = [cb.get(f"xn{i}", [1024]) for i in range(2)]
        HT = [cb.get(f"HT{i}", [8, 512]) for i in range(2)]
        HTB = [(Buf("hlo"), Buf("hhi")) for i in range(2)]
        UTs_ = [cb.get("UTs0", [4, 512]), cz.get("UTs1", [4, 512])]
        SGAs_ = [cb.get("SGAs0", [4, 512]), cz.get("SGAs1", [4, 512])]
        QTs_ = [cb.get("QTs0", [4, 512]), cz.get("QTs1", [4, 512])]
        KTs_ = [cb.get("KTs0", [4, 512]), cz.get("KTs1", [4, 512])]
        VST = [cb.get(f"VST{i}", [4, 8, 65]) for i in range(2)]
        SGBS = [cb.get(f"SGBS{i}", [4, 512]) for i in range(2)]
        SQ = [cb.get(f"SQ{i}", [512]) for i in range(2)]
        ZERO, bZERO = cb.get("zero", [4, 16])
        S.add("dve", lambda e: e.memset(ZERO, 0.0), writes=[bZERO])
        for i in range(2):
            S.add("dve", lambda e, i=i: e.memset(VST[i][0][:, :, :, 64:65], 1.0), writes=[VST[i][1]])
        sUv = sU.rearrange("c p t -> p c t")
        for b in range(4):
            dma("pool", sUv[:, :, b * 288:b * 288 + 16], ZERO, [bZERO], [bU])
            dma("pool", sUv[:, :, b * 288 + 272:b * 288 + 288], ZERO, [bZERO], [bU])
        dma("pool", sUv[:, :, UB:UB + 16], ZERO, [bZERO], [bU])

        supertiles = [(0, list(range(0, 4))), (0, list(range(4, 8)))]
        supertiles += [(1, list(range(8 + a, 8 + min(a + 4, NTS)))) for a in range(0, NTS, 4)]
        if build_program.phases < 1:
            supertiles = []
        supertiles = supertiles[:build_program.nst]

        xloaded = set()

        def stA(sti, ci, tiles):
            sl = sti % 2
            nt = len(tiles)
            N = nt * 128
            hT, bhT0 = HT[sl]
            bhT = HTB[sl]
            vst, bvst = VST[sl]
            sgbs, bsgbs = SGBS[sl]
            UTs, bUTs = UTs_[sl]
            SGAs, bSGAs = SGAs_[sl]
            QTs, bQTs = QTs_[sl]
            KTs, bKTs = KTs_[sl]
            def xload(sti_, jj_):
                ci_, tiles_ = supertiles[sti_]
                tg_ = tiles_[jj_]
                Xt_, bX_ = X_[(sti_ % 2) * 2 + jj_ % 2]
                src = xp[tg_ * 128:(tg_ + 1) * 128, :] if ci_ == 0 else xs[(tg_ - 8) * 128:(tg_ - 7) * 128, :]
                dma("sp", Xt_, src, writes=[bX_])
                xloaded.add((sti_, jj_))

            for jj in range(min(2, nt)):
                if (sti, jj) not in xloaded:
                    xload(sti, jj)
            rsn = {}

            def stats_of(j_):
                st_, bs_ = stat()
                Xj, bXj = X_[sl * 2 + j_ % 2]
                act(JUNK, Xj, AF.Square, [bXj], [Buf("j"), bs_], accum=st_[:, 0:1])
                act(st_[:, 1:2], st_[:, 0:1], AF.Ln, [bs_, bPRM], [bs_], bias=PRM[:, 0:1], scale=1.0 / 1024)
                act(st_[:, 1:2], st_[:, 1:2], AF.Exp, [bs_], [bs_], scale=-0.5)
                rsn[j_] = (st_[:, 1:2], bs_)

            stats_of(0)
            for jj, tg in enumerate(tiles):
                Xt, bX = X_[sl * 2 + jj % 2]
                xn, bxn = XN[sl]
                nxt_stats = (lambda j_=jj + 1: stats_of(j_)) if jj + 1 < nt else None
                norm_to_hT(Xt, bX, xn, bxn, JUNK, Buf("j"), hT[:, :, jj * 128:(jj + 1) * 128], bhT, 0, ci,
                           rs_pre=rsn[jj], mid=nxt_stats)
                if jj + 2 < nt:
                    xload(sti, jj + 2)
                yield
            if sti + 2 < len(supertiles):
                for jj in range(min(2, len(supertiles[sti + 2][1]))):
                    xload(sti + 2, jj)

            last = (ci == 1 and tiles[0] - 8 == 16)
            Nab, Nga, Nq, ngb = (144, 128, 128, 1) if last else (N, N, N, nt)

            def fm_group(col0, n):
                pb_, bpb_ = bankf()
                lst = [(pb_[:, :n], W0[:, k, col0:col0 + 128], hT[:, k, :n], k == 0, k == 7) for k in range(8)]
                mms(lst, [bW0g[col0 // 512], *bhT], [bpb_])
                return pb_, bpb_

            for c in range(4):
                pa, bpa = fm_group(c * 128, Nab)
                pbb, bpbb = fm_group(512 + c * 128, Nab)
                t0, bt0 = TMP[sl * 2]
                act(t0[:, :Nab], pbb[:, :Nab], AF.Tanh, [bpbb], [bt0], scale=0.5)
                stt(UTs[:, c, :Nab], t0[:, :Nab], 1.0, pa[:, :Nab], ALU.add, ALU.mult, [bt0, bpa], [bUTs])
                yield
            for c in range(4):
                pg_, bpg_ = fm_group(1024 + c * 128, Nga)
                t0, bt0 = TMP[sl * 2 + 1]
                act(t0[:, :Nga], pg_[:, :Nga], AF.Tanh, [bpg_], [bt0], scale=0.5)
                stt(SGAs[:, c, :Nga], t0[:, :Nga], 1.0, pg_[:, :Nga], ALU.add, ALU.mult, [bt0, bpg_], [bSGAs])
                yield
            for jj, tg in enumerate(tiles):
                hs = hT[:, :, jj * 128:(jj + 1) * 128]

                def tm_group(col0):
                    pb_, bpb_ = bankf()
                    lst = [(pb_[:, :], hs[:, k, :], W0[:, k, col0:col0 + 512], k == 0, k == 7) for k in range(8)]
                    mms(lst, [bW0g[col0 // 512], *bhT], [bpb_])
                    return pb_, bpb_

                pv, bpv = tm_group(2560)
                cpy(vst[:, jj, :, 0:64], pv[:, :].rearrange("p (h d) -> p h d", h=8), [bpv], [bvst])
                if ci == 0:
                    v32, bv32 = V32[sl]
                    cpy(v32, pv[:, :], [bpv], [bv32])
                    dma("pool", vo[tg * 128:(tg + 1) * 128, :], v32, [bv32], [])
                yield
                if jj >= ngb:
                    continue
                pgb, bpgb = tm_group(3072)
                t0, bt0 = TMP[sl * 2 + 1]
                act(t0, pgb[:, :], AF.Tanh, [bpgb], [bt0], scale=0.5)
                stt(sgbs[:, jj, :], t0, 1.0, pgb[:, :], ALU.add, ALU.mult, [bt0, bpgb], [bsgbs])
                yield
            def prompt_k(jj):
                tg = tiles[jj]
                hs = hT[:, :, jj * 128:(jj + 1) * 128]
                pk, bpk = bankf()
                lst = [(pk[:, :], hs[:, k, :], W0[:, k, 2048:2560], k == 0, k == 7) for k in range(8)]
                mms(lst, [bW0g[4], *bhT], [bpk])
                k32, bk32 = K32[sl]
                t1, bt1 = TMP[sl * 2 + 1]
                act(t1, pk[:, :], AF.Square, [bpk], [bt1])
                kss, bs = stat()
                S.add("dve", lambda e, t1=t1, kss=kss: e.tensor_reduce(
                    out=kss, in_=t1.rearrange("p (h d) -> p h d", h=8), axis=mybir.AxisListType.X, op=ALU.add),
                    [bt1], [bs])
                act(kss, kss, AF.Ln, [bs, bPRM], [bs], bias=PRM[:, 1:2])
                act(kss, kss, AF.Exp, [bs], [bs], scale=-0.5)
                tt(k32.rearrange("p (h d) -> p h d", h=8), pk[:, :].rearrange("p (h d) -> p h d", h=8),
                   kss.unsqueeze(2).to_broadcast([128, 8, 64]), ALU.mult, [bpk, bs], [bk32])
                tt(k32.rearrange("p (h d) -> p h d", h=8), k32.rearrange("p (h d) -> p h d", h=8),
                   PRM[:, 16:80].unsqueeze(1).to_broadcast([128, 8, 64]), ALU.mult, [bk32, bPRM], [bk32])
                dma("pool", ko[tg * 128:(tg + 1) * 128, :], k32, [bk32], [])

            for (col, dst, bdst, scl, nn) in ((1536, QTs, bQTs, PRM[:, 2:3], Nq), (2048, KTs, bKTs, PRM[:, 3:4], N)):
                for c in range(4):
                    pq, bpq = fm_group(col + c * 128, nn)
                    sq, bsq = SQ[sl]
                    act(sq[:, :nn], pq[:, :nn], AF.Square, [bpq], [bsq])
                    yield
                    p2, bp2 = bankf()
                    mms([(p2[:, :nn], BDB[:], sq[:, :nn], True, True)], [bBDB, bsq], [bp2])
                    t0, bt0 = TMP[sl * 2]
                    act(t0[:, :nn], p2[:, :nn], AF.Ln, [bp2, bPRM], [bt0], bias=PRM[:, 1:2])
                    act(t0[:, :nn], t0[:, :nn], AF.Exp, [bt0], [bt0], scale=-0.5)
                    stt(dst[:, c, :nn], pq[:, :nn], scl, t0[:, :nn], ALU.mult, ALU.mult, [bpq, bt0, bPRM], [bdst])
                    if ci == 0 and col == 1536 and c < nt:
                        prompt_k(c)
                    yield
            if ci == 0:
                t0c = tiles[0] * 128
                for bb in range(2):
                    b = tiles[0] // 2 + bb
                    dma("pool", sUv[:, :, b * 288 + 16:b * 288 + 272], UTs[:, :, bb * 256:(bb + 1) * 256], [bUTs], [bU])
            else:
                t0c = 1024 + (tiles[0] - 8) * 128
                u0 = UB + 16 + (tiles[0] - 8) * 128
                dma("pool", sUv[:, :, u0:u0 + Nab], UTs[:, :, :Nab], [bUTs], [bU])
            dma("pool", sGA.rearrange("c p t -> p c t")[:, :, t0c:t0c + Nga], SGAs[:, :, :Nga], [bSGAs], [bGA])
            dma("pool", sQ.rearrange("c p t -> p c t")[:, :, t0c:t0c + Nq], QTs[:, :, :Nq], [bQTs], [bQ])
            dma("pool", sK.rearrange("c p t -> p c t")[:, :, t0c:t0c + N], KTs[:, :, :N], [bKTs], [bK])
            g0 = tiles[0]
            dma("pool", sV.rearrange("j p f -> p j f")[:, g0:g0 + nt, :],
                vst[:, 0:nt, :, :].rearrange("p j h d -> p j (h d)"), [bvst], [bV])
            dma("pool", sGB.rearrange("j p f -> p j f")[:, g0:g0 + ngb, :], sgbs[:, 0:ngb, :], [bsgbs], [bGB])
            yield

        run_streams((stA(sti, ci, tiles) for sti, (ci, tiles) in enumerate(supertiles)), 2, stagger=0,
                    bgen=modgen(WFq, BGq))
        psf_n[0] = 4

        S.barrier()
        PHASES_DONE = build_program.phases

        if PHASES_DONE >= 2:
            cz.reset()
            cb.reset()
            cf.reset()
            WO0, bWO0 = cz.get("WO0", [8, 1024])
            ET, bET = cz.get("ET", [8, 13, 128])
            CK, bCK = cz.get("CK", [4, 512])
            CV, bCV = cz.get("CV", [4, 8, 65])
            DG, bDG = cz.get("DG", [124, 128])
            dma("pool", WO0, w_out_0.rearrange("(k p) n -> p k n", p=128), writes=[bWO0])
            dma("pool", CK, ckT.rearrange("(c p) k -> p c k", p=128), writes=[bCK])
            S.add("dve", lambda e: e.memset(CV[:, :, :, 64:65], 1.0), writes=[bCV])
            for j in range(4):
                dma("pool", CV[:, j, :, 0:64], cvv[j * 128:(j + 1) * 128, :].rearrange("p (h d) -> p h d", h=8), writes=[bCV])
            CWH, bCWH = CWHT[:], Buf("CWH")
            tsc(CWH, SM[:, CW:CW + 124], 0.5, None, ALU.mult, None, [bSM], [bCWH])
            bDGc = [Buf(f"DG{c}") for c in range(4)]
            for c in range(4):
                tt(DG[:, c * 31:(c + 1) * 31, :], IDB[:].unsqueeze(1).to_broadcast([128, 31, 128]),
                   CWH[:, c * 31:(c + 1) * 31].unsqueeze(2).to_broadcast([128, 31, 128]),
                   ALU.mult, [bIDB, bCWH], [bDGc[c]], eng=DG_ENG)
            al = [Buf("CY"), Buf("CY2"), Buf("MEAN"), Buf("RSTD")]
            cf.reset()
            CY, _ = cf.get("CY", [4, 512])
            CY2, _ = cf.get("CY2", [4, 512])
            MEAN, _ = cf.get("MEAN", [512])
            RSTD, _ = cf.get("RSTD", [512])
            bCY, bCY2, bMEAN, bRSTD = al
            TN = [cf.get(f"TN{i}", [512]) for i in range(2)]
            XR = [cf.get(f"XR{i}", [1024]) for i in range(2)]
            Y0 = [cf.get(f"Y0{i}", [1024]) for i in range(2)]
            OT, bOT = cf.get("OT", [512])
            UP, bUP = cb.get("UP", [4, 576])
            SGA, bSGA = cb.get("SGA", [4, 512])
            ZT_ = [cb.get(f"ZT{i}", [8, 512]) for i in range(2)]
            QT, bQT = cb.get("QT", [4, 512])
            KW, bKW = cb.get("KW", [4, 1024])
            VW, bVW = cb.get("VW", [8, 8, 65])
            SGB, bSGB = cb.get("SGB", [4, 512])
            PT = [cb.get(f"PT{i}", [9, 128]) for i in range(3)]
            YA, bYA = cb.get("YA", [512])
            SN, bSN = cb.get("SN", [512])

            stB = [(0, [0, 1, 2, 3]), (0, [4, 5, 6, 7])] + [(1, list(range(a, min(a + 4, NTB)))) for a in range(0, NTB, 4)]
            psf_n[0] = 4
            rot[0] = [0, 1, 2, 6]
            psb_n[0] = 1
            pt_i = [0]
            xr_i = [0]

            def Xgen(ci, tiles, zi):
                ZT, bZT = ZT_[zi]
                nt = len(tiles)
                N = nt * 128
                if ci == 0:
                    tcol = tiles[0] * 128
                    seqs = [(0, 256, 0), (256, 256, 288)]
                    b0 = tiles[0] // 2
                    dma("sp", UP[:, :, 0:576], sUv[:, :, b0 * 288:b0 * 288 + 576], [bU], [bUP])
                else:
                    tcol = 1024 + tiles[0] * 128
                    seqs = [(0, N, 0)]
                    u0 = UB + tiles[0] * 128
                    dma("sp", UP[:, :, 0:N + 32], sUv[:, :, u0:u0 + N + 32], [bU], [bUP])
                dma("sp", SGA[:, :, :N], sGA.rearrange("c p t -> p c t")[:, :, tcol:tcol + N], [bGA], [bSGA])
                for c in range(4):
                    pc, bpc = PSF[3], bPSF[3]
                    lst = []
                    for (oc, n, uo) in seqs:
                        for k in range(31):
                            lst.append((pc[:, oc:oc + n], DG[:, c * 31 + k, :], UP[:, c, uo + k + 1:uo + k + 1 + n],
                                        k == 0, k == 30))
                    npc = 4
                    per = (len(lst) + npc - 1) // npc
                    for pi in range(npc):
                        mms(lst[pi * per:(pi + 1) * per], [bDGc[c], bUP], [bpc])
                        if pi == npc - 1:
                            act(CY[:, c, :N], pc[:, :N], AF.Identity, [bpc, bSM], [bCY], bias=SM[:, CB + c:CB + c + 1])
                            act(CY2[:, c, :N], pc[:, :N], AF.Square, [bpc, bSM], [bCY2], bias=SM[:, CB + c:CB + c + 1])
                        yield
                pm, bpm = bankf()
                mms([(pm[:, :N], ON32[:], CY[:, c, :N], c == 0, c == 3) for c in range(4)], [bON, bCY], [bpm])
                cpy(MEAN[:, :N], pm[:, :N], [bpm], [bMEAN])
                yield
                pq2, bpq2 = bankf()
                mms([(pq2[:, :N], ON32[:], CY2[:, c, :N], c == 0, c == 3) for c in range(4)], [bON, bCY2], [bpq2])
                tt(RSTD[:, :N], MEAN[:, :N], MEAN[:, :N], ALU.mult, [bMEAN], [bRSTD])
                tt(RSTD[:, :N], pq2[:, :N], RSTD[:, :N], ALU.subtract, [bpq2, bRSTD], [bRSTD])
                act(RSTD[:, :N], RSTD[:, :N], AF.Ln, [bRSTD, bPRM], [bRSTD], bias=PRM[:, 0:1])
                act(RSTD[:, :N], RSTD[:, :N], AF.Exp, [bRSTD], [bRSTD], scale=-0.5)
                yield
                for c in range(4):
                    tn, btn = TN[c % 2]
                    tt(tn[:, :N], CY[:, c, :N], MEAN[:, :N], ALU.subtract, [bCY, bMEAN], [btn])
                    yield
                    tt(tn[:, :N], tn[:, :N], RSTD[:, :N], ALU.mult, [btn, bRSTD], [btn])
                    yield
                    tsc(tn[:, :N], tn[:, :N], SM[:, LG + c:LG + c + 1], SM[:, LB + c:LB + c + 1], ALU.mult, ALU.add,
                        [btn, bSM], [btn])
                    act(SN[:, :N], tn[:, :N], AF.Tanh, [btn], [bSN], scale=0.5)
                    yield
                    stt(tn[:, :N], SN[:, :N], 1.0, tn[:, :N], ALU.add, ALU.mult, [bSN, btn], [btn])
                    yield
                    stt(ZT[:, c, :N], tn[:, :N], 0.25, SGA[:, c, :N], ALU.mult, ALU.mult, [btn, bSGA], [bZT])
                    yield

            def Yparams(ci, tiles):
                nt = len(tiles)
                if ci == 0:
                    tcol = tiles[0] * 128
                    gt0 = tiles[0]
                    kt0, nkt = tiles[0], 4
                else:
                    tcol = 1024 + tiles[0] * 128
                    gt0 = 8 + tiles[0]
                    kt0 = max(tiles[0] - 2, 0)
                    nkt = min(max(tiles[-1] + 2, 3) + 1, NTS) - kt0
                return nt, nt * 128, tcol, gt0, kt0, nkt

            def Yloads(ci, tiles):
                nt, N, tcol, gt0, kt0, nkt = Yparams(ci, tiles)
                dma("sp", QT[:, :, :N], sQ.rearrange("c p t -> p c t")[:, :, tcol:tcol + N], [bQ], [bQT])
                kcol = (kt0 * 128) if ci == 0 else (1024 + kt0 * 128)
                dma("sp", KW[:, :, :nkt * 128], sK.rearrange("c p t -> p c t")[:, :, kcol:kcol + nkt * 128], [bK], [bKW])
                gk0 = kt0 if ci == 0 else 8 + kt0
                dma("sp", VW[:, 0:nkt, :, :].rearrange("p j h d -> p j (h d)"),
                    sV.rearrange("j p f -> p j f")[:, gk0:gk0 + nkt, :], [bV], [bVW])
                dma("sp", SGB[:, 0:nt, :], sGB.rearrange("j p f -> p j f")[:, gt0:gt0 + nt, :], [bGB], [bSGB])

            def Ygen(ci, tiles, zi, nxt=None):
                ZT, bZT = ZT_[zi]
                nt, N, tcol, gt0, kt0, nkt = Yparams(ci, tiles)
                tinfo = {}
                for jj, t in enumerate(tiles):
                    if ci == 0:
                        bt = (t // 2) * 2
                        wt = [bt - tiles[0], bt - tiles[0] + 1]
                        pat0 = None
                        nctx = 0
                    else:
                        kts = key_tiles(t)
                        wt = [j - kt0 for j in kts]
                        pat0 = pat_index(t, kts[0])
                        nctx = 4
                    tinfo[jj] = (wt, pat0, nctx)
                ob = [(PSF[4], bPSF[4]), (PSF[5], bPSF[5])]
                pts = {}

                def stage1(jj, h):
                    wt, pat0, nctx = tinfo[jj]
                    nw = len(wt)
                    c, hh = h // 2, h % 2
                    pr = slice(64 * hh, 64 * hh + 64)
                    pt, bpt = PT[pt_i[0] % 3]
                    pt_i[0] += 1
                    pts[(jj, h)] = (pt, bpt)
                    qs = QT[pr, c, jj * 128:(jj + 1) * 128]
                    groups = [wt[0:4]] + ([wt[4:]] if nw > 4 else [])
                    so = 0
                    for grp in groups:
                        ps_, bps_ = bankf()
                        lst = [(ps_[:, i * 128:(i + 1) * 128], KW[pr, c, s_ * 128:(s_ + 1) * 128], qs, True, True)
                               for i, s_ in enumerate(grp)]
                        mms(lst, [bKW, bQT], [bps_])
                        act(pt[:, so:so + len(grp), :].rearrange("p a b -> p (a b)"), ps_[:, :len(grp) * 128],
                            AF.Exp, [bps_], [bpt])
                        so += len(grp)
                    if pat0 is not None:
                        tt(pt[:, 0:nw, :], pt[:, 0:nw, :], ET[:, h, pat0:pat0 + nw, :], ALU.mult, [bpt, bETh[h]], [bpt])
                    if nctx:
                        ps_, bps_ = bankf()
                        lst = [(ps_[:, i * 128:(i + 1) * 128], CK[pr, c, i * 128:(i + 1) * 128], qs, True, True)
                               for i in range(4)]
                        mms(lst, [bCK, bQT], [bps_])
                        act(pt[:, nw:nw + 4, :].rearrange("p a b -> p (a b)"), ps_[:, :], AF.Exp, [bps_], [bpt])

                def stage2(jj, h):
                    wt, pat0, nctx = tinfo[jj]
                    nw = len(wt)
                    tot = nw + nctx
                    pt, bpt = pts[(jj, h)]
                    po, bpo = ob[h // 4]
                    oo = (h % 4) * 65
                    lst = []
                    for i, s_ in enumerate(wt):
                        lst.append((po[:, oo:oo + 65], pt[:, i, :], VW[:, s_, h, :], i == 0, i == tot - 1))
                    for i in range(nctx):
                        lst.append((po[:, oo:oo + 65], pt[:, nw + i, :], CV[:, i, h, :], False, nw + i == tot - 1))
                    mms(lst, [bpt, bVW, bCV], [bpo])

                def tail1(jj):
                    for hb in range(2):
                        po, bpo = ob[hb]
                        pov = po[:, 0:260].rearrange("p (h d) -> p h d", h=4)
                        st_, bs = stat()
                        rd = st_[:, 0:4]
                        S.add("dve", lambda e, rd=rd, pov=pov: e.reciprocal(out=rd, in_=pov[:, :, 64]), [bpo], [bs])
                        otv = OT[:, hb * 256:(hb + 1) * 256].rearrange("p (h d) -> p h d", h=4)
                        stt(otv, pov[:, :, 0:64], 0.5, rd.unsqueeze(2).to_broadcast([128, 4, 64]), ALU.mult, ALU.mult,
                            [bpo, bs], [bOT])
                    tt(YA, OT, SGB[:, jj, :], ALU.mult, [bOT, bSGB], [bYA])

                def tail2(jj):
                    pb, bpb = bankb()
                    transposes([(pb[:, k * 128:(k + 1) * 128], YA[:, k * 128:(k + 1) * 128]) for k in range(4)],
                               [bYA], [bpb])
                    cpy(ZT[:, 4:8, jj * 128:(jj + 1) * 128], pb[:, 0:512].rearrange("p (a b) -> p a b", a=4),
                        [bpb], [bZT])

                items = [(jj, h) for jj in range(nt) for h in range(8)]
                stage1(*items[0])
                stage1(*items[1])
                pend = None
                for i_, (jj, h) in enumerate(items):
                    if i_ + 2 < len(items):
                        stage1(*items[i_ + 2])
                    stage2(jj, h)
                    if pend is not None and h == 1:
                        tail2(pend)
                        pend = None
                    if h == 7:
                        tail1(jj)
                        pend = jj
                    yield
                if pend is not None:
                    tail2(pend)
                if nxt is not None:
                    Yloads(*nxt)
                yield
                for jj, t in enumerate(tiles):
                    xr, bxr = XR[xr_i[0] % 2]
                    y0, by0 = Y0[xr_i[0] % 2]
                    xr_i[0] += 1
                    src = xp[t * 128:(t + 1) * 128, :] if ci == 0 else xs[t * 128:(t + 1) * 128, :]
                    dma("sp", xr, src, writes=[bxr])
                    for n in range(2):
                        po, bpo = bankf()
                        lst = [(po[:, :], ZT[:, k, jj * 128:(jj + 1) * 128], WO0[:, k, n * 512:(n + 1) * 512], k == 0, k == 7)
                               for k in range(8)]
                        mms(lst, [bZT, bWO0], [bpo])
                        tt(y0[:, n * 512:(n + 1) * 512], po[:, :], G[:, ci, n * 512:(n + 1) * 512], ALU.mult, [bpo, bGl[0]], [by0])
                        yield
                    tt(y0, y0, xr, ALU.add, [by0, bxr], [by0])
                    gt = t if ci == 0 else 8 + t
                    dma("pool", sY0[gt], y0, [by0], [bY0])
                    st_, bs = stat()
                    act(xr, y0, AF.Square, [by0], [bxr, bs], accum=st_[:, 0:1])
                    act(RS1[:, gt:gt + 1], st_[:, 0:1], AF.Ln, [bs, bPRM], [bRS1[gt]], bias=PRM[:, 0:1], scale=1.0 / 1024)
                    act(RS1[:, gt:gt + 1], RS1[:, gt:gt + 1], AF.Exp, [bRS1[gt]], [bRS1[gt]], scale=-0.5)

            def zip2(g1, g2):
                gs = [g for g in (g1, g2) if g is not None]
                while gs:
                    for g in list(gs):
                        try:
                            next(g)
                        except StopIteration:
                            gs.remove(g)

            bETh = [Buf(f"ET{h}") for h in range(8)]

            def build_ET():
                for h in range(8):
                    dma("pool", ET[:, h, :, :].rearrange("p a b -> p (a b)"), ebias[:, h * 1664:(h + 1) * 1664],
                        writes=[bETh[h]])
                yield
                for h in range(8):
                    eh = ET[:, h, :, :].rearrange("p a b -> p (a b)")
                    act(eh, eh, AF.Exp, [bETh[h]], [bETh[h]])
                    yield

            def wzip(gws):
                alive = [True] * len(gws)
                while alive[0]:
                    for gi, (g, reps) in enumerate(gws):
                        for _ in range(reps):
                            if alive[gi]:
                                try:
                                    next(g)
                                except StopIteration:
                                    alive[gi] = False
                for gi, (g, reps) in enumerate(gws):
                    if alive[gi]:
                        for _ in g:
                            pass

            W1OFF = 25632
            W1 = ZB[:, W1OFF:W1OFF + 16384].rearrange("p (k n) -> p k n", k=8)
            bW1 = Buf("W1")
            w1v = w_in_1.rearrange("(k p) n -> p k n", p=128)

            def Xchain():
                for si, (ci, tiles) in enumerate(stB):
                    yield from Xgen(ci, tiles, si % 2)
                for k in range(0, 8, 2):
                    dma("pool", W1[:, k:k + 2, :], w1v[:, k:k + 2, :], writes=[bW1] + bDGc)
                yield

            def Ychain():
                for si, (ci, tiles) in enumerate(stB):
                    yield from Ygen(ci, tiles, si % 2, nxt=(stB[si + 1] if si + 1 < len(stB) else None))

            xc, yc, ec = Xchain(), Ychain(), build_ET()
            alive = {"x": True, "y": True, "e": True}

            def stepg(name, g):
                if alive[name]:
                    try:
                        next(g)
                    except StopIteration:
                        alive[name] = False

            for i_ in range(16):
                stepg("x", xc)
                if i_ == 0:
                    Yloads(*stB[0])
                if i_ % 4 == 3:
                    stepg("e", ec)
            while alive["x"] or alive["y"] or alive["e"]:
                stepg("y", yc)
                stepg("x", xc)
                stepg("e", ec)
            psf_n[0] = 4
            rot[0] = [0, 1, 2, 3, 4, 5]
            psb_n[0] = 2
            S.barrier()

        if PHASES_DONE >= 3:
            cz.reset()
            cb.reset()
            cf.reset()
            PW, bPW = cz.get("PW", [8, 256])
            WO1, bWO1 = cz.get("WO1", [8, 1024])
            BND, bBND = cz.get("BND", [32, 128])
            assert cz.off <= W1OFF
            dma("pool", PW, pool_w.rearrange("g (kc p) e -> p (g kc) e", p=128), writes=[bPW])
            dma("pool", WO1, w_out_1.rearrange("(k p) n -> p k n", p=128), writes=[bWO1])
            dma("pool", BND, band.rearrange("p (a b) -> p a b", a=32), writes=[bBND])
            JUNK, bJUNK = cb.get("junk", [1024])
            NSTR = 2
            NYR = 4
            YR = [[cf.get(f"YR{q}_{i}", [1024]) for i in range(NYR)] for q in range(NSTR)]
            TG = [[cb.get(f"TG{q}_{i}", [512]) for i in range(2)] for q in range(NSTR)]
            TO = [[cf.get(f"TO{q}_{i}", [512]) for i in range(2)] for q in range(NSTR)]
            XN = [cb.get(f"xn{q}", [1024]) for q in range(NSTR)]
            H1 = [cb.get(f"H1{q}", [8, 128]) for q in range(NSTR)]
            H1B = [(Buf("h1lo"), Buf("h1hi")) for q in range(NSTR)]
            U1 = [[cb.get(f"U1{q}_{i}", [1024]) for i in range(4)] for q in range(NSTR)]
            SG1 = [[cb.get(f"SG1{q}_{i}", [8, 128]) for i in range(3)] for q in range(NSTR)]
            DT = [cb.get(f"DT{q}", [8, 128]) for q in range(NSTR)]
            Z1 = [cb.get(f"Z1{q}", [8, 128]) for q in range(NSTR)]

            def zipn(gs):
                gs = [g for g in gs if g is not None]
                while gs:
                    for g in list(gs):
                        try:
                            next(g)
                        except StopIteration:
                            gs.remove(g)
                    yield

            def seq_stream(q, seqs):
                cnt = [0]
                for (ci, tl, nout) in seqs:
                    info = {}

                    def C1(t, ci=ci, info=info):
                        i = cnt[0]
                        cnt[0] += 1
                        info[t] = i
                        yr, byr = YR[q][i % NYR]
                        xn, bxn = XN[q]
                        h1, _ = H1[q]
                        bh1 = H1B[q]
                        u1, bu1 = U1[q][i % 4]
                        sg1, bsg1 = SG1[q][i % 3]
                        gt = t if ci == 0 else 8 + t
                        dma("sp", yr, sY0[gt], [bY0], [byr])
                        norm_to_hT(yr, byr, xn, bxn, JUNK, Buf("j"), h1, bh1, 1, ci, evac_act=True,
                                   rs_pre=(RS1[:, gt:gt + 1], bRS1[gt]))
                        yield
                        for n in range(2):
                            po, bpo = bankf()
                            lst = [(po[:, :], h1[:, k, :], W1[:, k, n * 512:(n + 1) * 512], k == 0, k == 7) for k in range(8)]
                            mms(lst, [*bh1, bW1], [bpo])
                            act(u1[:, n * 512:(n + 1) * 512], po[:, :], AF.Copy, [bpo], [bu1])
                            yield
                        for hf in range(2):
                            po, bpo = bankf()
                            lst = []
                            for bq in range(4):
                                col = 1024 + (hf * 4 + bq) * 128
                                for k in range(8):
                                    lst.append((po[:, bq * 128:(bq + 1) * 128], W1[:, k, col:col + 128], h1[:, k, :], k == 0, k == 7))
                            mms(lst, [*bh1, bW1], [bpo])
                            tg, btg = TG[q][hf]
                            act(tg, po[:, :], AF.Tanh, [bpo], [btg], scale=0.5)
                            stt(sg1[:, hf * 4:(hf + 1) * 4, :].rearrange("p a b -> p (a b)"), tg, 1.0, po[:, :], ALU.add, ALU.mult,
                                [btg, bpo], [bsg1])
                            yield

                    def C2(t, first, last_true, ci=ci, info=info):
                        i = info[t]
                        yr, byr = YR[q][i % NYR]
                        sg1, bsg1 = SG1[q][i % 3]
                        dt, bdt = DT[q]
                        z1, bz1 = Z1[q]
                        nbs = []
                        base = 0 if ci == 0 else 16
                        if not first:
                            nbs.append(((i - 1) % 4, base + 0))
                        nbs.append((i % 4, base + (4 if first else 8)))
                        if not last_true:
                            nbs.append(((i + 1) % 4, base + 12))
                        for hf in range(2):
                            po, bpo = bankf()
                            lst = []
                            for bq in range(4):
                                cc = hf * 4 + bq
                                g = cc // 2
                                for j_, (sl_, bi) in enumerate(nbs):
                                    lst.append((po[:, bq * 128:(bq + 1) * 128], U1[q][sl_][0][:, cc * 128:(cc + 1) * 128],
                                                BND[:, bi + g, :], j_ == 0, j_ == len(nbs) - 1))
                            rd = [U1[q][sl_][1] for (sl_, bi) in nbs]
                            mms(lst, rd + [bBND], [bpo])
                            act(dt[:, hf * 4:(hf + 1) * 4, :].rearrange("p a b -> p (a b)"), po[:, :], AF.Copy, [bpo], [bdt])
                            yield
                        for hf in range(2):
                            po, bpo = bankf()
                            lst = []
                            for bq in range(4):
                                ob_ = hf * 4 + bq
                                g, eb = ob_ // 2, ob_ % 2
                                for kc in range(2):
                                    lst.append((po[:, bq * 128:(bq + 1) * 128], PW[:, g * 2 + kc, eb * 128:(eb + 1) * 128],
                                                dt[:, g * 2 + kc, :], kc == 0, kc == 1))
                            mms(lst, [bPW, bdt], [bpo])
                            for bq in range(4):
                                ob_ = hf * 4 + bq
                                stt(z1[:, ob_, :], po[:, bq * 128:(bq + 1) * 128], PRM[:, 8 + ob_:9 + ob_], sg1[:, ob_, :],
                                    ALU.mult, ALU.mult, [bpo, bPRM, bsg1], [bz1])
                            yield
                        for n in range(2):
                            to, bto = TO[q][n]
                            po, bpo = bankf()
                            lst = [(po[:, :], z1[:, k, :], WO1[:, k, n * 512:(n + 1) * 512], k == 0, k == 7) for k in range(8)]
                            mms(lst, [bz1, bWO1], [bpo])
                            tt(to, po[:, :], G[:, 2 + ci, n * 512:(n + 1) * 512], ALU.mult, [bpo, bGl[1]], [bto])
                            tt(yr[:, n * 512:(n + 1) * 512], to, yr[:, n * 512:(n + 1) * 512], ALU.add, [bto, byr], [byr],
                               eng="pool")
                            yield
                        dst = yp[t * 128:(t + 1) * 128, :] if ci == 0 else ys[t * 128:(t + 1) * 128, :]
                        dma("pool", dst, yr, [byr], [])

                    yield from zipn([C1(tl[0])])
                    if len(tl) > 1:
                        yield from zipn([C1(tl[1])])
                    for i_, t in enumerate(tl):
                        g1 = C1(tl[i_ + 2]) if i_ + 2 < len(tl) else None
                        g2 = C2(t, i_ == 0, (ci == 0 and i_ == len(tl) - 1)) if i_ < nout else None
                        yield from zipn([g1, g2])

            seqP = [(0, [2 * b, 2 * b + 1], 2) for b in range(4)]
            seqS = [(1, list(range(NTB)), 16)]
            gS, gP = seq_stream(0, seqS), seq_stream(1, seqP)
            alive = [True, True]
            while any(alive):
                for gi, (g, reps) in enumerate(((gS, 1), (gP, 1))):
                    for _ in range(reps):
                        if alive[gi]:
                            try:
                                next(g)
                            except StopIteration:
                                alive[gi] = False

        S.finish()
        S.emit(block, sems, dsems)
    return nc


build_program.phases = 3
build_program.nst = 99
DG_ENG = "dve"


def _att_tables(rpb, flip):
    pats = [(10, 10 + d) for d in (-2, -1, 0, 1, 2)] + [(0, j) for j in range(4)] + [(1, j) for j in range(4)]
    idx = np.arange(128)
    bias = np.zeros((8, 13, 128, 128), np.float32)
    mask = np.zeros((13, 128, 128), np.float32)
    for pi, (t, j) in enumerate(pats):
        ik = 2 * j + idx // 64
        ckl = idx % 64
        iq = 2 * t + idx // 64
        cql = idx % 64
        if flip:
            rk, ck_, rq, cq = 63 - ik, 63 - ckl, 63 - iq, 63 - cql
        else:
            rk, ck_, rq, cq = ik, ckl, iq, cql
        RK, RQ = rk[:, None], rq[None, :]
        CK_, CQ = ck_[:, None], cq[None, :]
        start = np.clip(RQ - 4, 0, 56)
        rowok = (RK >= start) & (RK < start + 8)
        qcs = np.clip(CQ - 8, 0, 48)
        colok = (CK_ >= qcs) & (CK_ < qcs + 16)
        dr = np.clip(RK - RQ + 7, 0, 14)
        dc = np.clip(CK_ - CQ + 15, 0, 30)
        mask[pi] = (rowok & colok).astype(np.float32)
        dr = np.broadcast_to(dr, (128, 128))
        dc = np.broadcast_to(dc, (128, 128))
        bias[:, pi] = rpb[:, dr, dc]
    bias = np.where(mask[None] > 0, bias, np.float32(-30000.0)).astype(np.float32)
    eb = np.ascontiguousarray(bias.transpose(2, 0, 1, 3)).reshape(128, 8 * 13 * 128)
    em = np.ascontiguousarray(mask.transpose(1, 0, 2)).reshape(128, 13 * 128)
    return eb, em


def _band_tables2(flip):
    out = np.zeros((32, 128, 128), np.float32)
    wins = (2, 4, 8, 16)
    for sset in range(2):
        fl = flip
        for g, w in enumerate(wins):
            lo, hi = -(w // 2), w - w // 2
            if fl:
                lo, hi = -(w - w // 2) + 1, w // 2 + 1
            for role in range(4):
                M = np.zeros((128, 128), np.float32)
                for t in range(128):
                    a, b = t + lo, t + hi
                    ca, cbb = a, b
                    if role == 1:
                        ca = max(a, 0)
                    if role == 2 and sset == 0:
                        cbb = min(b, 128)
                    if role in (0, 3):
                        cnt = w
                    else:
                        cnt = cbb - ca
                    for s in range(ca, cbb):
                        if role == 0 and s < 0:
                            M[s + 128, t] += 1.0 / cnt
                        elif role == 3 and s >= 128:
                            M[s - 128, t] += 1.0 / cnt
                        elif role in (1, 2) and 0 <= s < 128:
                            M[s, t] += 1.0 / cnt
                    if role in (1, 2):
                        M[t, t] -= 1.0
                out[sset * 16 + role * 4 + g] = M
    return np.ascontiguousarray(out.transpose(1, 0, 2)).reshape(128, 32 * 128)


_CACHE = {}


def kernel(x_prompt, x_sample, cache_k_0, cache_v_0, c, c_ctx,
           norm_g_0, w_ada_0, b_ada_0, w_in_0, conv_w_0, conv_b_0, conv_ln_g_0, conv_ln_b_0,
           q_norm_0, k_norm_0, rpb_0, w_out_0,
           norm_g_1, w_ada_1, b_ada_1, w_in_1, pool_w_1, pool_scale_1, w_out_1):
    f = lambda a: np.ascontiguousarray(np.asarray(a, dtype=np.float32))
    x_prompt, x_sample, cache_k_0, cache_v_0, c, c_ctx = map(f, (x_prompt, x_sample, cache_k_0, cache_v_0, c, c_ctx))
    if "nc" not in _CACHE:
        _CACHE["nc"] = build_program()
    nc = _CACHE["nc"]
    fm = lambda v, nb: f(v).reshape(nb, 128).T
    shared = dict(w_ada_0=f(w_ada_0), w_ada_1=f(w_ada_1), w_in_0=f(w_in_0), w_out_0=f(w_out_0), w_in_1=f(w_in_1),
                  pool_w=f(pool_w_1), w_out_1=f(w_out_1), ident=np.eye(128, dtype=np.float32),
                  bdiag=np.kron(np.eye(2, dtype=np.float32), np.ones((64, 64), np.float32)),
                  ones32=np.full((128, 128), 1.0 / 512, np.float32))
    bgr = np.concatenate([np.broadcast_to(f(b_ada_0)[2048:], (128, 1024)),
                          np.broadcast_to(f(b_ada_1)[2048:], (128, 1024))], axis=1)
    shared["bg"] = np.ascontiguousarray(bgr)
    tabs = {fl: (_att_tables(f(rpb_0), fl), _band_tables2(fl)) for fl in (False, True)}
    in_maps = []
    for i in range(8):
        b, half = i // 2, i % 2
        flip = half == 1
        sm = np.zeros((128, NSM), np.float32)
        sm[:, NG0:NG0 + 8] = fm(norm_g_0, 8)
        sm[:, NG1:NG1 + 8] = fm(norm_g_1, 8)
        sm[:, BA0:BA0 + 24] = fm(b_ada_0, 24)
        sm[:, BA1:BA1 + 24] = fm(b_ada_1, 24)
        cw = f(conv_w_0)[::-1] if flip else f(conv_w_0)
        sm[:, CW:CW + 124] = cw.T.reshape(4, 128, 31).transpose(1, 0, 2).reshape(128, 124)
        sm[:, CB:CB + 4] = fm(conv_b_0, 4)
        sm[:, LG:LG + 4] = fm(conv_ln_g_0, 4)
        sm[:, LB:LB + 4] = fm(conv_ln_b_0, 4)
        sm[:, QN] = np.tile(f(q_norm_0), 2)
        sm[:, KN] = np.tile(f(k_norm_0), 2)
        sm[:, PSC:PSC + 8] = fm(pool_scale_1, 8)
        cv2 = np.stack([fm(c_ctx, 8), fm(c[b], 8)], axis=2)
        sm[:, CVEC:CVEC + 16] = cv2.reshape(128, 16)
        sm[:, KNR:KNR + 64] = np.broadcast_to(f(k_norm_0), (128, 64))
        xsb = x_sample[b][::-1] if flip else x_sample[b]
        (eb, em), bnd = tabs[flip]
        m = dict(shared)
        xpb = x_prompt[4 * i:4 * i + 4][:, ::-1] if flip else x_prompt[4 * i:4 * i + 4]
        m.update(xp=np.ascontiguousarray(xpb).reshape(1024, 1024),
                 xs=np.ascontiguousarray(xsb[:TS]),
                 ckT=np.ascontiguousarray(cache_k_0[b].reshape(512, 512).T),
                 cvv=np.ascontiguousarray(cache_v_0[b].reshape(512, 512)),
                 smallp=sm, ebias=eb, emask=em, band=bnd)
        in_maps.append(m)
    if _CACHE.get("in_maps_only"):
        return in_maps
    res = run_bass_kernel_spmd(nc, in_maps, core_ids=list(range(8)))
    y_prompt = np.zeros((32, 256, 1024), np.float32)
    y_sample = np.zeros((4, 4096, 1024), np.float32)
    k_ctx = np.zeros((32, 256, 8, 64), np.float32)
    v_ctx = np.zeros((32, 256, 8, 64), np.float32)
    for i in range(8):
        r = res.results[i]
        b, half = i // 2, i % 2
        sl = slice(None, None, -1) if half == 1 else slice(None)
        y_prompt[4 * i:4 * i + 4] = r["yp"].reshape(4, 256, 1024)[:, sl]
        k_ctx[4 * i:4 * i + 4] = r["ko"].reshape(4, 256, 8, 64)[:, sl]
        v_ctx[4 * i:4 * i + 4] = r["vo"].reshape(4, 256, 8, 64)[:, sl]
        if half == 0:
            y_sample[b, :2048] = r["ys"]
        else:
            y_sample[b, 2048:] = r["ys"][::-1]
    return (y_prompt, y_sample, k_ctx, v_ctx)
```

```python
import numpy as np
from contextlib import ExitStack
import concourse.bass as bass
import concourse.mybir as mybir
from concourse.bass_utils import run_bass_kernel_spmd

F32 = mybir.dt.float32
BF16 = mybir.dt.bfloat16
ALU = mybir.AluOpType
AF = mybir.ActivationFunctionType
EPS = 1e-6
NTS = 19
NTB = 17
TS = NTS * 128
TTOT = 1024 + TS
UB = 4 * 288
ULEN = UB + 16 + TS + 16
NG0, NG1, BA0, BA1, CW, CB, LG, LB, QN, KN, PSC, CVEC, KNR, NSM = 0, 8, 16, 40, 64, 188, 192, 196, 200, 201, 202, 210, 226, 290


class Buf:
    __slots__ = ("name", "lw", "rs", "excl")

    def __init__(self, name, excl=False):
        self.name = name
        self.excl = excl
        self.lw = None
        self.rs = []


class Sched:
    ENG = ["pe", "act", "dve", "pool", "sp"]

    def __init__(self, ndma=24):
        self.ops = {e: [] for e in self.ENG}
        self.ndma = ndma
        self.dma_count = [0] * ndma
        self.dma_next = [0, 0]
        self.bar = {e: [] for e in self.ENG}

    def add(self, eng, fn, reads=(), writes=(), dma=False):
        deps = set(self.bar[eng])
        self.bar[eng] = []
        idx = len(self.ops[eng])
        if dma:
            half = self.ndma // 2
            qi = 0 if eng == "sp" else 1
            k = qi * half + self.dma_next[qi]
            self.dma_next[qi] = (self.dma_next[qi] + 1) % half
            if self.dma_count[k] > 0:
                deps.add(("dma", k, 16 * self.dma_count[k]))
            self.dma_count[k] += 1
            ref = ("dma", k, 16 * self.dma_count[k])
        else:
            ref = ("op", eng, idx)
        excl_reads = [b for b in reads if b.excl]
        if excl_reads:
            reads = [b for b in reads if not b.excl]
            writes = list(writes) + excl_reads
        for b in reads:
            if b.lw is not None:
                deps.add(b.lw)
        for b in writes:
            if b.lw is not None:
                deps.add(b.lw)
            deps.update(b.rs)
        for b in reads:
            b.rs.append(ref)
        for b in writes:
            b.lw = ref
            b.rs = []
        deps.discard(ref)
        if eng == "pe":
            deps = {d for d in deps if not (d[0] == "op" and d[1] == "pe")}
        self.ops[eng].append(dict(fn=fn, deps=deps, ref=ref, dma=dma))
        return ref

    def barrier(self, skip_sems=()):
        refs = []
        for e in self.ENG:
            for op in reversed(self.ops[e]):
                if op["fn"] is not None and not op["dma"]:
                    refs.append(op["ref"])
                    break
        for k in range(self.ndma):
            if self.dma_count[k] > 0 and k not in skip_sems:
                refs.append(("dma", k, 16 * self.dma_count[k]))
        for e in self.ENG:
            self.bar[e] = list(refs)

    def finish(self):
        self.barrier()
        for e in self.ENG:
            self.ops[e].append(dict(fn=None, deps=set(self.bar[e]), ref=None, dma=False))
            self.bar[e] = []

    def emit(self, block, sems, dma_sems):
        need = {e: set() for e in self.ENG}
        for e in self.ENG:
            for op in self.ops[e]:
                for d in op["deps"]:
                    if d[0] == "op":
                        need[d[1]].add(d[2])
        count = {}
        for e in self.ENG:
            count[e] = {}
            c = 0
            for idx in sorted(need[e]):
                c += 1
                count[e][idx] = c

        def run(e):
            def f(eng):
                waited = {}
                for idx, op in enumerate(self.ops[e]):
                    for d in sorted(op["deps"], key=str):
                        if d[0] == "op":
                            key = ("op", d[1])
                            val = count[d[1]][d[2]]
                            sem = sems[d[1]]
                        else:
                            key = ("dma", d[1])
                            val = d[2]
                            sem = dma_sems[d[1]]
                        if waited.get(key, 0) >= val:
                            continue
                        eng.wait_ge(sem, val)
                        waited[key] = val
                    if op["fn"] is None:
                        continue
                    ins = op["fn"](eng)
                    if op["dma"]:
                        ins.then_inc(dma_sems[op["ref"][1]], 16)
                    elif idx in count[e]:
                        ins.then_inc(sems[e], 1)
            return f

        block.tensor(run("pe"))
        block.scalar(run("act"))
        block.vector(run("dve"))
        block.gpsimd(run("pool"))
        block.sync(run("sp"))


def pat_index(t, j):
    if t >= 2:
        return j - t + 2
    return 5 + 4 * t + j


def key_tiles(t):
    return list(range(max(t - 2, 0), max(t + 2, 3) + 1))


def build_program():
    nc = bass.Bass("TRN2", target_bir_lowering=False)
    S = Sched()

    def din(name, shape):
        return nc.dram_tensor(name, list(shape), F32, kind="ExternalInput").ap()

    def dout(name, shape):
        return nc.dram_tensor(name, list(shape), F32, kind="ExternalOutput").ap()

    xp = din("xp", [1024, 1024])
    xs = din("xs", [TS, 1024])
    ckT = din("ckT", [512, 512])
    cvv = din("cvv", [512, 512])
    smallp = din("smallp", [128, NSM])
    bg = din("bg", [128, 2048])
    w_ada = [din("w_ada_0", [1024, 3072]), din("w_ada_1", [1024, 3072])]
    w_in_0 = din("w_in_0", [1024, 3584])
    w_out_0 = din("w_out_0", [1024, 1024])
    w_in_1 = din("w_in_1", [1024, 2048])
    pool_w = din("pool_w", [4, 256, 256])
    w_out_1 = din("w_out_1", [1024, 1024])
    ebias = din("ebias", [128, 8 * 13 * 128])
    emask = din("emask", [128, 13 * 128])
    band = din("band", [128, 32 * 128])
    ident = din("ident", [128, 128])
    bdiag = din("bdiag", [128, 128])
    ones32 = din("ones32", [128, 128])
    yp = dout("yp", [1024, 1024])
    ys = dout("ys", [2048, 1024])
    ko = dout("ko", [1024, 512])
    vo = dout("vo", [1024, 512])
    sU = nc.dram_tensor("sU", [4, 128, ULEN], BF16).ap()
    sGA = nc.dram_tensor("sGA", [4, 128, TTOT], BF16).ap()
    sQ = nc.dram_tensor("sQ", [4, 128, TTOT], BF16).ap()
    sK = nc.dram_tensor("sK", [4, 128, TTOT], BF16).ap()
    sV = nc.dram_tensor("sV", [27, 128, 520], BF16).ap()
    sGB = nc.dram_tensor("sGB", [27, 128, 512], BF16).ap()
    sY0 = nc.dram_tensor("sY0", [25, 128, 1024], F32).ap()
    bU, bGA, bQ, bK, bV, bGB, bY0 = (Buf(n) for n in ["sU", "sGA", "sQ", "sK", "sV", "sGB", "sY0"])

    with ExitStack() as es:
        ZB = es.enter_context(nc.sbuf_tensor("ZB", [128, 43008], BF16))
        WKB = es.enter_context(nc.sbuf_tensor("WKB", [128, 29440], BF16))
        WK = es.enter_context(nc.sbuf_tensor("WK", [128, 10752], F32))
        SM = es.enter_context(nc.sbuf_tensor("SM", [128, NSM], F32))
        G = es.enter_context(nc.sbuf_tensor("G", [128, 4, 1024], F32))
        MOD = es.enter_context(nc.sbuf_tensor("MOD", [128, 2, 32], F32))
        AM = es.enter_context(nc.sbuf_tensor("AM", [128, 4, 8], F32))
        IDB = es.enter_context(nc.sbuf_tensor("IDB", [128, 128], BF16))
        BDB = es.enter_context(nc.sbuf_tensor("BDB", [128, 128], BF16))
        ON32 = es.enter_context(nc.sbuf_tensor("ON32", [128, 128], F32))
        SC = es.enter_context(nc.sbuf_tensor("SC", [128, 16], F32))
        ST = es.enter_context(nc.sbuf_tensor("ST", [128, 64], F32))
        PRM = es.enter_context(nc.sbuf_tensor("PRM", [128, 80], F32))
        CWHT = es.enter_context(nc.sbuf_tensor("CWHT", [128, 124], F32))
        RS1 = es.enter_context(nc.sbuf_tensor("RS1", [128, 32], F32))
        bRS1 = [Buf(f"rs1_{i}") for i in range(32)]
        PSF = [es.enter_context(nc.psum_tensor(f"psf{i}", [128, 512], F32)) for i in range(7)]
        PSB0 = es.enter_context(nc.psum_tensor("psb0", [128, 1024], BF16))
        bPSF = [Buf(f"psf{i}", True) for i in range(7)]
        PSB = [PSB0, PSF[6][:, :].bitcast(BF16)]
        bPSB = [Buf("psb0", True), bPSF[6]]
        sems = {e: es.enter_context(nc.semaphore("s_" + e)) for e in S.ENG}
        dsems = [es.enter_context(nc.semaphore(f"d{i}")) for i in range(S.ndma)]
        block = es.enter_context(nc.Block())

        bSM, bG, bMOD, bAM, bIDB, bBDB, bON, bSC, bPRM = (Buf(n) for n in
                                                             ["SM", "G", "MOD", "AM", "IDB", "BDB", "ON", "SC", "PRM"])
        st_ctr = [0]
        stat_bufs = [Buf(f"stc{i}") for i in range(8)]
        TC = es.enter_context(nc.sbuf_tensor("TC", [128, 16], F32))

        def stat():
            s_ = st_ctr[0] % 8
            st_ctr[0] += 1
            return ST[:, 8 * s_:8 * s_ + 8], stat_bufs[s_]

        psf_i = [0]
        psb_i = [0]
        psf_n = [4]
        psb_n = [2]
        rot = [[0, 1, 2, 3, 4, 5]]

        def bankf():
            i = psf_i[0]
            i = i % psf_n[0]
            psf_i[0] = (i + 1) % psf_n[0]
            i = rot[0][i]
            return PSF[i], bPSF[i]

        def bankb():
            i = psb_i[0] % psb_n[0]
            psb_i[0] = (i + 1) % psb_n[0]
            return PSB[i], bPSB[i]

        class Carver:
            def __init__(self, t, size):
                self.t, self.size, self.off = t, size, 0

            def reset(self):
                self.off = 0

            def get(self, name, shape):
                n = int(np.prod(shape))
                assert self.off + n <= self.size, (name, self.off, n, self.size)
                v = self.t[:, self.off:self.off + n]
                self.off += n
                if len(shape) == 2:
                    v = v.rearrange("p (a b) -> p a b", a=shape[0])
                elif len(shape) == 3:
                    v = v.rearrange("p (a b c) -> p a b c", a=shape[0], b=shape[1])
                return v, Buf(name)

        cz, cb, cf = Carver(ZB, 43008), Carver(WKB, 29440), Carver(WK, 10752)

        def dma(q, out, in_, reads=(), writes=()):
            return S.add(q, lambda e: e.dma_start(out=out, in_=in_), reads, writes, dma=True)

        def act(out, in_, func, reads, writes, bias=None, scale=None, accum=None):
            def fn(e):
                kw = {}
                if bias is not None:
                    kw["bias"] = bias
                if scale is not None:
                    kw["scale"] = scale
                if accum is not None:
                    kw["accum_out"] = accum
                return e.activation(out=out, in_=in_, func=func, **kw)
            return S.add("act", fn, reads, writes)

        def tt(out, in0, in1, op, reads, writes, eng="dve"):
            return S.add(eng, lambda e: e.tensor_tensor(out=out, in0=in0, in1=in1, op=op), reads, writes)

        def tsc(out, in0, s1, s2, op0, op1, reads, writes, eng="dve"):
            if s2 is None:
                return S.add(eng, lambda e: e.tensor_scalar(out=out, in0=in0, scalar1=s1, scalar2=None, op0=op0),
                             reads, writes)
            return S.add(eng, lambda e: e.tensor_scalar(out=out, in0=in0, scalar1=s1, scalar2=s2, op0=op0, op1=op1),
                         reads, writes)

        def stt(out, in0, scalar, in1, op0, op1, reads, writes, eng="dve"):
            return S.add(eng, lambda e: e.scalar_tensor_tensor(out=out, in0=in0, scalar=scalar, in1=in1,
                                                               op0=op0, op1=op1), reads, writes)

        def cpy(out, in_, reads, writes, eng="dve"):
            return S.add(eng, lambda e: e.tensor_copy(out=out, in_=in_), reads, writes)

        def mms(lst, reads, writes):
            def fn(e):
                ins = None
                for (o, l, r, st, sp) in lst:
                    ins = e.matmul(o, lhsT=l, rhs=r, start=st, stop=sp)
                return ins
            return S.add("pe", fn, reads, writes)

        def transposes(lst, reads, writes):
            def fn(e):
                ins = None
                for (o, i) in lst:
                    ins = e.transpose(out=o, in_=i, identity=IDB[:])
                return ins
            return S.add("pe", fn, list(reads) + [bIDB], writes)

        def rsqrt_act(out, in_, reads, writes, bias_ap, scale=1.0):
            act(out, in_, AF.Ln, reads, writes, bias=bias_ap, scale=scale)
            act(out, out, AF.Exp, writes, writes, scale=-0.5)

        dma("sp", SM[:], smallp, writes=[bSM])
        dma("sp", ON32[:], ones32, writes=[bON])
        dma("pool", IDB[:], ident, writes=[bIDB])
        dma("pool", BDB[:], bdiag, writes=[bBDB])
        W0, bW0 = cz.get("W0", [8, 3584])
        w0v = w_in_0.rearrange("(k p) n -> p k n", p=128)
        bW0g = [Buf(f"W0g{g}") for g in range(7)]
        w0_sems = set()
        for g in (0, 1, 2, 5, 6, 3, 4):
            r_ = dma("pool", W0[:, :, g * 512:(g + 1) * 512], w0v[:, :, g * 512:(g + 1) * 512], writes=[bW0g[g]])
            w0_sems.add(r_[1])
        S.add("dve", lambda e: e.memset(PRM[:, 0:1], EPS), writes=[bPRM])
        S.add("dve", lambda e: e.memset(PRM[:, 1:2], 64 * EPS), writes=[bPRM])
        cpy(PRM[:, 2:3], SM[:, QN:QN + 1], [bSM], [bPRM])
        tsc(PRM[:, 3:4], SM[:, KN:KN + 1], 8.0, None, ALU.mult, None, [bSM], [bPRM])
        tsc(PRM[:, 8:16], SM[:, PSC:PSC + 8], 0.5, None, ALU.mult, None, [bSM], [bPRM])
        tsc(PRM[:, 16:80], SM[:, KNR:KNR + 64], 8.0, None, ALU.mult, None, [bSM], [bPRM])
        tanh_c, btc = TC[:], Buf('TC')
        act(tanh_c, SM[:, CVEC:CVEC + 16], AF.Tanh, [bSM], [btc], scale=0.5)
        stt(SC[:], tanh_c, 1.0, SM[:, CVEC:CVEC + 16], ALU.add, ALU.mult, [btc, bSM], [bSC])
        tsc(SC[:], SC[:], 0.5, None, ALU.mult, None, [bSC], [bSC])
        cf.reset()
        cb.reset()
        bMODl = [Buf("MOD0"), Buf("MOD1")]
        bAMl = [Buf("AM0"), Buf("AM1")]
        bGl = [Buf("G0"), Buf("G1")]
        WA = [cb.get(f"WAb{i}", [8, 256]) for i in range(4)]
        WF = [cf.get(f"WAf{i}", [8, 256]) for i in range(4)]
        SCB, bSCB = cz.get("SCBb", [16, 128])
        SCh, bSCh = cz.get("SCh", [32])
        WAq = [cz.get(f"WAq{i}", [8, 128]) for i in range(2)]
        cpy(SCh[:, 0:16], SC[:, 0:16], [bSC], [bSCh])
        cpy(SCB, SC[:, 0:16].unsqueeze(2).to_broadcast([128, 16, 128]), [bSC], [bSCB])

        def finish_mod(l, pfm, bpfm):
            ba = BA0 if l == 0 else BA1
            ng = NG0 if l == 0 else NG1
            tt(MOD[:, l, :].rearrange("p (a b) -> p a b", b=2), pfm[:, 0:32].rearrange("p (a b) -> p a b", b=2),
               SM[:, ba:ba + 16].unsqueeze(2).to_broadcast([128, 16, 2]), ALU.add, [bpfm, bSM], [bMODl[l]])
            mv = MOD[:, l, :].rearrange("p (a b) -> p a b", b=2)
            for ci in range(2):
                stt(AM[:, l * 2 + ci, :], mv[:, 8:16, ci], 1.0, SM[:, ng:ng + 8], ALU.add, ALU.mult,
                    [bMODl[l], bSM], [bAMl[l]])

        wav0 = w_ada[0].rearrange("(k p) n -> p k n", p=128)
        pfm, bpfm = PSF[4], bPSF[4]
        for g in range(4):
            dma("sp" if g % 2 == 0 else "act", WF[g][0], wav0[:, :, g * 256:(g + 1) * 256], writes=[WF[g][1]])
        for g in range(8):
            wa, bwa = WA[g % 4]
            wf, bwf = WF[g % 4]
            if g % 2 == 0:
                cpy(wa, wf, [bwf], [bwa])
            else:
                act(wa, wf, AF.Copy, [bwf], [bwa])
            if g + 4 < 8:
                dma("sp" if g % 2 == 0 else "act", wf, wav0[:, :, (g + 4) * 256:(g + 5) * 256], writes=[bwf])
            lst = []
            for blk in range(2):
                o = pfm[:, (g * 2 + blk) * 2:(g * 2 + blk) * 2 + 2]
                for k in range(8):
                    lst.append((o, wa[:, k, blk * 128:(blk + 1) * 128], SCh[:, 2 * k:2 * k + 2], k == 0, k == 7))
            mms(lst, [bwa, bSCh], [bpfm])
        finish_mod(0, pfm, bpfm)

        def modgen(WFq, BGq):
            pieces = [(0, "g", j) for j in range(8)] + [(1, "f", j) for j in range(16)] + [(1, "g", j) for j in range(8)]
            pf1, bpf1 = PSF[5], bPSF[5]

            def issue(i):
                l, kind, j = pieces[i]
                col0 = (2048 if kind == "g" else 0) + j * 128
                wv = w_ada[l].rearrange("(k p) n -> p k n", p=128)
                dma("sp", WFq[i % 2][0], wv[:, :, col0:col0 + 128], writes=[WFq[i % 2][1]])
                if kind == "g":
                    dma("sp", BGq[i % 2][0], bg[:, l * 1024 + j * 128:l * 1024 + (j + 1) * 128], writes=[BGq[i % 2][1]])

            issue(0)
            yield
            for i, (l, kind, j) in enumerate(pieces):
                if i + 1 < len(pieces):
                    issue(i + 1)
                    yield
                wf, bwf = WFq[i % 2]
                wa, bwa = WAq[i % 2]
                if i % 2 == 0:
                    cpy(wa, wf, [bwf], [bwa])
                else:
                    act(wa, wf, AF.Copy, [bwf], [bwa])
                yield
                if kind == "f":
                    lst = [(pf1[:, j * 2:j * 2 + 2], wa[:, k, :], SCh[:, 2 * k:2 * k + 2], k == 0, k == 7) for k in range(8)]
                    mms(lst, [bwa, bSCh], [bpf1])
                    if j == 15:
                        finish_mod(1, pf1, bpf1)
                else:
                    pg, bpg = bankf()
                    lst = []
                    for ci in range(2):
                        for k in range(8):
                            lst.append((pg[:, ci * 128:(ci + 1) * 128], SCB[:, 2 * k + ci, :], wa[:, k, :], k == 0, k == 7))
                    mms(lst, [bwa, bSCB], [bpg])
                    for ci in range(2):
                        tt(G[:, l * 2 + ci, j * 128:(j + 1) * 128], pg[:, ci * 128:(ci + 1) * 128], BGq[i % 2][0],
                           ALU.add, [bpg, BGq[i % 2][1]], [bGl[l]])
                yield

        def Asc(l, ci, k):
            return AM[:, l * 2 + ci, k:k + 1]

        def Bsc(l, ci, k):
            return MOD[:, l, 2 * k + ci:2 * k + ci + 1]

        S.barrier(skip_sems=w0_sems)

        def norm_to_hT(Xt, bX, xn, bxn, junk, bjunk, hT_dst, bhT, l, ci, evac_act=False, rs_pre=None, mid=None):
            if rs_pre is not None:
                rs, bs = rs_pre
            else:
                st_, bs = stat()
                ss, rs = st_[:, 0:1], st_[:, 1:2]
                act(junk, Xt, AF.Square, [bX], [bjunk, bs], accum=ss)
                act(rs, ss, AF.Ln, [bs, bPRM], [bs], bias=PRM[:, 0:1], scale=1.0 / 1024)
                act(rs, rs, AF.Exp, [bs], [bs], scale=-0.5)
            tsc(xn, Xt, rs, None, ALU.mult, None, [bX, bs], [bxn])
            for hf in range(2):
                pb, bpb = PSB[hf], bPSB[hf]
                transposes([(pb[:, j * 128:(j + 1) * 128], xn[:, (hf * 4 + j) * 128:(hf * 4 + j + 1) * 128])
                            for j in range(4)], [bxn], [bpb])
            if mid is not None:
                mid()
            for j in range(4):
                for hf in range(2):
                    k = hf * 4 + j
                    pb, bpb = PSB[hf], bPSB[hf]
                    if hf == 0:
                        act(hT_dst[:, k, :], pb[:, j * 128:(j + 1) * 128], AF.Identity, [bpb, bAMl[l], bMODl[l]], [bhT[0]],
                            scale=Asc(l, ci, k), bias=Bsc(l, ci, k))
                    else:
                        tsc(hT_dst[:, k, :], pb[:, j * 128:(j + 1) * 128], Asc(l, ci, k), Bsc(l, ci, k), ALU.mult, ALU.add,
                            [bpb, bAMl[l], bMODl[l]], [bhT[1]])

        def run_streams(gens, width, stagger=0, bgen=None):
            active = []
            it = iter(gens)
            if stagger:
                g0 = next(it, None)
                if g0 is not None:
                    active.append(g0)
                    for _ in range(stagger):
                        try:
                            next(g0)
                        except StopIteration:
                            active.remove(g0)
                            break
            while True:
                while len(active) < width:
                    g = next(it, None)
                    if g is None:
                        break
                    active.append(g)
                if not active:
                    break
                for g in list(active):
                    try:
                        next(g)
                    except StopIteration:
                        active.remove(g)
                if bgen is not None:
                    try:
                        next(bgen)
                    except StopIteration:
                        bgen = None
            if bgen is not None:
                for _ in bgen:
                    pass

        cb.reset()
        cf.reset()
        psf_n[0] = 5
        X_ = [cf.get(f"X{i}", [1024]) for i in range(4)]
        TMP = [cf.get(f"TMP{i}", [512]) for i in range(4)]
        K32 = [cf.get(f"K32{i}", [512]) for i in range(2)]
        V32 = [cf.get(f"V32{i}", [512]) for i in range(2)]
        WFq = [cf.get(f"WFq{i}", [8, 128]) for i in range(2)]
        BGq = [cf.get(f"BGq{i}", [128]) for i in range(2)]
        JUNK, bJUNK = cb.get("junk", [1024])
        XN = [cb.get(f"xn{i}", [1024]) for i in range(2)]
        HT = [cb.get(f"HT{i}", [8, 512]) for i in range(2)]
        HTB = [(Buf("hlo"), Buf("hhi")) for i in range(2)]
        UTs_ = [cb.get("UTs0", [4, 512]), cz.get("UTs1", [4, 512])]
        SGAs_ = [cb.get("SGAs0", [4, 512]), cz.get("SGAs1", [4, 512])]
        QTs_ = [cb.get("QTs0", [4, 512]), cz.get("QTs1", [4, 512])]
        KTs_ = [cb.get("KTs0", [4, 512]), cz.get("KTs1", [4, 512])]
        VST = [cb.get(f"VST{i}", [4, 8, 65]) for i in range(2)]
        SGBS = [cb.get(f"SGBS{i}", [4, 512]) for i in range(2)]
        SQ = [cb.get(f"SQ{i}", [512]) for i in range(2)]
        ZERO, bZERO = cb.get("zero", [4, 16])
        S.add("dve", lambda e: e.memset(ZERO, 0.0), writes=[bZERO])
        for i in range(2):
            S.add("dve", lambda e, i=i: e.memset(VST[i][0][:, :, :, 64:65], 1.0), writes=[VST[i][1]])
        sUv = sU.rearrange("c p t -> p c t")
        for b in range(4):
            dma("pool", sUv[:, :, b * 288:b * 288 + 16], ZERO, [bZERO], [bU])
            dma("pool", sUv[:, :, b * 288 + 272:b * 288 + 288], ZERO, [bZERO], [bU])
        dma("pool", sUv[:, :, UB:UB + 16], ZERO, [bZERO], [bU])

        supertiles = [(0, list(range(0, 4))), (0, list(range(4, 8)))]
        supertiles += [(1, list(range(8 + a, 8 + min(a + 4, NTS)))) for a in range(0, NTS, 4)]
        if build_program.phases < 1:
            supertiles = []
        supertiles = supertiles[:build_program.nst]

        xloaded = set()

        def stA(sti, ci, tiles):
            sl = sti % 2
            nt = len(tiles)
            N = nt * 128
            hT, bhT0 = HT[sl]
            bhT = HTB[sl]
            vst, bvst = VST[sl]
            sgbs, bsgbs = SGBS[sl]
            UTs, bUTs = UTs_[sl]
            SGAs, bSGAs = SGAs_[sl]
            QTs, bQTs = QTs_[sl]
            KTs, bKTs = KTs_[sl]
            def xload(sti_, jj_):
                ci_, tiles_ = supertiles[sti_]
                tg_ = tiles_[jj_]
                Xt_, bX_ = X_[(sti_ % 2) * 2 + jj_ % 2]
                src = xp[tg_ * 128:(tg_ + 1) * 128, :] if ci_ == 0 else xs[(tg_ - 8) * 128:(tg_ - 7) * 128, :]
                dma("sp", Xt_, src, writes=[bX_])
                xloaded.add((sti_, jj_))

            for jj in range(min(2, nt)):
                if (sti, jj) not in xloaded:
                    xload(sti, jj)
            rsn = {}

            def stats_of(j_):
                st_, bs_ = stat()
                Xj, bXj = X_[sl * 2 + j_ % 2]
                act(JUNK, Xj, AF.Square, [bXj], [Buf("j"), bs_], accum=st_[:, 0:1])
                act(st_[:, 1:2], st_[:, 0:1], AF.Ln, [bs_, bPRM], [bs_], bias=PRM[:, 0:1], scale=1.0 / 1024)
                act(st_[:, 1:2], st_[:, 1:2], AF.Exp, [bs_], [bs_], scale=-0.5)
                rsn[j_] = (st_[:, 1:2], bs_)

            stats_of(0)
            for jj, tg in enumerate(tiles):
                Xt, bX = X_[sl * 2 + jj % 2]
                xn, bxn = XN[sl]
                nxt_stats = (lambda j_=jj + 1: stats_of(j_)) if jj + 1 < nt else None
                norm_to_hT(Xt, bX, xn, bxn, JUNK, Buf("j"), hT[:, :, jj * 128:(jj + 1) * 128], bhT, 0, ci,
                           rs_pre=rsn[jj], mid=nxt_stats)
                if jj + 2 < nt:
                    xload(sti, jj + 2)
                yield
            if sti + 2 < len(supertiles):
                for jj in range(min(2, len(supertiles[sti + 2][1]))):
                    xload(sti + 2, jj)

            last = (ci == 1 and tiles[0] - 8 == 16)
            Nab, Nga, Nq, ngb = (144, 128, 128, 1) if last else (N, N, N, nt)

            def fm_group(col0, n):
                pb_, bpb_ = bankf()
                lst = [(pb_[:, :n], W0[:, k, col0:col0 + 128], hT[:, k, :n], k == 0, k == 7) for k in range(8)]
                mms(lst, [bW0g[col0 // 512], *bhT], [bpb_])
                return pb_, bpb_

            for c in range(4):
                pa, bpa = fm_group(c * 128, Nab)
                pbb, bpbb = fm_group(512 + c * 128, Nab)
                t0, bt0 = TMP[sl * 2]
                act(t0[:, :Nab], pbb[:, :Nab], AF.Tanh, [bpbb], [bt0], scale=0.5)
                stt(UTs[:, c, :Nab], t0[:, :Nab], 1.0, pa[:, :Nab], ALU.add, ALU.mult, [bt0, bpa], [bUTs])
                yield
            for c in range(4):
                pg_, bpg_ = fm_group(1024 + c * 128, Nga)
                t0, bt0 = TMP[sl * 2 + 1]
                act(t0[:, :Nga], pg_[:, :Nga], AF.Tanh, [bpg_], [bt0], scale=0.5)
                stt(SGAs[:, c, :Nga], t0[:, :Nga], 1.0, pg_[:, :Nga], ALU.add, ALU.mult, [bt0, bpg_], [bSGAs])
                yield
            for jj, tg in enumerate(tiles):
                hs = hT[:, :, jj * 128:(jj + 1) * 128]

                def tm_group(col0):
                    pb_, bpb_ = bankf()
                    lst = [(pb_[:, :], hs[:, k, :], W0[:, k, col0:col0 + 512], k == 0, k == 7) for k in range(8)]
                    mms(lst, [bW0g[col0 // 512], *bhT], [bpb_])
                    return pb_, bpb_

                pv, bpv = tm_group(2560)
                cpy(vst[:, jj, :, 0:64], pv[:, :].rearrange("p (h d) -> p h d", h=8), [bpv], [bvst])
                if ci == 0:
                    v32, bv32 = V32[sl]
                    cpy(v32, pv[:, :], [bpv], [bv32])
                    dma("pool", vo[tg * 128:(tg + 1) * 128, :], v32, [bv32], [])
                yield
                if jj >= ngb:
                    continue
                pgb, bpgb = tm_group(3072)
                t0, bt0 = TMP[sl * 2 + 1]
                act(t0, pgb[:, :], AF.Tanh, [bpgb], [bt0], scale=0.5)
                stt(sgbs[:, jj, :], t0, 1.0, pgb[:, :], ALU.add, ALU.mult, [bt0, bpgb], [bsgbs])
                yield
            def prompt_k(jj):
                tg = tiles[jj]
                hs = hT[:, :, jj * 128:(jj + 1) * 128]
                pk, bpk = bankf()
                lst = [(pk[:, :], hs[:, k, :], W0[:, k, 2048:2560], k == 0, k == 7) for k in range(8)]
                mms(lst, [bW0g[4], *bhT], [bpk])
                k32, bk32 = K32[sl]
                t1, bt1 = TMP[sl * 2 + 1]
                act(t1, pk[:, :], AF.Square, [bpk], [bt1])
                kss, bs = stat()
                S.add("dve", lambda e, t1=t1, kss=kss: e.tensor_reduce(
                    out=kss, in_=t1.rearrange("p (h d) -> p h d", h=8), axis=mybir.AxisListType.X, op=ALU.add),
                    [bt1], [bs])
                act(kss, kss, AF.Ln, [bs, bPRM], [bs], bias=PRM[:, 1:2])
                act(kss, kss, AF.Exp, [bs], [bs], scale=-0.5)
                tt(k32.rearrange("p (h d) -> p h d", h=8), pk[:, :].rearrange("p (h d) -> p h d", h=8),
                   kss.unsqueeze(2).to_broadcast([128, 8, 64]), ALU.mult, [bpk, bs], [bk32])
                tt(k32.rearrange("p (h d) -> p h d", h=8), k32.rearrange("p (h d) -> p h d", h=8),
                   PRM[:, 16:80].unsqueeze(1).to_broadcast([128, 8, 64]), ALU.mult, [bk32, bPRM], [bk32])
                dma("pool", ko[tg * 128:(tg + 1) * 128, :], k32, [bk32], [])

            for (col, dst, bdst, scl, nn) in ((1536, QTs, bQTs, PRM[:, 2:3], Nq), (2048, KTs, bKTs, PRM[:, 3:4], N)):
                for c in range(4):
                    pq, bpq = fm_group(col + c * 128, nn)
                    sq, bsq = SQ[sl]
                    act(sq[:, :nn], pq[:, :nn], AF.Square, [bpq], [bsq])
                    yield
                    p2, bp2 = bankf()
                    mms([(p2[:, :nn], BDB[:], sq[:, :nn], True, True)], [bBDB, bsq], [bp2])
                    t0, bt0 = TMP[sl * 2]
                    act(t0[:, :nn], p2[:, :nn], AF.Ln, [bp2, bPRM], [bt0], bias=PRM[:, 1:2])
                    act(t0[:, :nn], t0[:, :nn], AF.Exp, [bt0], [bt0], scale=-0.5)
                    stt(dst[:, c, :nn], pq[:, :nn], scl, t0[:, :nn], ALU.mult, ALU.mult, [bpq, bt0, bPRM], [bdst])
                    if ci == 0 and col == 1536 and c < nt:
                        prompt_k(c)
                    yield
            if ci == 0:
                t0c = tiles[0] * 128
                for bb in range(2):
                    b = tiles[0] // 2 + bb
                    dma("pool", sUv[:, :, b * 288 + 16:b * 288 + 272], UTs[:, :, bb * 256:(bb + 1) * 256], [bUTs], [bU])
            else:
                t0c = 1024 + (tiles[0] - 8) * 128
                u0 = UB + 16 + (tiles[0] - 8) * 128
                dma("pool", sUv[:, :, u0:u0 + Nab], UTs[:, :, :Nab], [bUTs], [bU])
            dma("pool", sGA.rearrange("c p t -> p c t")[:, :, t0c:t0c + Nga], SGAs[:, :, :Nga], [bSGAs], [bGA])
            dma("pool", sQ.rearrange("c p t -> p c t")[:, :, t0c:t0c + Nq], QTs[:, :, :Nq], [bQTs], [bQ])
            dma("pool", sK.rearrange("c p t -> p c t")[:, :, t0c:t0c + N], KTs[:, :, :N], [bKTs], [bK])
            g0 = tiles[0]
            dma("pool", sV.rearrange("j p f -> p j f")[:, g0:g0 + nt, :],
                vst[:, 0:nt, :, :].rearrange("p j h d -> p j (h d)"), [bvst], [bV])
            dma("pool", sGB.rearrange("j p f -> p j f")[:, g0:g0 + ngb, :], sgbs[:, 0:ngb, :], [bsgbs], [bGB])
            yield

        run_streams((stA(sti, ci, tiles) for sti, (ci, tiles) in enumerate(supertiles)), 2, stagger=0,
                    bgen=modgen(WFq, BGq))
        psf_n[0] = 4

        S.barrier()
        PHASES_DONE = build_program.phases

        if PHASES_DONE >= 2:
            cz.reset()
            cb.reset()
            cf.reset()
            WO0, bWO0 = cz.get("WO0", [8, 1024])
            ET, bET = cz.get("ET", [8, 13, 128])
            CK, bCK = cz.get("CK", [4, 512])
            CV, bCV = cz.get("CV", [4, 8, 65])
            DG, bDG = cz.get("DG", [124, 128])
            dma("pool", WO0, w_out_0.rearrange("(k p) n -> p k n", p=128), writes=[bWO0])
            dma("pool", CK, ckT.rearrange("(c p) k -> p c k", p=128), writes=[bCK])
            S.add("dve", lambda e: e.memset(CV[:, :, :, 64:65], 1.0), writes=[bCV])
            for j in range(4):
                dma("pool", CV[:, j, :, 0:64], cvv[j * 128:(j + 1) * 128, :].rearrange("p (h d) -> p h d", h=8), writes=[bCV])
            CWH, bCWH = CWHT[:], Buf("CWH")
            tsc(CWH, SM[:, CW:CW + 124], 0.5, None, ALU.mult, None, [bSM], [bCWH])
            bDGc = [Buf(f"DG{c}") for c in range(4)]
            for c in range(4):
                tt(DG[:, c * 31:(c + 1) * 31, :], IDB[:].unsqueeze(1).to_broadcast([128, 31, 128]),
                   CWH[:, c * 31:(c + 1) * 31].unsqueeze(2).to_broadcast([128, 31, 128]),
                   ALU.mult, [bIDB, bCWH], [bDGc[c]], eng=DG_ENG)
            al = [Buf("CY"), Buf("CY2"), Buf("MEAN"), Buf("RSTD")]
            cf.reset()
            CY, _ = cf.get("CY", [4, 512])
            CY2, _ = cf.get("CY2", [4, 512])
            MEAN, _ = cf.get("MEAN", [512])
            RSTD, _ = cf.get("RSTD", [512])
            bCY, bCY2, bMEAN, bRSTD = al
            TN = [cf.get(f"TN{i}", [512]) for i in range(2)]
            XR = [cf.get(f"XR{i}", [1024]) for i in range(2)]
            Y0 = [cf.get(f"Y0{i}", [1024]) for i in range(2)]
            OT, bOT = cf.get("OT", [512])
            UP, bUP = cb.get("UP", [4, 576])
            SGA, bSGA = cb.get("SGA", [4, 512])
            ZT_ = [cb.get(f"ZT{i}", [8, 512]) for i in range(2)]
            QT, bQT = cb.get("QT", [4, 512])
            KW, bKW = cb.get("KW", [4, 1024])
            VW, bVW = cb.get("VW", [8, 8, 65])
            SGB, bSGB = cb.get("SGB", [4, 512])
            PT = [cb.get(f"PT{i}", [9, 128]) for i in range(3)]
            YA, bYA = cb.get("YA", [512])
            SN, bSN = cb.get("SN", [512])

            stB = [(0, [0, 1, 2, 3]), (0, [4, 5, 6, 7])] + [(1, list(range(a, min(a + 4, NTB)))) for a in range(0, NTB, 4)]
            psf_n[0] = 4
            rot[0] = [0, 1, 2, 6]
            psb_n[0] = 1
            pt_i = [0]
            xr_i = [0]

            def Xgen(ci, tiles, zi):
                ZT, bZT = ZT_[zi]
                nt = len(tiles)
                N = nt * 128
                if ci == 0:
                    tcol = tiles[0] * 128
                    seqs = [(0, 256, 0), (256, 256, 288)]
                    b0 = tiles[0] // 2
                    dma("sp", UP[:, :, 0:576], sUv[:, :, b0 * 288:b0 * 288 + 576], [bU], [bUP])
                else:
                    tcol = 1024 + tiles[0] * 128
                    seqs = [(0, N, 0)]
                    u0 = UB + tiles[0] * 128
                    dma("sp", UP[:, :, 0:N + 32], sUv[:, :, u0:u0 + N + 32], [bU], [bUP])
                dma("sp", SGA[:, :, :N], sGA.rearrange("c p t -> p c t")[:, :, tcol:tcol + N], [bGA], [bSGA])
                for c in range(4):
                    pc, bpc = PSF[3], bPSF[3]
                    lst = []
                    for (oc, n, uo) in seqs:
                        for k in range(31):
                            lst.append((pc[:, oc:oc + n], DG[:, c * 31 + k, :], UP[:, c, uo + k + 1:uo + k + 1 + n],
                                        k == 0, k == 30))
                    npc = 4
                    per = (len(lst) + npc - 1) // npc
                    for pi in range(npc):
                        mms(lst[pi * per:(pi + 1) * per], [bDGc[c], bUP], [bpc])
                        if pi == npc - 1:
                            act(CY[:, c, :N], pc[:, :N], AF.Identity, [bpc, bSM], [bCY], bias=SM[:, CB + c:CB + c + 1])
                            act(CY2[:, c, :N], pc[:, :N], AF.Square, [bpc, bSM], [bCY2], bias=SM[:, CB + c:CB + c + 1])
                        yield
                pm, bpm = bankf()
                mms([(pm[:, :N], ON32[:], CY[:, c, :N], c == 0, c == 3) for c in range(4)], [bON, bCY], [bpm])
                cpy(MEAN[:, :N], pm[:, :N], [bpm], [bMEAN])
                yield
                pq2, bpq2 = bankf()
                mms([(pq2[:, :N], ON32[:], CY2[:, c, :N], c == 0, c == 3) for c in range(4)], [bON, bCY2], [bpq2])
                tt(RSTD[:, :N], MEAN[:, :N], MEAN[:, :N], ALU.mult, [bMEAN], [bRSTD])
                tt(RSTD[:, :N], pq2[:, :N], RSTD[:, :N], ALU.subtract, [bpq2, bRSTD], [bRSTD])
                act(RSTD[:, :N], RSTD[:, :N], AF.Ln, [bRSTD, bPRM], [bRSTD], bias=PRM[:, 0:1])
                act(RSTD[:, :N], RSTD[:, :N], AF.Exp, [bRSTD], [bRSTD], scale=-0.5)
                yield
                for c in range(4):
                    tn, btn = TN[c % 2]
                    tt(tn[:, :N], CY[:, c, :N], MEAN[:, :N], ALU.subtract, [bCY, bMEAN], [btn])
                    yield
                    tt(tn[:, :N], tn[:, :N], RSTD[:, :N], ALU.mult, [btn, bRSTD], [btn])
                    yield
                    tsc(tn[:, :N], tn[:, :N], SM[:, LG + c:LG + c + 1], SM[:, LB + c:LB + c + 1], ALU.mult, ALU.add,
                        [btn, bSM], [btn])
                    act(SN[:, :N], tn[:, :N], AF.Tanh, [btn], [bSN], scale=0.5)
                    yield
                    stt(tn[:, :N], SN[:, :N], 1.0, tn[:, :N], ALU.add, ALU.mult, [bSN, btn], [btn])
                    yield
                    stt(ZT[:, c, :N], tn[:, :N], 0.25, SGA[:, c, :N], ALU.mult, ALU.mult, [btn, bSGA], [bZT])
                    yield

            def Yparams(ci, tiles):
                nt = len(tiles)
                if ci == 0:
                    tcol = tiles[0] * 128
                    gt0 = tiles[0]
                    kt0, nkt = tiles[0], 4
                else:
                    tcol = 1024 + tiles[0] * 128
                    gt0 = 8 + tiles[0]
                    kt0 = max(tiles[0] - 2, 0)
                    nkt = min(max(tiles[-1] + 2, 3) + 1, NTS) - kt0
                return nt, nt * 128, tcol, gt0, kt0, nkt

            def Yloads(ci, tiles):
                nt, N, tcol, gt0, kt0, nkt = Yparams(ci, tiles)
                dma("sp", QT[:, :, :N], sQ.rearrange("c p t -> p c t")[:, :, tcol:tcol + N], [bQ], [bQT])
                kcol = (kt0 * 128) if ci == 0 else (1024 + kt0 * 128)
                dma("sp", KW[:, :, :nkt * 128], sK.rearrange("c p t -> p c t")[:, :, kcol:kcol + nkt * 128], [bK], [bKW])
                gk0 = kt0 if ci == 0 else 8 + kt0
                dma("sp", VW[:, 0:nkt, :, :].rearrange("p j h d -> p j (h d)"),
                    sV.rearrange("j p f -> p j f")[:, gk0:gk0 + nkt, :], [bV], [bVW])
                dma("sp", SGB[:, 0:nt, :], sGB.rearrange("j p f -> p j f")[:, gt0:gt0 + nt, :], [bGB], [bSGB])

            def Ygen(ci, tiles, zi, nxt=None):
                ZT, bZT = ZT_[zi]
                nt, N, tcol, gt0, kt0, nkt = Yparams(ci, tiles)
                tinfo = {}
                for jj, t in enumerate(tiles):
                    if ci == 0:
                        bt = (t // 2) * 2
                        wt = [bt - tiles[0], bt - tiles[0] + 1]
                        pat0 = None
                        nctx = 0
                    else:
                        kts = key_tiles(t)
                        wt = [j - kt0 for j in kts]
                        pat0 = pat_index(t, kts[0])
                        nctx = 4
                    tinfo[jj] = (wt, pat0, nctx)
                ob = [(PSF[4], bPSF[4]), (PSF[5], bPSF[5])]
                pts = {}

                def stage1(jj, h):
                    wt, pat0, nctx = tinfo[jj]
                    nw = len(wt)
                    c, hh = h // 2, h % 2
                    pr = slice(64 * hh, 64 * hh + 64)
                    pt, bpt = PT[pt_i[0] % 3]
                    pt_i[0] += 1
                    pts[(jj, h)] = (pt, bpt)
                    qs = QT[pr, c, jj * 128:(jj + 1) * 128]
                    groups = [wt[0:4]] + ([wt[4:]] if nw > 4 else [])
                    so = 0
                    for grp in groups:
                        ps_, bps_ = bankf()
                        lst = [(ps_[:, i * 128:(i + 1) * 128], KW[pr, c, s_ * 128:(s_ + 1) * 128], qs, True, True)
                               for i, s_ in enumerate(grp)]
                        mms(lst, [bKW, bQT], [bps_])
                        act(pt[:, so:so + len(grp), :].rearrange("p a b -> p (a b)"), ps_[:, :len(grp) * 128],
                            AF.Exp, [bps_], [bpt])
                        so += len(grp)
                    if pat0 is not None:
                        tt(pt[:, 0:nw, :], pt[:, 0:nw, :], ET[:, h, pat0:pat0 + nw, :], ALU.mult, [bpt, bETh[h]], [bpt])
                    if nctx:
                        ps_, bps_ = bankf()
                        lst = [(ps_[:, i * 128:(i + 1) * 128], CK[pr, c, i * 128:(i + 1) * 128], qs, True, True)
                               for i in range(4)]
                        mms(lst, [bCK, bQT], [bps_])
                        act(pt[:, nw:nw + 4, :].rearrange("p a b -> p (a b)"), ps_[:, :], AF.Exp, [bps_], [bpt])

                def stage2(jj, h):
                    wt, pat0, nctx = tinfo[jj]
                    nw = len(wt)
                    tot = nw + nctx
                    pt, bpt = pts[(jj, h)]
                    po, bpo = ob[h // 4]
                    oo = (h % 4) * 65
                    lst = []
                    for i, s_ in enumerate(wt):
                        lst.append((po[:, oo:oo + 65], pt[:, i, :], VW[:, s_, h, :], i == 0, i == tot - 1))
                    for i in range(nctx):
                        lst.append((po[:, oo:oo + 65], pt[:, nw + i, :], CV[:, i, h, :], False, nw + i == tot - 1))
                    mms(lst, [bpt, bVW, bCV], [bpo])

                def tail1(jj):
                    for hb in range(2):
                        po, bpo = ob[hb]
                        pov = po[:, 0:260].rearrange("p (h d) -> p h d", h=4)
                        st_, bs = stat()
                        rd = st_[:, 0:4]
                        S.add("dve", lambda e, rd=rd, pov=pov: e.reciprocal(out=rd, in_=pov[:, :, 64]), [bpo], [bs])
                        otv = OT[:, hb * 256:(hb + 1) * 256].rearrange("p (h d) -> p h d", h=4)
                        stt(otv, pov[:, :, 0:64], 0.5, rd.unsqueeze(2).to_broadcast([128, 4, 64]), ALU.mult, ALU.mult,
                            [bpo, bs], [bOT])
                    tt(YA, OT, SGB[:, jj, :], ALU.mult, [bOT, bSGB], [bYA])

                def tail2(jj):
                    pb, bpb = bankb()
                    transposes([(pb[:, k * 128:(k + 1) * 128], YA[:, k * 128:(k + 1) * 128]) for k in range(4)],
                               [bYA], [bpb])
                    cpy(ZT[:, 4:8, jj * 128:(jj + 1) * 128], pb[:, 0:512].rearrange("p (a b) -> p a b", a=4),
                        [bpb], [bZT])

                items = [(jj, h) for jj in range(nt) for h in range(8)]
                stage1(*items[0])
                stage1(*items[1])
                pend = None
                for i_, (jj, h) in enumerate(items):
                    if i_ + 2 < len(items):
                        stage1(*items[i_ + 2])
                    stage2(jj, h)
                    if pend is not None and h == 1:
                        tail2(pend)
                        pend = None
                    if h == 7:
                        tail1(jj)
                        pend = jj
                    yield
                if pend is not None:
                    tail2(pend)
                if nxt is not None:
                    Yloads(*nxt)
                yield
                for jj, t in enumerate(tiles):
                    xr, bxr = XR[xr_i[0] % 2]
                    y0, by0 = Y0[xr_i[0] % 2]
                    xr_i[0] += 1
                    src = xp[t * 128:(t + 1) * 128, :] if ci == 0 else xs[t * 128:(t + 1) * 128, :]
                    dma("sp", xr, src, writes=[bxr])
                    for n in range(2):
                        po, bpo = bankf()
                        lst = [(po[:, :], ZT[:, k, jj * 128:(jj + 1) * 128], WO0[:, k, n * 512:(n + 1) * 512], k == 0, k == 7)
                               for k in range(8)]
                        mms(lst, [bZT, bWO0], [bpo])
                        tt(y0[:, n * 512:(n + 1) * 512], po[:, :], G[:, ci, n * 512:(n + 1) * 512], ALU.mult, [bpo, bGl[0]], [by0])
                        yield
                    tt(y0, y0, xr, ALU.add, [by0, bxr], [by0])
                    gt = t if ci == 0 else 8 + t
                    dma("pool", sY0[gt], y0, [by0], [bY0])
                    st_, bs = stat()
                    act(xr, y0, AF.Square, [by0], [bxr, bs], accum=st_[:, 0:1])
                    act(RS1[:, gt:gt + 1], st_[:, 0:1], AF.Ln, [bs, bPRM], [bRS1[gt]], bias=PRM[:, 0:1], scale=1.0 / 1024)
                    act(RS1[:, gt:gt + 1], RS1[:, gt:gt + 1], AF.Exp, [bRS1[gt]], [bRS1[gt]], scale=-0.5)

            def zip2(g1, g2):
                gs = [g for g in (g1, g2) if g is not None]
                while gs:
                    for g in list(gs):
                        try:
                            next(g)
                        except StopIteration:
                            gs.remove(g)

            bETh = [Buf(f"ET{h}") for h in range(8)]

            def build_ET():
                for h in range(8):
                    dma("pool", ET[:, h, :, :].rearrange("p a b -> p (a b)"), ebias[:, h * 1664:(h + 1) * 1664],
                        writes=[bETh[h]])
                yield
                for h in range(8):
                    eh = ET[:, h, :, :].rearrange("p a b -> p (a b)")
                    act(eh, eh, AF.Exp, [bETh[h]], [bETh[h]])
                    yield

            def wzip(gws):
                alive = [True] * len(gws)
                while alive[0]:
                    for gi, (g, reps) in enumerate(gws):
                        for _ in range(reps):
                            if alive[gi]:
                                try:
                                    next(g)
                                except StopIteration:
                                    alive[gi] = False
                for gi, (g, reps) in enumerate(gws):
                    if alive[gi]:
                        for _ in g:
                            pass

            W1OFF = 25632
            W1 = ZB[:, W1OFF:W1OFF + 16384].rearrange("p (k n) -> p k n", k=8)
            bW1 = Buf("W1")
            w1v = w_in_1.rearrange("(k p) n -> p k n", p=128)

            def Xchain():
                for si, (ci, tiles) in enumerate(stB):
                    yield from Xgen(ci, tiles, si % 2)
                for k in range(0, 8, 2):
                    dma("pool", W1[:, k:k + 2, :], w1v[:, k:k + 2, :], writes=[bW1] + bDGc)
                yield

            def Ychain():
                for si, (ci, tiles) in enumerate(stB):
                    yield from Ygen(ci, tiles, si % 2, nxt=(stB[si + 1] if si + 1 < len(stB) else None))

            xc, yc, ec = Xchain(), Ychain(), build_ET()
            alive = {"x": True, "y": True, "e": True}

            def stepg(name, g):
                if alive[name]:
                    try:
                        next(g)
                    except StopIteration:
                        alive[name] = False

            for i_ in range(16):
                stepg("x", xc)
                if i_ == 0:
                    Yloads(*stB[0])
                if i_ % 4 == 3:
                    stepg("e", ec)
            while alive["x"] or alive["y"] or alive["e"]:
                stepg("y", yc)
                stepg("x", xc)
                stepg("e", ec)
            psf_n[0] = 4
            rot[0] = [0, 1, 2, 3, 4, 5]
            psb_n[0] = 2
            S.barrier()

        if PHASES_DONE >= 3:
            cz.reset()
            cb.reset()
            cf.reset()
            PW, bPW = cz.get("PW", [8, 256])
            WO1, bWO1 = cz.get("WO1", [8, 1024])
            BND, bBND = cz.get("BND", [32, 128])
            assert cz.off <= W1OFF
            dma("pool", PW, pool_w.rearrange("g (kc p) e -> p (g kc) e", p=128), writes=[bPW])
            dma("pool", WO1, w_out_1.rearrange("(k p) n -> p k n", p=128), writes=[bWO1])
            dma("pool", BND, band.rearrange("p (a b) -> p a b", a=32), writes=[bBND])
            JUNK, bJUNK = cb.get("junk", [1024])
            NSTR = 2
            NYR = 4
            YR = [[cf.get(f"YR{q}_{i}", [1024]) for i in range(NYR)] for q in range(NSTR)]
            TG = [[cb.get(f"TG{q}_{i}", [512]) for i in range(2)] for q in range(NSTR)]
            TO = [[cf.get(f"TO{q}_{i}", [512]) for i in range(2)] for q in range(NSTR)]
            XN = [cb.get(f"xn{q}", [1024]) for q in range(NSTR)]
            H1 = [cb.get(f"H1{q}", [8, 128]) for q in range(NSTR)]
            H1B = [(Buf("h1lo"), Buf("h1hi")) for q in range(NSTR)]
            U1 = [[cb.get(f"U1{q}_{i}", [1024]) for i in range(4)] for q in range(NSTR)]
            SG1 = [[cb.get(f"SG1{q}_{i}", [8, 128]) for i in range(3)] for q in range(NSTR)]
            DT = [cb.get(f"DT{q}", [8, 128]) for q in range(NSTR)]
            Z1 = [cb.get(f"Z1{q}", [8, 128]) for q in range(NSTR)]

            def zipn(gs):
                gs = [g for g in gs if g is not None]
                while gs:
                    for g in list(gs):
                        try:
                            next(g)
                        except StopIteration:
                            gs.remove(g)
                    yield

            def seq_stream(q, seqs):
                cnt = [0]
                for (ci, tl, nout) in seqs:
                    info = {}

                    def C1(t, ci=ci, info=info):
                        i = cnt[0]
                        cnt[0] += 1
                        info[t] = i
                        yr, byr = YR[q][i % NYR]
                        xn, bxn = XN[q]
                        h1, _ = H1[q]
                        bh1 = H1B[q]
                        u1, bu1 = U1[q][i % 4]
                        sg1, bsg1 = SG1[q][i % 3]
                        gt = t if ci == 0 else 8 + t
                        dma("sp", yr, sY0[gt], [bY0], [byr])
                        norm_to_hT(yr, byr, xn, bxn, JUNK, Buf("j"), h1, bh1, 1, ci, evac_act=True,
                                   rs_pre=(RS1[:, gt:gt + 1], bRS1[gt]))
                        yield
                        for n in range(2):
                            po, bpo = bankf()
                            lst = [(po[:, :], h1[:, k, :], W1[:, k, n * 512:(n + 1) * 512], k == 0, k == 7) for k in range(8)]
                            mms(lst, [*bh1, bW1], [bpo])
                            act(u1[:, n * 512:(n + 1) * 512], po[:, :], AF.Copy, [bpo], [bu1])
                            yield
                        for hf in range(2):
                            po, bpo = bankf()
                            lst = []
                            for bq in range(4):
                                col = 1024 + (hf * 4 + bq) * 128
                                for k in range(8):
                                    lst.append((po[:, bq * 128:(bq + 1) * 128], W1[:, k, col:col + 128], h1[:, k, :], k == 0, k == 7))
                            mms(lst, [*bh1, bW1], [bpo])
                            tg, btg = TG[q][hf]
                            act(tg, po[:, :], AF.Tanh, [bpo], [btg], scale=0.5)
                            stt(sg1[:, hf * 4:(hf + 1) * 4, :].rearrange("p a b -> p (a b)"), tg, 1.0, po[:, :], ALU.add, ALU.mult,
                                [btg, bpo], [bsg1])
                            yield

                    def C2(t, first, last_true, ci=ci, info=info):
                        i = info[t]
                        yr, byr = YR[q][i % NYR]
                        sg1, bsg1 = SG1[q][i % 3]
                        dt, bdt = DT[q]
                        z1, bz1 = Z1[q]
                        nbs = []
                        base = 0 if ci == 0 else 16
                        if not first:
                            nbs.append(((i - 1) % 4, base + 0))
                        nbs.append((i % 4, base + (4 if first else 8)))
                        if not last_true:
                            nbs.append(((i + 1) % 4, base + 12))
                        for hf in range(2):
                            po, bpo = bankf()
                            lst = []
                            for bq in range(4):
                                cc = hf * 4 + bq
                                g = cc // 2
                                for j_, (sl_, bi) in enumerate(nbs):
                                    lst.append((po[:, bq * 128:(bq + 1) * 128], U1[q][sl_][0][:, cc * 128:(cc + 1) * 128],
                                                BND[:, bi + g, :], j_ == 0, j_ == len(nbs) - 1))
                            rd = [U1[q][sl_][1] for (sl_, bi) in nbs]
                            mms(lst, rd + [bBND], [bpo])
                            act(dt[:, hf * 4:(hf + 1) * 4, :].rearrange("p a b -> p (a b)"), po[:, :], AF.Copy, [bpo], [bdt])
                            yield
                        for hf in range(2):
                            po, bpo = bankf()
                            lst = []
                            for bq in range(4):
                                ob_ = hf * 4 + bq
                                g, eb = ob_ // 2, ob_ % 2
                                for kc in range(2):
                                    lst.append((po[:, bq * 128:(bq + 1) * 128], PW[:, g * 2 + kc, eb * 128:(eb + 1) * 128],
                                                dt[:, g * 2 + kc, :], kc == 0, kc == 1))
                            mms(lst, [bPW, bdt], [bpo])
                            for bq in range(4):
                                ob_ = hf * 4 + bq
                                stt(z1[:, ob_, :], po[:, bq * 128:(bq + 1) * 128], PRM[:, 8 + ob_:9 + ob_], sg1[:, ob_, :],
                                    ALU.mult, ALU.mult, [bpo, bPRM, bsg1], [bz1])
                            yield
                        for n in range(2):
                            to, bto = TO[q][n]
                            po, bpo = bankf()
                            lst = [(po[:, :], z1[:, k, :], WO1[:, k, n * 512:(n + 1) * 512], k == 0, k == 7) for k in range(8)]
                            mms(lst, [bz1, bWO1], [bpo])
                            tt(to, po[:, :], G[:, 2 + ci, n * 512:(n + 1) * 512], ALU.mult, [bpo, bGl[1]], [bto])
                            tt(yr[:, n * 512:(n + 1) * 512], to, yr[:, n * 512:(n + 1) * 512], ALU.add, [bto, byr], [byr],
                               eng="pool")
                            yield
                        dst = yp[t * 128:(t + 1) * 128, :] if ci == 0 else ys[t * 128:(t + 1) * 128, :]
                        dma("pool", dst, yr, [byr], [])

                    yield from zipn([C1(tl[0])])
                    if len(tl) > 1:
                        yield from zipn([C1(tl[1])])
                    for i_, t in enumerate(tl):
                        g1 = C1(tl[i_ + 2]) if i_ + 2 < len(tl) else None
                        g2 = C2(t, i_ == 0, (ci == 0 and i_ == len(tl) - 1)) if i_ < nout else None
                        yield from zipn([g1, g2])

            seqP = [(0, [2 * b, 2 * b + 1], 2) for b in range(4)]
            seqS = [(1, list(range(NTB)), 16)]
            gS, gP = seq_stream(0, seqS), seq_stream(1, seqP)
            alive = [True, True]
            while any(alive):
                for gi, (g, reps) in enumerate(((gS, 1), (gP, 1))):
                    for _ in range(reps):
                        if alive[gi]:
                            try:
                                next(g)
                            except StopIteration:
                                alive[gi] = False

        S.finish()
        S.emit(block, sems, dsems)
    return nc


build_program.phases = 3
build_program.nst = 99
DG_ENG = "dve"


def _att_tables(rpb, flip):
    pats = [(10, 10 + d) for d in (-2, -1, 0, 1, 2)] + [(0, j) for j in range(4)] + [(1, j) for j in range(4)]
    idx = np.arange(128)
    bias = np.zeros((8, 13, 128, 128), np.float32)
    mask = np.zeros((13, 128, 128), np.float32)
    for pi, (t, j) in enumerate(pats):
        ik = 2 * j + idx // 64
        ckl = idx % 64
        iq = 2 * t + idx // 64
        cql = idx % 64
        if flip:
            rk, ck_, rq, cq = 63 - ik, 63 - ckl, 63 - iq, 63 - cql
        else:
            rk, ck_, rq, cq = ik, ckl, iq, cql
        RK, RQ = rk[:, None], rq[None, :]
        CK_, CQ = ck_[:, None], cq[None, :]
        start = np.clip(RQ - 4, 0, 56)
        rowok = (RK >= start) & (RK < start + 8)
        qcs = np.clip(CQ - 8, 0, 48)
        colok = (CK_ >= qcs) & (CK_ < qcs + 16)
        dr = np.clip(RK - RQ + 7, 0, 14)
        dc = np.clip(CK_ - CQ + 15, 0, 30)
        mask[pi] = (rowok & colok).astype(np.float32)
        dr = np.broadcast_to(dr, (128, 128))
        dc = np.broadcast_to(dc, (128, 128))
        bias[:, pi] = rpb[:, dr, dc]
    bias = np.where(mask[None] > 0, bias, np.float32(-30000.0)).astype(np.float32)
    eb = np.ascontiguousarray(bias.transpose(2, 0, 1, 3)).reshape(128, 8 * 13 * 128)
    em = np.ascontiguousarray(mask.transpose(1, 0, 2)).reshape(128, 13 * 128)
    return eb, em


def _band_tables2(flip):
    out = np.zeros((32, 128, 128), np.float32)
    wins = (2, 4, 8, 16)
    for sset in range(2):
        fl = flip
        for g, w in enumerate(wins):
            lo, hi = -(w // 2), w - w // 2
            if fl:
                lo, hi = -(w - w // 2) + 1, w // 2 + 1
            for role in range(4):
                M = np.zeros((128, 128), np.float32)
                for t in range(128):
                    a, b = t + lo, t + hi
                    ca, cbb = a, b
                    if role == 1:
                        ca = max(a, 0)
                    if role == 2 and sset == 0:
                        cbb = min(b, 128)
                    if role in (0, 3):
                        cnt = w
                    else:
                        cnt = cbb - ca
                    for s in range(ca, cbb):
                        if role == 0 and s < 0:
                            M[s + 128, t] += 1.0 / cnt
                        elif role == 3 and s >= 128:
                            M[s - 128, t] += 1.0 / cnt
                        elif role in (1, 2) and 0 <= s < 128:
                            M[s, t] += 1.0 / cnt
                    if role in (1, 2):
                        M[t, t] -= 1.0
                out[sset * 16 + role * 4 + g] = M
    return np.ascontiguousarray(out.transpose(1, 0, 2)).reshape(128, 32 * 128)


_CACHE = {}


def kernel(x_prompt, x_sample, cache_k_0, cache_v_0, c, c_ctx,
           norm_g_0, w_ada_0, b_ada_0, w_in_0, conv_w_0, conv_b_0, conv_ln_g_0, conv_ln_b_0,
           q_norm_0, k_norm_0, rpb_0, w_out_0,
           norm_g_1, w_ada_1, b_ada_1, w_in_1, pool_w_1, pool_scale_1, w_out_1):
    f = lambda a: np.ascontiguousarray(np.asarray(a, dtype=np.float32))
    x_prompt, x_sample, cache_k_0, cache_v_0, c, c_ctx = map(f, (x_prompt, x_sample, cache_k_0, cache_v_0, c, c_ctx))
    if "nc" not in _CACHE:
        _CACHE["nc"] = build_program()
    nc = _CACHE["nc"]
    fm = lambda v, nb: f(v).reshape(nb, 128).T
    shared = dict(w_ada_0=f(w_ada_0), w_ada_1=f(w_ada_1), w_in_0=f(w_in_0), w_out_0=f(w_out_0), w_in_1=f(w_in_1),
                  pool_w=f(pool_w_1), w_out_1=f(w_out_1), ident=np.eye(128, dtype=np.float32),
                  bdiag=np.kron(np.eye(2, dtype=np.float32), np.ones((64, 64), np.float32)),
                  ones32=np.full((128, 128), 1.0 / 512, np.float32))
    bgr = np.concatenate([np.broadcast_to(f(b_ada_0)[2048:], (128, 1024)),
                          np.broadcast_to(f(b_ada_1)[2048:], (128, 1024))], axis=1)
    shared["bg"] = np.ascontiguousarray(bgr)
    tabs = {fl: (_att_tables(f(rpb_0), fl), _band_tables2(fl)) for fl in (False, True)}
    in_maps = []
    for i in range(8):
        b, half = i // 2, i % 2
        flip = half == 1
        sm = np.zeros((128, NSM), np.float32)
        sm[:, NG0:NG0 + 8] = fm(norm_g_0, 8)
        sm[:, NG1:NG1 + 8] = fm(norm_g_1, 8)
        sm[:, BA0:BA0 + 24] = fm(b_ada_0, 24)
        sm[:, BA1:BA1 + 24] = fm(b_ada_1, 24)
        cw = f(conv_w_0)[::-1] if flip else f(conv_w_0)
        sm[:, CW:CW + 124] = cw.T.reshape(4, 128, 31).transpose(1, 0, 2).reshape(128, 124)
        sm[:, CB:CB + 4] = fm(conv_b_0, 4)
        sm[:, LG:LG + 4] = fm(conv_ln_g_0, 4)
        sm[:, LB:LB + 4] = fm(conv_ln_b_0, 4)
        sm[:, QN] = np.tile(f(q_norm_0), 2)
        sm[:, KN] = np.tile(f(k_norm_0), 2)
        sm[:, PSC:PSC + 8] = fm(pool_scale_1, 8)
        cv2 = np.stack([fm(c_ctx, 8), fm(c[b], 8)], axis=2)
        sm[:, CVEC:CVEC + 16] = cv2.reshape(128, 16)
        sm[:, KNR:KNR + 64] = np.broadcast_to(f(k_norm_0), (128, 64))
        xsb = x_sample[b][::-1] if flip else x_sample[b]
        (eb, em), bnd = tabs[flip]
        m = dict(shared)
        xpb = x_prompt[4 * i:4 * i + 4][:, ::-1] if flip else x_prompt[4 * i:4 * i + 4]
        m.update(xp=np.ascontiguousarray(xpb).reshape(1024, 1024),
                 xs=np.ascontiguousarray(xsb[:TS]),
                 ckT=np.ascontiguousarray(cache_k_0[b].reshape(512, 512).T),
                 cvv=np.ascontiguousarray(cache_v_0[b].reshape(512, 512)),
                 smallp=sm, ebias=eb, emask=em, band=bnd)
        in_maps.append(m)
    if _CACHE.get("in_maps_only"):
        return in_maps
    res = run_bass_kernel_spmd(nc, in_maps, core_ids=list(range(8)))
    y_prompt = np.zeros((32, 256, 1024), np.float32)
    y_sample = np.zeros((4, 4096, 1024), np.float32)
    k_ctx = np.zeros((32, 256, 8, 64), np.float32)
    v_ctx = np.zeros((32, 256, 8, 64), np.float32)
    for i in range(8):
        r = res.results[i]
        b, half = i // 2, i % 2
        sl = slice(None, None, -1) if half == 1 else slice(None)
        y_prompt[4 * i:4 * i + 4] = r["yp"].reshape(4, 256, 1024)[:, sl]
        k_ctx[4 * i:4 * i + 4] = r["ko"].reshape(4, 256, 8, 64)[:, sl]
        v_ctx[4 * i:4 * i + 4] = r["vo"].reshape(4, 256, 8, 64)[:, sl]
        if half == 0:
            y_sample[b, :2048] = r["ys"]
        else:
            y_sample[b, 2048:] = r["ys"][::-1]
    return (y_prompt, y_sample, k_ctx, v_ctx)
```

```python
import numpy as np
from contextlib import ExitStack
import concourse.bass as bass
import concourse.mybir as mybir
from concourse.bass_utils import run_bass_kernel_spmd

F32 = mybir.dt.float32
BF16 = mybir.dt.bfloat16
ALU = mybir.AluOpType
AF = mybir.ActivationFunctionType
EPS = 1e-6
NTS = 19
NTB = 17
TS = NTS * 128
TTOT = 1024 + TS
UB = 4 * 288
ULEN = UB + 16 + TS + 16
NG0, NG1, BA0, BA1, CW, CB, LG, LB, QN, KN, PSC, CVEC, KNR, NSM = 0, 8, 16, 40, 64, 188, 192, 196, 200, 201, 202, 210, 226, 290


class Buf:
    __slots__ = ("name", "lw", "rs", "excl")

    def __init__(self, name, excl=False):
        self.name = name
        self.excl = excl
        self.lw = None
        self.rs = []


class Sched:
    ENG = ["pe", "act", "dve", "pool", "sp"]

    def __init__(self, ndma=24):
        self.ops = {e: [] for e in self.ENG}
        self.ndma = ndma
        self.dma_count = [0] * ndma
        self.dma_next = [0, 0]
        self.bar = {e: [] for e in self.ENG}

    def add(self, eng, fn, reads=(), writes=(), dma=False):
        deps = set(self.bar[eng])
        self.bar[eng] = []
        idx = len(self.ops[eng])
        if dma:
            half = self.ndma // 2
            qi = 0 if eng == "sp" else 1
            k = qi * half + self.dma_next[qi]
            self.dma_next[qi] = (self.dma_next[qi] + 1) % half
            if self.dma_count[k] > 0:
                deps.add(("dma", k, 16 * self.dma_count[k]))
            self.dma_count[k] += 1
            ref = ("dma", k, 16 * self.dma_count[k])
        else:
            ref = ("op", eng, idx)
        excl_reads = [b for b in reads if b.excl]
        if excl_reads:
            reads = [b for b in reads if not b.excl]
            writes = list(writes) + excl_reads
        for b in reads:
            if b.lw is not None:
                deps.add(b.lw)
        for b in writes:
            if b.lw is not None:
                deps.add(b.lw)
            deps.update(b.rs)
        for b in reads:
            b.rs.append(ref)
        for b in writes:
            b.lw = ref
            b.rs = []
        deps.discard(ref)
        if eng == "pe":
            deps = {d for d in deps if not (d[0] == "op" and d[1] == "pe")}
        self.ops[eng].append(dict(fn=fn, deps=deps, ref=ref, dma=dma))
        return ref

    def barrier(self, skip_sems=()):
        refs = []
        for e in self.ENG:
            for op in reversed(self.ops[e]):
                if op["fn"] is not None and not op["dma"]:
                    refs.append(op["ref"])
                    break
        for k in range(self.ndma):
            if self.dma_count[k] > 0 and k not in skip_sems:
                refs.append(("dma", k, 16 * self.dma_count[k]))
        for e in self.ENG:
            self.bar[e] = list(refs)

    def finish(self):
        self.barrier()
        for e in self.ENG:
            self.ops[e].append(dict(fn=None, deps=set(self.bar[e]), ref=None, dma=False))
            self.bar[e] = []

    def emit(self, block, sems, dma_sems):
        need = {e: set() for e in self.ENG}
        for e in self.ENG:
            for op in self.ops[e]:
                for d in op["deps"]:
                    if d[0] == "op":
                        need[d[1]].add(d[2])
        count = {}
        for e in self.ENG:
            count[e] = {}
            c = 0
            for idx in sorted(need[e]):
                c += 1
                count[e][idx] = c

        def run(e):
            def f(eng):
                waited = {}
                for idx, op in enumerate(self.ops[e]):
                    for d in sorted(op["deps"], key=str):
                        if d[0] == "op":
                            key = ("op", d[1])
                            val = count[d[1]][d[2]]
                            sem = sems[d[1]]
                        else:
                            key = ("dma", d[1])
                            val = d[2]
                            sem = dma_sems[d[1]]
                        if waited.get(key, 0) >= val:
                            continue
                        eng.wait_ge(sem, val)
                        waited[key] = val
                    if op["fn"] is None:
                        continue
                    ins = op["fn"](eng)
                    if op["dma"]:
                        ins.then_inc(dma_sems[op["ref"][1]], 16)
                    elif idx in count[e]:
                        ins.then_inc(sems[e], 1)
            return f

        block.tensor(run("pe"))
        block.scalar(run("act"))
        block.vector(run("dve"))
        block.gpsimd(run("pool"))
        block.sync(run("sp"))


def pat_index(t, j):
    if t >= 2:
        return j - t + 2
    return 5 + 4 * t + j


def key_tiles(t):
    return list(range(max(t - 2, 0), max(t + 2, 3) + 1))


def build_program():
    nc = bass.Bass("TRN2", target_bir_lowering=False)
    S = Sched()

    def din(name, shape):
        return nc.dram_tensor(name, list(shape), F32, kind="ExternalInput").ap()

    def dout(name, shape):
        return nc.dram_tensor(name, list(shape), F32, kind="ExternalOutput").ap()

    xp = din("xp", [1024, 1024])
    xs = din("xs", [TS, 1024])
    ckT = din("ckT", [512, 512])
    cvv = din("cvv", [512, 512])
    smallp = din("smallp", [128, NSM])
    bg = din("bg", [128, 2048])
    w_ada = [din("w_ada_0", [1024, 3072]), din("w_ada_1", [1024, 3072])]
    w_in_0 = din("w_in_0", [1024, 3584])
    w_out_0 = din("w_out_0", [1024, 1024])
    w_in_1 = din("w_in_1", [1024, 2048])
    pool_w = din("pool_w", [4, 256, 256])
    w_out_1 = din("w_out_1", [1024, 1024])
    ebias = din("ebias", [128, 8 * 13 * 128])
    emask = din("emask", [128, 13 * 128])
    band = din("band", [128, 32 * 128])
    ident = din("ident", [128, 128])
    bdiag = din("bdiag", [128, 128])
    ones32 = din("ones32", [128, 128])
    yp = dout("yp", [1024, 1024])
    ys = dout("ys", [2048, 1024])
    ko = dout("ko", [1024, 512])
    vo = dout("vo", [1024, 512])
    sU = nc.dram_tensor("sU", [4, 128, ULEN], BF16).ap()
    sGA = nc.dram_tensor("sGA", [4, 128, TTOT], BF16).ap()
    sQ = nc.dram_tensor("sQ", [4, 128, TTOT], BF16).ap()
    sK = nc.dram_tensor("sK", [4, 128, TTOT], BF16).ap()
    sV = nc.dram_tensor("sV", [27, 128, 520], BF16).ap()
    sGB = nc.dram_tensor("sGB", [27, 128, 512], BF16).ap()
    sY0 = nc.dram_tensor("sY0", [25, 128, 1024], F32).ap()
    bU, bGA, bQ, bK, bV, bGB, bY0 = (Buf(n) for n in ["sU", "sGA", "sQ", "sK", "sV", "sGB", "sY0"])

    with ExitStack() as es:
        ZB = es.enter_context(nc.sbuf_tensor("ZB", [128, 43008], BF16))
        WKB = es.enter_context(nc.sbuf_tensor("WKB", [128, 29440], BF16))
        WK = es.enter_context(nc.sbuf_tensor("WK", [128, 10752], F32))
        SM = es.enter_context(nc.sbuf_tensor("SM", [128, NSM], F32))
        G = es.enter_context(nc.sbuf_tensor("G", [128, 4, 1024], F32))
        MOD = es.enter_context(nc.sbuf_tensor("MOD", [128, 2, 32], F32))
        AM = es.enter_context(nc.sbuf_tensor("AM", [128, 4, 8], F32))
        IDB = es.enter_context(nc.sbuf_tensor("IDB", [128, 128], BF16))
        BDB = es.enter_context(nc.sbuf_tensor("BDB", [128, 128], BF16))
        ON32 = es.enter_context(nc.sbuf_tensor("ON32", [128, 128], F32))
        SC = es.enter_context(nc.sbuf_tensor("SC", [128, 16], F32))
        ST = es.enter_context(nc.sbuf_tensor("ST", [128, 64], F32))
        PRM = es.enter_context(nc.sbuf_tensor("PRM", [128, 80], F32))
        CWHT = es.enter_context(nc.sbuf_tensor("CWHT", [128, 124], F32))
        RS1 = es.enter_context(nc.sbuf_tensor("RS1", [128, 32], F32))
        bRS1 = [Buf(f"rs1_{i}") for i in range(32)]
        PSF = [es.enter_context(nc.psum_tensor(f"psf{i}", [128, 512], F32)) for i in range(7)]
        PSB0 = es.enter_context(nc.psum_tensor("psb0", [128, 1024], BF16))
        bPSF = [Buf(f"psf{i}", True) for i in range(7)]
        PSB = [PSB0, PSF[6][:, :].bitcast(BF16)]
        bPSB = [Buf("psb0", True), bPSF[6]]
        sems = {e: es.enter_context(nc.semaphore("s_" + e)) for e in S.ENG}
        dsems = [es.enter_context(nc.semaphore(f"d{i}")) for i in range(S.ndma)]
        block = es.enter_context(nc.Block())

        bSM, bG, bMOD, bAM, bIDB, bBDB, bON, bSC, bPRM = (Buf(n) for n in
                                                             ["SM", "G", "MOD", "AM", "IDB", "BDB", "ON", "SC", "PRM"])
        st_ctr = [0]
        stat_bufs = [Buf(f"stc{i}") for i in range(8)]
        TC = es.enter_context(nc.sbuf_tensor("TC", [128, 16], F32))

        def stat():
            s_ = st_ctr[0] % 8
            st_ctr[0] += 1
            return ST[:, 8 * s_:8 * s_ + 8], stat_bufs[s_]

        psf_i = [0]
        psb_i = [0]
        psf_n = [4]
        psb_n = [2]
        rot = [[0, 1, 2, 3, 4, 5]]

        def bankf():
            i = psf_i[0]
            i = i % psf_n[0]
            psf_i[0] = (i + 1) % psf_n[0]
            i = rot[0][i]
            return PSF[i], bPSF[i]

        def bankb():
            i = psb_i[0] % psb_n[0]
            psb_i[0] = (i + 1) % psb_n[0]
            return PSB[i], bPSB[i]

        class Carver:
            def __init__(self, t, size):
                self.t, self.size, self.off = t, size, 0

            def reset(self):
                self.off = 0

            def get(self, name, shape):
                n = int(np.prod(shape))
                assert self.off + n <= self.size, (name, self.off, n, self.size)
                v = self.t[:, self.off:self.off + n]
                self.off += n
                if len(shape) == 2:
                    v = v.rearrange("p (a b) -> p a b", a=shape[0])
                elif len(shape) == 3:
                    v = v.rearrange("p (a b c) -> p a b c", a=shape[0], b=shape[1])
                return v, Buf(name)

        cz, cb, cf = Carver(ZB, 43008), Carver(WKB, 29440), Carver(WK, 10752)

        def dma(q, out, in_, reads=(), writes=()):
            return S.add(q, lambda e: e.dma_start(out=out, in_=in_), reads, writes, dma=True)

        def act(out, in_, func, reads, writes, bias=None, scale=None, accum=None):
            def fn(e):
                kw = {}
                if bias is not None:
                    kw["bias"] = bias
                if scale is not None:
                    kw["scale"] = scale
                if accum is not None:
                    kw["accum_out"] = accum
                return e.activation(out=out, in_=in_, func=func, **kw)
            return S.add("act", fn, reads, writes)

        def tt(out, in0, in1, op, reads, writes, eng="dve"):
            return S.add(eng, lambda e: e.tensor_tensor(out=out, in0=in0, in1=in1, op=op), reads, writes)

        def tsc(out, in0, s1, s2, op0, op1, reads, writes, eng="dve"):
            if s2 is None:
                return S.add(eng, lambda e: e.tensor_scalar(out=out, in0=in0, scalar1=s1, scalar2=None, op0=op0),
                             reads, writes)
            return S.add(eng, lambda e: e.tensor_scalar(out=out, in0=in0, scalar1=s1, scalar2=s2, op0=op0, op1=op1),
                         reads, writes)

        def stt(out, in0, scalar, in1, op0, op1, reads, writes, eng="dve"):
            return S.add(eng, lambda e: e.scalar_tensor_tensor(out=out, in0=in0, scalar=scalar, in1=in1,
                                                               op0=op0, op1=op1), reads, writes)

        def cpy(out, in_, reads, writes, eng="dve"):
            return S.add(eng, lambda e: e.tensor_copy(out=out, in_=in_), reads, writes)

        def mms(lst, reads, writes):
            def fn(e):
                ins = None
                for (o, l, r, st, sp) in lst:
                    ins = e.matmul(o, lhsT=l, rhs=r, start=st, stop=sp)
                return ins
            return S.add("pe", fn, reads, writes)

        def transposes(lst, reads, writes):
            def fn(e):
                ins = None
                for (o, i) in lst:
                    ins = e.transpose(out=o, in_=i, identity=IDB[:])
                return ins
            return S.add("pe", fn, list(reads) + [bIDB], writes)

        def rsqrt_act(out, in_, reads, writes, bias_ap, scale=1.0):
            act(out, in_, AF.Ln, reads, writes, bias=bias_ap, scale=scale)
            act(out, out, AF.Exp, writes, writes, scale=-0.5)

        dma("sp", SM[:], smallp, writes=[bSM])
        dma("sp", ON32[:], ones32, writes=[bON])
        dma("pool", IDB[:], ident, writes=[bIDB])
        dma("pool", BDB[:], bdiag, writes=[bBDB])
        W0, bW0 = cz.get("W0", [8, 3584])
        w0v = w_in_0.rearrange("(k p) n -> p k n", p=128)
        bW0g = [Buf(f"W0g{g}") for g in range(7)]
        w0_sems = set()
        S.add("dve", lambda e: e.memset(PRM[:, 0:1], EPS), writes=[bPRM])
        S.add("dve", lambda e: e.memset(PRM[:, 1:2], 64 * EPS), writes=[bPRM])
        cpy(PRM[:, 2:3], SM[:, QN:QN + 1], [bSM], [bPRM])
        tsc(PRM[:, 3:4], SM[:, KN:KN + 1], 8.0, None, ALU.mult, None, [bSM], [bPRM])
        tsc(PRM[:, 8:16], SM[:, PSC:PSC + 8], 0.5, None, ALU.mult, None, [bSM], [bPRM])
        tsc(PRM[:, 16:80], SM[:, KNR:KNR + 64], 8.0, None, ALU.mult, None, [bSM], [bPRM])
        tanh_c, btc = TC[:], Buf('TC')
        act(tanh_c, SM[:, CVEC:CVEC + 16], AF.Tanh, [bSM], [btc], scale=0.5)
        stt(SC[:], tanh_c, 1.0, SM[:, CVEC:CVEC + 16], ALU.add, ALU.mult, [btc, bSM], [bSC])
        tsc(SC[:], SC[:], 0.5, None, ALU.mult, None, [bSC], [bSC])
        cf.reset()
        cb.reset()
        bMODl = [Buf("MOD0"), Buf("MOD1")]
        bAMl = [Buf("AM0"), Buf("AM1")]
        bGl = [Buf("G0"), Buf("G1")]
        WA = [cb.get(f"WAb{i}", [8, 256]) for i in range(8)]
        WF = [cf.get(f"WAf{i}", [8, 256]) for i in range(4)]
        SCB, bSCB = cz.get("SCBb", [16, 128])
        SCh, bSCh = cz.get("SCh", [32])
        WAq = [cz.get(f"WAq{i}", [8, 128]) for i in range(2)]
        cpy(SCh[:, 0:16], SC[:, 0:16], [bSC], [bSCh])
        cpy(SCB, SC[:, 0:16].unsqueeze(2).to_broadcast([128, 16, 128]), [bSC], [bSCB])

        def finish_mod(l, pfm, bpfm):
            ba = BA0 if l == 0 else BA1
            ng = NG0 if l == 0 else NG1
            tt(MOD[:, l, :].rearrange("p (a b) -> p a b", b=2), pfm[:, 0:32].rearrange("p (a b) -> p a b", b=2),
               SM[:, ba:ba + 16].unsqueeze(2).to_broadcast([128, 16, 2]), ALU.add, [bpfm, bSM], [bMODl[l]])
            mv = MOD[:, l, :].rearrange("p (a b) -> p a b", b=2)
            for ci in range(2):
                stt(AM[:, l * 2 + ci, :], mv[:, 8:16, ci], 1.0, SM[:, ng:ng + 8], ALU.add, ALU.mult,
                    [bMODl[l], bSM], [bAMl[l]])

        wav0 = w_ada[0].rearrange("(k p) n -> p k n", p=128)
        pfm, bpfm = PSF[4], bPSF[4]
        for g in range(0, 8, 2):
            dma("sp", WF[g // 2][0], wav0[:, :, g * 256:(g + 1) * 256], writes=[WF[g // 2][1]])
        for g in range(1, 8, 2):
            dma("pool", WA[g][0], wav0[:, :, g * 256:(g + 1) * 256], writes=[WA[g][1]])
        for g in (0, 1, 2, 5, 6, 3, 4):
            r_ = dma("pool", W0[:, :, g * 512:(g + 1) * 512], w0v[:, :, g * 512:(g + 1) * 512], writes=[bW0g[g]])
            w0_sems.add(r_[1])
        for g in range(8):
            wa, bwa = WA[g]
            if g % 2 == 0:
                wf, bwf = WF[g // 2]
                if g % 4 == 0:
                    cpy(wa, wf, [bwf], [bwa])
                else:
                    act(wa, wf, AF.Copy, [bwf], [bwa])
            lst = []
            for blk in range(2):
                o = pfm[:, (g * 2 + blk) * 2:(g * 2 + blk) * 2 + 2]
                for k in range(8):
                    lst.append((o, wa[:, k, blk * 128:(blk + 1) * 128], SCh[:, 2 * k:2 * k + 2], k == 0, k == 7))
            mms(lst, [bwa, bSCh], [bpfm])
        finish_mod(0, pfm, bpfm)

        def modgen(WFq, BGq):
            pieces = [(0, "g", j) for j in range(8)] + [(1, "f", j) for j in range(16)] + [(1, "g", j) for j in range(8)]
            pf1, bpf1 = PSF[5], bPSF[5]

            def issue(i):
                l, kind, j = pieces[i]
                col0 = (2048 if kind == "g" else 0) + j * 128
                wv = w_ada[l].rearrange("(k p) n -> p k n", p=128)
                dma("sp", WFq[i % 2][0], wv[:, :, col0:col0 + 128], writes=[WFq[i % 2][1]])
                if kind == "g":
                    dma("sp", BGq[i % 2][0], bg[:, l * 1024 + j * 128:l * 1024 + (j + 1) * 128], writes=[BGq[i % 2][1]])

            issue(0)
            yield
            for i, (l, kind, j) in enumerate(pieces):
                if i + 1 < len(pieces):
                    issue(i + 1)
                    yield
                wf, bwf = WFq[i % 2]
                wa, bwa = WAq[i % 2]
                if i % 2 == 0:
                    cpy(wa, wf, [bwf], [bwa])
                else:
                    act(wa, wf, AF.Copy, [bwf], [bwa])
                yield
                if kind == "f":
                    lst = [(pf1[:, j * 2:j * 2 + 2], wa[:, k, :], SCh[:, 2 * k:2 * k + 2], k == 0, k == 7) for k in range(8)]
                    mms(lst, [bwa, bSCh], [bpf1])
                    if j == 15:
                        finish_mod(1, pf1, bpf1)
                else:
                    pg, bpg = bankf()
                    lst = []
                    for ci in range(2):
                        for k in range(8):
                            lst.append((pg[:, ci * 128:(ci + 1) * 128], SCB[:, 2 * k + ci, :], wa[:, k, :], k == 0, k == 7))
                    mms(lst, [bwa, bSCB], [bpg])
                    for ci in range(2):
                        tt(G[:, l * 2 + ci, j * 128:(j + 1) * 128], pg[:, ci * 128:(ci + 1) * 128], BGq[i % 2][0],
                           ALU.add, [bpg, BGq[i % 2][1]], [bGl[l]])
                yield

        def Asc(l, ci, k):
            return AM[:, l * 2 + ci, k:k + 1]

        def Bsc(l, ci, k):
            return MOD[:, l, 2 * k + ci:2 * k + ci + 1]

        S.barrier(skip_sems=w0_sems)

        def norm_to_hT(Xt, bX, xn, bxn, junk, bjunk, hT_dst, bhT, l, ci, evac_act=False, rs_pre=None, mid=None):
            if rs_pre is not None:
                rs, bs = rs_pre
            else:
                st_, bs = stat()
                ss, rs = st_[:, 0:1], st_[:, 1:2]
                act(junk, Xt, AF.Square, [bX], [bjunk, bs], accum=ss)
                act(rs, ss, AF.Ln, [bs, bPRM], [bs], bias=PRM[:, 0:1], scale=1.0 / 1024)
                act(rs, rs, AF.Exp, [bs], [bs], scale=-0.5)
            tsc(xn, Xt, rs, None, ALU.mult, None, [bX, bs], [bxn])
            for hf in range(2):
                pb, bpb = PSB[hf], bPSB[hf]
                transposes([(pb[:, j * 128:(j + 1) * 128], xn[:, (hf * 4 + j) * 128:(hf * 4 + j + 1) * 128])
                            for j in range(4)], [bxn], [bpb])
            if mid is not None:
                mid()
            for j in range(4):
                for hf in range(2):
                    k = hf * 4 + j
                    pb, bpb = PSB[hf], bPSB[hf]
                    if hf == 0:
                        act(hT_dst[:, k, :], pb[:, j * 128:(j + 1) * 128], AF.Identity, [bpb, bAMl[l], bMODl[l]], [bhT[0]],
                            scale=Asc(l, ci, k), bias=Bsc(l, ci, k))
                    else:
                        tsc(hT_dst[:, k, :], pb[:, j * 128:(j + 1) * 128], Asc(l, ci, k), Bsc(l, ci, k), ALU.mult, ALU.add,
                            [bpb, bAMl[l], bMODl[l]], [bhT[1]])

        def run_streams(gens, width, stagger=0, bgen=None):
            active = []
            it = iter(gens)
            if stagger:
                g0 = next(it, None)
                if g0 is not None:
                    active.append(g0)
                    for _ in range(stagger):
                        try:
                            next(g0)
                        except StopIteration:
                            active.remove(g0)
                            break
            while True:
                while len(active) < width:
                    g = next(it, None)
                    if g is None:
                        break
                    active.append(g)
                if not active:
                    break
                for g in list(active):
                    try:
                        next(g)
                    except StopIteration:
                        active.remove(g)
                if bgen is not None:
                    try:
                        next(bgen)
                    except StopIteration:
                        bgen = None
            if bgen is not None:
                for _ in bgen:
                    pass

        cb.reset()
        cf.reset()
        psf_n[0] = 5
        X_ = [cf.get(f"X{i}", [1024]) for i in range(4)]
        TMP = [cf.get(f"TMP{i}", [512]) for i in range(4)]
        K32 = [cf.get(f"K32{i}", [512]) for i in range(2)]
        V32 = [cf.get(f"V32{i}", [512]) for i in range(2)]
        WFq = [cf.get(f"WFq{i}", [8, 128]) for i in range(2)]
        BGq = [cf.get(f"BGq{i}", [128]) for i in range(2)]
        JUNK, bJUNK = cb.get("junk", [1024])
        XN = [cb.get(f"xn{i}", [1024]) for i in range(2)]
        HT = [cb.get(f"HT{i}", [8, 512]) for i in range(2)]
        HTB = [(Buf("hlo"), Buf("hhi")) for i in range(2)]
        UTs_ = [cb.get("UTs0", [4, 512]), cz.get("UTs1", [4, 512])]
        SGAs_ = [cb.get("SGAs0", [4, 512]), cz.get("SGAs1", [4, 512])]
        QTs_ = [cb.get("QTs0", [4, 512]), cz.get("QTs1", [4, 512])]
        KTs_ = [cb.get("KTs0", [4, 512]), cz.get("KTs1", [4, 512])]
        VST = [cb.get(f"VST{i}", [4, 8, 65]) for i in range(2)]
        SGBS = [cb.get(f"SGBS{i}", [4, 512]) for i in range(2)]
        SQ = [cb.get(f"SQ{i}", [512]) for i in range(2)]
        ZERO, bZERO = cb.get("zero", [4, 16])
        S.add("dve", lambda e: e.memset(ZERO, 0.0), writes=[bZERO])
        for i in range(2):
            S.add("dve", lambda e, i=i: e.memset(VST[i][0][:, :, :, 64:65], 1.0), writes=[VST[i][1]])
        sUv = sU.rearrange("c p t -> p c t")
        for b in range(4):
            dma("pool", sUv[:, :, b * 288:b * 288 + 16], ZERO, [bZERO], [bU])
            dma("pool", sUv[:, :, b * 288 + 272:b * 288 + 288], ZERO, [bZERO], [bU])
        dma("pool", sUv[:, :, UB:UB + 16], ZERO, [bZERO], [bU])

        supertiles = [(0, list(range(0, 4))), (0, list(range(4, 8)))]
        supertiles += [(1, list(range(8 + a, 8 + min(a + 4, NTS)))) for a in range(0, NTS, 4)]
        if build_program.phases < 1:
            supertiles = []
        supertiles = supertiles[:build_program.nst]

        xloaded = set()

        def stA(sti, ci, tiles):
            sl = sti % 2
            nt = len(tiles)
            N = nt * 128
            hT, bhT0 = HT[sl]
            bhT = HTB[sl]
            vst, bvst = VST[sl]
            sgbs, bsgbs = SGBS[sl]
            UTs, bUTs = UTs_[sl]
            SGAs, bSGAs = SGAs_[sl]
            QTs, bQTs = QTs_[sl]
            KTs, bKTs = KTs_[sl]
            def xload(sti_, jj_):
                ci_, tiles_ = supertiles[sti_]
                tg_ = tiles_[jj_]
                Xt_, bX_ = X_[(sti_ % 2) * 2 + jj_ % 2]
                src = xp[tg_ * 128:(tg_ + 1) * 128, :] if ci_ == 0 else xs[(tg_ - 8) * 128:(tg_ - 7) * 128, :]
                dma("sp", Xt_, src, writes=[bX_])
                xloaded.add((sti_, jj_))

            for jj in range(min(2, nt)):
                if (sti, jj) not in xloaded:
                    xload(sti, jj)
            rsn = {}

            def stats_of(j_):
                st_, bs_ = stat()
                Xj, bXj = X_[sl * 2 + j_ % 2]
                act(JUNK, Xj, AF.Square, [bXj], [Buf("j"), bs_], accum=st_[:, 0:1])
                act(st_[:, 1:2], st_[:, 0:1], AF.Ln, [bs_, bPRM], [bs_], bias=PRM[:, 0:1], scale=1.0 / 1024)
                act(st_[:, 1:2], st_[:, 1:2], AF.Exp, [bs_], [bs_], scale=-0.5)
                rsn[j_] = (st_[:, 1:2], bs_)

            stats_of(0)
            for jj, tg in enumerate(tiles):
                Xt, bX = X_[sl * 2 + jj % 2]
                xn, bxn = XN[sl]
                nxt_stats = (lambda j_=jj + 1: stats_of(j_)) if jj + 1 < nt else None
                norm_to_hT(Xt, bX, xn, bxn, JUNK, Buf("j"), hT[:, :, jj * 128:(jj + 1) * 128], bhT, 0, ci,
                           rs_pre=rsn[jj], mid=nxt_stats)
                if jj + 2 < nt:
                    xload(sti, jj + 2)
                yield
            if sti + 2 < len(supertiles):
                for jj in range(min(2, len(supertiles[sti + 2][1]))):
                    xload(sti + 2, jj)

            last = (ci == 1 and tiles[0] - 8 == 16)
            Nab, Nga, Nq, ngb = (144, 128, 128, 1) if last else (N, N, N, nt)

            def fm_group(col0, n):
                pb_, bpb_ = bankf()
                lst = [(pb_[:, :n], W0[:, k, col0:col0 + 128], hT[:, k, :n], k == 0, k == 7) for k in range(8)]
                mms(lst, [bW0g[col0 // 512], *bhT], [bpb_])
                return pb_, bpb_

            for c in range(4):
                pa, bpa = fm_group(c * 128, Nab)
                pbb, bpbb = fm_group(512 + c * 128, Nab)
                t0, bt0 = TMP[sl * 2]
                act(t0[:, :Nab], pbb[:, :Nab], AF.Tanh, [bpbb], [bt0], scale=0.5)
                stt(UTs[:, c, :Nab], t0[:, :Nab], 1.0, pa[:, :Nab], ALU.add, ALU.mult, [bt0, bpa], [bUTs])
                yield
            for c in range(4):
                pg_, bpg_ = fm_group(1024 + c * 128, Nga)
                t0, bt0 = TMP[sl * 2 + 1]
                act(t0[:, :Nga], pg_[:, :Nga], AF.Tanh, [bpg_], [bt0], scale=0.5)
                stt(SGAs[:, c, :Nga], t0[:, :Nga], 1.0, pg_[:, :Nga], ALU.add, ALU.mult, [bt0, bpg_], [bSGAs])
                yield
            for jj, tg in enumerate(tiles):
                hs = hT[:, :, jj * 128:(jj + 1) * 128]

                def tm_group(col0):
                    pb_, bpb_ = bankf()
                    lst = [(pb_[:, :], hs[:, k, :], W0[:, k, col0:col0 + 512], k == 0, k == 7) for k in range(8)]
                    mms(lst, [bW0g[col0 // 512], *bhT], [bpb_])
                    return pb_, bpb_

                pv, bpv = tm_group(2560)
                cpy(vst[:, jj, :, 0:64], pv[:, :].rearrange("p (h d) -> p h d", h=8), [bpv], [bvst])
                if ci == 0:
                    v32, bv32 = V32[sl]
                    cpy(v32, pv[:, :], [bpv], [bv32])
                    dma("pool", vo[tg * 128:(tg + 1) * 128, :], v32, [bv32], [])
                yield
                if jj >= ngb:
                    continue
                pgb, bpgb = tm_group(3072)
                t0, bt0 = TMP[sl * 2 + 1]
                act(t0, pgb[:, :], AF.Tanh, [bpgb], [bt0], scale=0.5)
                stt(sgbs[:, jj, :], t0, 1.0, pgb[:, :], ALU.add, ALU.mult, [bt0, bpgb], [bsgbs])
                yield
            def prompt_k(jj):
                tg = tiles[jj]
                hs = hT[:, :, jj * 128:(jj + 1) * 128]
                pk, bpk = bankf()
                lst = [(pk[:, :], hs[:, k, :], W0[:, k, 2048:2560], k == 0, k == 7) for k in range(8)]
                mms(lst, [bW0g[4], *bhT], [bpk])
                k32, bk32 = K32[sl]
                t1, bt1 = TMP[sl * 2 + 1]
                act(t1, pk[:, :], AF.Square, [bpk], [bt1])
                kss, bs = stat()
                S.add("dve", lambda e, t1=t1, kss=kss: e.tensor_reduce(
                    out=kss, in_=t1.rearrange("p (h d) -> p h d", h=8), axis=mybir.AxisListType.X, op=ALU.add),
                    [bt1], [bs])
                act(kss, kss, AF.Ln, [bs, bPRM], [bs], bias=PRM[:, 1:2])
                act(kss, kss, AF.Exp, [bs], [bs], scale=-0.5)
                tt(k32.rearrange("p (h d) -> p h d", h=8), pk[:, :].rearrange("p (h d) -> p h d", h=8),
                   kss.unsqueeze(2).to_broadcast([128, 8, 64]), ALU.mult, [bpk, bs], [bk32])
                tt(k32.rearrange("p (h d) -> p h d", h=8), k32.rearrange("p (h d) -> p h d", h=8),
                   PRM[:, 16:80].unsqueeze(1).to_broadcast([128, 8, 64]), ALU.mult, [bk32, bPRM], [bk32])
                dma("pool", ko[tg * 128:(tg + 1) * 128, :], k32, [bk32], [])

            for (col, dst, bdst, scl, nn) in ((1536, QTs, bQTs, PRM[:, 2:3], Nq), (2048, KTs, bKTs, PRM[:, 3:4], N)):
                for c in range(4):
                    pq, bpq = fm_group(col + c * 128, nn)
                    sq, bsq = SQ[sl]
                    act(sq[:, :nn], pq[:, :nn], AF.Square, [bpq], [bsq])
                    yield
                    p2, bp2 = bankf()
                    mms([(p2[:, :nn], BDB[:], sq[:, :nn], True, True)], [bBDB, bsq], [bp2])
                    t0, bt0 = TMP[sl * 2]
                    act(t0[:, :nn], p2[:, :nn], AF.Ln, [bp2, bPRM], [bt0], bias=PRM[:, 1:2])
                    act(t0[:, :nn], t0[:, :nn], AF.Exp, [bt0], [bt0], scale=-0.5)
                    stt(dst[:, c, :nn], pq[:, :nn], scl, t0[:, :nn], ALU.mult, ALU.mult, [bpq, bt0, bPRM], [bdst])
                    if ci == 0 and col == 1536 and c < nt:
                        prompt_k(c)
                    yield
            if ci == 0:
                t0c = tiles[0] * 128
                for bb in range(2):
                    b = tiles[0] // 2 + bb
                    dma("pool", sUv[:, :, b * 288 + 16:b * 288 + 272], UTs[:, :, bb * 256:(bb + 1) * 256], [bUTs], [bU])
            else:
                t0c = 1024 + (tiles[0] - 8) * 128
                u0 = UB + 16 + (tiles[0] - 8) * 128
                dma("pool", sUv[:, :, u0:u0 + Nab], UTs[:, :, :Nab], [bUTs], [bU])
            dma("pool", sGA.rearrange("c p t -> p c t")[:, :, t0c:t0c + Nga], SGAs[:, :, :Nga], [bSGAs], [bGA])
            dma("pool", sQ.rearrange("c p t -> p c t")[:, :, t0c:t0c + Nq], QTs[:, :, :Nq], [bQTs], [bQ])
            dma("pool", sK.rearrange("c p t -> p c t")[:, :, t0c:t0c + N], KTs[:, :, :N], [bKTs], [bK])
            g0 = tiles[0]
            dma("pool", sV.rearrange("j p f -> p j f")[:, g0:g0 + nt, :],
                vst[:, 0:nt, :, :].rearrange("p j h d -> p j (h d)"), [bvst], [bV])
            dma("pool", sGB.rearrange("j p f -> p j f")[:, g0:g0 + ngb, :], sgbs[:, 0:ngb, :], [bsgbs], [bGB])
            yield

        run_streams((stA(sti, ci, tiles) for sti, (ci, tiles) in enumerate(supertiles)), 2, stagger=0,
                    bgen=modgen(WFq, BGq))
        psf_n[0] = 4

        S.barrier()
        PHASES_DONE = build_program.phases

        if PHASES_DONE >= 2:
            cz.reset()
            cb.reset()
            cf.reset()
            WO0, bWO0 = cz.get("WO0", [8, 1024])
            ET, bET = cz.get("ET", [8, 13, 128])
            CK, bCK = cz.get("CK", [4, 512])
            CV, bCV = cz.get("CV", [4, 8, 65])
            DG, bDG = cz.get("DG", [124, 128])
            dma("pool", WO0, w_out_0.rearrange("(k p) n -> p k n", p=128), writes=[bWO0])
            dma("pool", CK, ckT.rearrange("(c p) k -> p c k", p=128), writes=[bCK])
            S.add("dve", lambda e: e.memset(CV[:, :, :, 64:65], 1.0), writes=[bCV])
            for j in range(4):
                dma("pool", CV[:, j, :, 0:64], cvv[j * 128:(j + 1) * 128, :].rearrange("p (h d) -> p h d", h=8), writes=[bCV])
            CWH, bCWH = CWHT[:], Buf("CWH")
            tsc(CWH, SM[:, CW:CW + 124], 0.5, None, ALU.mult, None, [bSM], [bCWH])
            bDGc = [Buf(f"DG{c}") for c in range(4)]
            for c in range(4):
                tt(DG[:, c * 31:(c + 1) * 31, :], IDB[:].unsqueeze(1).to_broadcast([128, 31, 128]),
                   CWH[:, c * 31:(c + 1) * 31].unsqueeze(2).to_broadcast([128, 31, 128]),
                   ALU.mult, [bIDB, bCWH], [bDGc[c]], eng=DG_ENG)
            al = [Buf("CY"), Buf("CY2"), Buf("MEAN"), Buf("RSTD")]
            cf.reset()
            CY, _ = cf.get("CY", [4, 512])
            CY2, _ = cf.get("CY2", [4, 512])
            MEAN, _ = cf.get("MEAN", [512])
            RSTD, _ = cf.get("RSTD", [512])
            bCY, bCY2, bMEAN, bRSTD = al
            TN = [cf.get(f"TN{i}", [512]) for i in range(2)]
            XR = [cf.get(f"XR{i}", [1024]) for i in range(2)]
            Y0 = [cf.get(f"Y0{i}", [1024]) for i in range(2)]
            OT, bOT = cf.get("OT", [512])
            UP, bUP = cb.get("UP", [4, 576])
            SGA, bSGA = cb.get("SGA", [4, 512])
            ZT_ = [cb.get(f"ZT{i}", [8, 512]) for i in range(2)]
            QT, bQT = cb.get("QT", [4, 512])
            KW, bKW = cb.get("KW", [4, 1024])
            VW, bVW = cb.get("VW", [8, 8, 65])
            SGB, bSGB = cb.get("SGB", [4, 512])
            PT = [cb.get(f"PT{i}", [9, 128]) for i in range(3)]
            YA, bYA = cb.get("YA", [512])
            SN, bSN = cb.get("SN", [512])

            stB = [(0, [0, 1, 2, 3]), (0, [4, 5, 6, 7])] + [(1, list(range(a, min(a + 4, NTB)))) for a in range(0, NTB, 4)]
            psf_n[0] = 4
            rot[0] = [0, 1, 2, 6]
            psb_n[0] = 1
            pt_i = [0]
            xr_i = [0]

            def Xgen(ci, tiles, zi):
                ZT, bZT = ZT_[zi]
                nt = len(tiles)
                N = nt * 128
                if ci == 0:
                    tcol = tiles[0] * 128
                    seqs = [(0, 256, 0), (256, 256, 288)]
                    b0 = tiles[0] // 2
                    dma("sp", UP[:, :, 0:576], sUv[:, :, b0 * 288:b0 * 288 + 576], [bU], [bUP])
                else:
                    tcol = 1024 + tiles[0] * 128
                    seqs = [(0, N, 0)]
                    u0 = UB + tiles[0] * 128
                    dma("sp", UP[:, :, 0:N + 32], sUv[:, :, u0:u0 + N + 32], [bU], [bUP])
                dma("sp", SGA[:, :, :N], sGA.rearrange("c p t -> p c t")[:, :, tcol:tcol + N], [bGA], [bSGA])
                for c in range(4):
                    pc, bpc = PSF[3], bPSF[3]
                    lst = []
                    for (oc, n, uo) in seqs:
                        for k in range(31):
                            lst.append((pc[:, oc:oc + n], DG[:, c * 31 + k, :], UP[:, c, uo + k + 1:uo + k + 1 + n],
                                        k == 0, k == 30))
                    npc = 4
                    per = (len(lst) + npc - 1) // npc
                    for pi in range(npc):
                        mms(lst[pi * per:(pi + 1) * per], [bDGc[c], bUP], [bpc])
                        if pi == npc - 1:
                            act(CY[:, c, :N], pc[:, :N], AF.Identity, [bpc, bSM], [bCY], bias=SM[:, CB + c:CB + c + 1])
                            act(CY2[:, c, :N], pc[:, :N], AF.Square, [bpc, bSM], [bCY2], bias=SM[:, CB + c:CB + c + 1])
                        yield
                pm, bpm = bankf()
                mms([(pm[:, :N], ON32[:], CY[:, c, :N], c == 0, c == 3) for c in range(4)], [bON, bCY], [bpm])
                cpy(MEAN[:, :N], pm[:, :N], [bpm], [bMEAN])
                yield
                pq2, bpq2 = bankf()
                mms([(pq2[:, :N], ON32[:], CY2[:, c, :N], c == 0, c == 3) for c in range(4)], [bON, bCY2], [bpq2])
                tt(RSTD[:, :N], MEAN[:, :N], MEAN[:, :N], ALU.mult, [bMEAN], [bRSTD])
                tt(RSTD[:, :N], pq2[:, :N], RSTD[:, :N], ALU.subtract, [bpq2, bRSTD], [bRSTD])
                act(RSTD[:, :N], RSTD[:, :N], AF.Ln, [bRSTD, bPRM], [bRSTD], bias=PRM[:, 0:1])
                act(RSTD[:, :N], RSTD[:, :N], AF.Exp, [bRSTD], [bRSTD], scale=-0.5)
                yield
                for c in range(4):
                    tn, btn = TN[c % 2]
                    tt(tn[:, :N], CY[:, c, :N], MEAN[:, :N], ALU.subtract, [bCY, bMEAN], [btn])
                    yield
                    tt(tn[:, :N], tn[:, :N], RSTD[:, :N], ALU.mult, [btn, bRSTD], [btn])
                    yield
                    tsc(tn[:, :N], tn[:, :N], SM[:, LG + c:LG + c + 1], SM[:, LB + c:LB + c + 1], ALU.mult, ALU.add,
                        [btn, bSM], [btn])
                    act(SN[:, :N], tn[:, :N], AF.Tanh, [btn], [bSN], scale=0.5)
                    yield
                    stt(tn[:, :N], SN[:, :N], 1.0, tn[:, :N], ALU.add, ALU.mult, [bSN, btn], [btn])
                    yield
                    stt(ZT[:, c, :N], tn[:, :N], 0.25, SGA[:, c, :N], ALU.mult, ALU.mult, [btn, bSGA], [bZT])
                    yield

            def Yparams(ci, tiles):
                nt = len(tiles)
                if ci == 0:
                    tcol = tiles[0] * 128
                    gt0 = tiles[0]
                    kt0, nkt = tiles[0], 4
                else:
                    tcol = 1024 + tiles[0] * 128
                    gt0 = 8 + tiles[0]
                    kt0 = max(tiles[0] - 2, 0)
                    nkt = min(max(tiles[-1] + 2, 3) + 1, NTS) - kt0
                return nt, nt * 128, tcol, gt0, kt0, nkt

            def Yloads(ci, tiles):
                nt, N, tcol, gt0, kt0, nkt = Yparams(ci, tiles)
                dma("sp", QT[:, :, :N], sQ.rearrange("c p t -> p c t")[:, :, tcol:tcol + N], [bQ], [bQT])
                kcol = (kt0 * 128) if ci == 0 else (1024 + kt0 * 128)
                dma("sp", KW[:, :, :nkt * 128], sK.rearrange("c p t -> p c t")[:, :, kcol:kcol + nkt * 128], [bK], [bKW])
                gk0 = kt0 if ci == 0 else 8 + kt0
                dma("sp", VW[:, 0:nkt, :, :].rearrange("p j h d -> p j (h d)"),
                    sV.rearrange("j p f -> p j f")[:, gk0:gk0 + nkt, :], [bV], [bVW])
                dma("sp", SGB[:, 0:nt, :], sGB.rearrange("j p f -> p j f")[:, gt0:gt0 + nt, :], [bGB], [bSGB])

            def Ygen(ci, tiles, zi, nxt=None):
                ZT, bZT = ZT_[zi]
                nt, N, tcol, gt0, kt0, nkt = Yparams(ci, tiles)
                tinfo = {}
                for jj, t in enumerate(tiles):
                    if ci == 0:
                        bt = (t // 2) * 2
                        wt = [bt - tiles[0], bt - tiles[0] + 1]
                        pat0 = None
                        nctx = 0
                    else:
                        kts = key_tiles(t)
                        wt = [j - kt0 for j in kts]
                        pat0 = pat_index(t, kts[0])
                        nctx = 4
                    tinfo[jj] = (wt, pat0, nctx)
                ob = [(PSF[4], bPSF[4]), (PSF[5], bPSF[5])]
                pts = {}

                def stage1(jj, h):
                    wt, pat0, nctx = tinfo[jj]
                    nw = len(wt)
                    c, hh = h // 2, h % 2
                    pr = slice(64 * hh, 64 * hh + 64)
                    pt, bpt = PT[pt_i[0] % 3]
                    pt_i[0] += 1
                    pts[(jj, h)] = (pt, bpt)
                    qs = QT[pr, c, jj * 128:(jj + 1) * 128]
                    groups = [wt[0:4]] + ([wt[4:]] if nw > 4 else [])
                    so = 0
                    for grp in groups:
                        ps_, bps_ = bankf()
                        lst = [(ps_[:, i * 128:(i + 1) * 128], KW[pr, c, s_ * 128:(s_ + 1) * 128], qs, True, True)
                               for i, s_ in enumerate(grp)]
                        mms(lst, [bKW, bQT], [bps_])
                        act(pt[:, so:so + len(grp), :].rearrange("p a b -> p (a b)"), ps_[:, :len(grp) * 128],
                            AF.Exp, [bps_], [bpt])
                        so += len(grp)
                    if pat0 is not None:
                        tt(pt[:, 0:nw, :], pt[:, 0:nw, :], ET[:, h, pat0:pat0 + nw, :], ALU.mult, [bpt, bETh[h]], [bpt])
                    if nctx:
                        ps_, bps_ = bankf()
                        lst = [(ps_[:, i * 128:(i + 1) * 128], CK[pr, c, i * 128:(i + 1) * 128], qs, True, True)
                               for i in range(4)]
                        mms(lst, [bCK, bQT], [bps_])
                        act(pt[:, nw:nw + 4, :].rearrange("p a b -> p (a b)"), ps_[:, :], AF.Exp, [bps_], [bpt])

                def stage2(jj, h):
                    wt, pat0, nctx = tinfo[jj]
                    nw = len(wt)
                    tot = nw + nctx
                    pt, bpt = pts[(jj, h)]
                    po, bpo = ob[h // 4]
                    oo = (h % 4) * 65
                    lst = []
                    for i, s_ in enumerate(wt):
                        lst.append((po[:, oo:oo + 65], pt[:, i, :], VW[:, s_, h, :], i == 0, i == tot - 1))
                    for i in range(nctx):
                        lst.append((po[:, oo:oo + 65], pt[:, nw + i, :], CV[:, i, h, :], False, nw + i == tot - 1))
                    mms(lst, [bpt, bVW, bCV], [bpo])

                def tail1(jj):
                    for hb in range(2):
                        po, bpo = ob[hb]
                        pov = po[:, 0:260].rearrange("p (h d) -> p h d", h=4)
                        st_, bs = stat()
                        rd = st_[:, 0:4]
                        S.add("dve", lambda e, rd=rd, pov=pov: e.reciprocal(out=rd, in_=pov[:, :, 64]), [bpo], [bs])
                        otv = OT[:, hb * 256:(hb + 1) * 256].rearrange("p (h d) -> p h d", h=4)
                        stt(otv, pov[:, :, 0:64], 0.5, rd.unsqueeze(2).to_broadcast([128, 4, 64]), ALU.mult, ALU.mult,
                            [bpo, bs], [bOT])
                    tt(YA, OT, SGB[:, jj, :], ALU.mult, [bOT, bSGB], [bYA])

                def tail2(jj):
                    pb, bpb = bankb()
                    transposes([(pb[:, k * 128:(k + 1) * 128], YA[:, k * 128:(k + 1) * 128]) for k in range(4)],
                               [bYA], [bpb])
                    cpy(ZT[:, 4:8, jj * 128:(jj + 1) * 128], pb[:, 0:512].rearrange("p (a b) -> p a b", a=4),
                        [bpb], [bZT])

                items = [(jj, h) for jj in range(nt) for h in range(8)]
                stage1(*items[0])
                stage1(*items[1])
                pend = None
                for i_, (jj, h) in enumerate(items):
                    if i_ + 2 < len(items):
                        stage1(*items[i_ + 2])
                    stage2(jj, h)
                    if pend is not None and h == 1:
                        tail2(pend)
                        pend = None
                    if h == 7:
                        tail1(jj)
                        pend = jj
                    yield
                if pend is not None:
                    tail2(pend)
                if nxt is not None:
                    Yloads(*nxt)
                yield
                for jj, t in enumerate(tiles):
                    xr, bxr = XR[xr_i[0] % 2]
                    y0, by0 = Y0[xr_i[0] % 2]
                    xr_i[0] += 1
                    src = xp[t * 128:(t + 1) * 128, :] if ci == 0 else xs[t * 128:(t + 1) * 128, :]
                    dma("sp", xr, src, writes=[bxr])
                    for n in range(2):
                        po, bpo = bankf()
                        lst = [(po[:, :], ZT[:, k, jj * 128:(jj + 1) * 128], WO0[:, k, n * 512:(n + 1) * 512], k == 0, k == 7)
                               for k in range(8)]
                        mms(lst, [bZT, bWO0], [bpo])
                        tt(y0[:, n * 512:(n + 1) * 512], po[:, :], G[:, ci, n * 512:(n + 1) * 512], ALU.mult, [bpo, bGl[0]], [by0])
                        yield
                    tt(y0, y0, xr, ALU.add, [by0, bxr], [by0])
                    gt = t if ci == 0 else 8 + t
                    dma("pool", sY0[gt], y0, [by0], [bY0])
                    st_, bs = stat()
                    act(xr, y0, AF.Square, [by0], [bxr, bs], accum=st_[:, 0:1])
                    act(RS1[:, gt:gt + 1], st_[:, 0:1], AF.Ln, [bs, bPRM], [bRS1[gt]], bias=PRM[:, 0:1], scale=1.0 / 1024)
                    act(RS1[:, gt:gt + 1], RS1[:, gt:gt + 1], AF.Exp, [bRS1[gt]], [bRS1[gt]], scale=-0.5)

            def zip2(g1, g2):
                gs = [g for g in (g1, g2) if g is not None]
                while gs:
                    for g in list(gs):
                        try:
                            next(g)
                        except StopIteration:
                            gs.remove(g)

            bETh = [Buf(f"ET{h}") for h in range(8)]

            def build_ET():
                for h in range(8):
                    dma("pool", ET[:, h, :, :].rearrange("p a b -> p (a b)"), ebias[:, h * 1664:(h + 1) * 1664],
                        writes=[bETh[h]])
                yield
                for h in range(8):
                    eh = ET[:, h, :, :].rearrange("p a b -> p (a b)")
                    act(eh, eh, AF.Exp, [bETh[h]], [bETh[h]])
                    yield

            def wzip(gws):
                alive = [True] * len(gws)
                while alive[0]:
                    for gi, (g, reps) in enumerate(gws):
                        for _ in range(reps):
                            if alive[gi]:
                                try:
                                    next(g)
                                except StopIteration:
                                    alive[gi] = False
                for gi, (g, reps) in enumerate(gws):
                    if alive[gi]:
                        for _ in g:
                            pass

            W1OFF = 25632
            W1 = ZB[:, W1OFF:W1OFF + 16384].rearrange("p (k n) -> p k n", k=8)
            bW1 = Buf("W1")
            w1v = w_in_1.rearrange("(k p) n -> p k n", p=128)

            def Xchain():
                for si, (ci, tiles) in enumerate(stB):
                    yield from Xgen(ci, tiles, si % 2)
                for k in range(0, 8, 2):
                    dma("pool", W1[:, k:k + 2, :], w1v[:, k:k + 2, :], writes=[bW1] + bDGc)
                yield

            def Ychain():
                for si, (ci, tiles) in enumerate(stB):
                    yield from Ygen(ci, tiles, si % 2, nxt=(stB[si + 1] if si + 1 < len(stB) else None))

            xc, yc, ec = Xchain(), Ychain(), build_ET()
            alive = {"x": True, "y": True, "e": True}

            def stepg(name, g):
                if alive[name]:
                    try:
                        next(g)
                    except StopIteration:
                        alive[name] = False

            for i_ in range(16):
                stepg("x", xc)
                if i_ == 0:
                    Yloads(*stB[0])
                if i_ % 4 == 3:
                    stepg("e", ec)
            while alive["x"] or alive["y"] or alive["e"]:
                stepg("y", yc)
                stepg("x", xc)
                stepg("e", ec)
            psf_n[0] = 4
            rot[0] = [0, 1, 2, 3, 4, 5]
            psb_n[0] = 2
            S.barrier()

        if PHASES_DONE >= 3:
            cz.reset()
            cb.reset()
            cf.reset()
            PW, bPW = cz.get("PW", [8, 256])
            WO1, bWO1 = cz.get("WO1", [8, 1024])
            BND, bBND = cz.get("BND", [32, 128])
            assert cz.off <= W1OFF
            dma("pool", PW, pool_w.rearrange("g (kc p) e -> p (g kc) e", p=128), writes=[bPW])
            dma("pool", WO1, w_out_1.rearrange("(k p) n -> p k n", p=128), writes=[bWO1])
            dma("pool", BND, band.rearrange("p (a b) -> p a b", a=32), writes=[bBND])
            JUNK, bJUNK = cb.get("junk", [1024])
            NSTR = 2
            NYR = 4
            YR = [[cf.get(f"YR{q}_{i}", [1024]) for i in range(NYR)] for q in range(NSTR)]
            TG = [[cb.get(f"TG{q}_{i}", [512]) for i in range(2)] for q in range(NSTR)]
            TO = [[cf.get(f"TO{q}_{i}", [512]) for i in range(2)] for q in range(NSTR)]
            XN = [cb.get(f"xn{q}", [1024]) for q in range(NSTR)]
            H1 = [cb.get(f"H1{q}", [8, 128]) for q in range(NSTR)]
            H1B = [(Buf("h1lo"), Buf("h1hi")) for q in range(NSTR)]
            U1 = [[cb.get(f"U1{q}_{i}", [1024]) for i in range(4)] for q in range(NSTR)]
            SG1 = [[cb.get(f"SG1{q}_{i}", [8, 128]) for i in range(3)] for q in range(NSTR)]
            DT = [cb.get(f"DT{q}", [8, 128]) for q in range(NSTR)]
            Z1 = [cb.get(f"Z1{q}", [8, 128]) for q in range(NSTR)]

            def zipn(gs):
                gs = [g for g in gs if g is not None]
                while gs:
                    for g in list(gs):
                        try:
                            next(g)
                        except StopIteration:
                            gs.remove(g)
                    yield

            def seq_stream(q, seqs):
                cnt = [0]
                for (ci, tl, nout) in seqs:
                    info = {}

                    def C1(t, ci=ci, info=info):
                        i = cnt[0]
                        cnt[0] += 1
                        info[t] = i
                        yr, byr = YR[q][i % NYR]
                        xn, bxn = XN[q]
                        h1, _ = H1[q]
                        bh1 = H1B[q]
                        u1, bu1 = U1[q][i % 4]
                        sg1, bsg1 = SG1[q][i % 3]
                        gt = t if ci == 0 else 8 + t
                        dma("sp", yr, sY0[gt], [bY0], [byr])
                        norm_to_hT(yr, byr, xn, bxn, JUNK, Buf("j"), h1, bh1, 1, ci, evac_act=True,
                                   rs_pre=(RS1[:, gt:gt + 1], bRS1[gt]))
                        yield
                        for n in range(2):
                            po, bpo = bankf()
                            lst = [(po[:, :], h1[:, k, :], W1[:, k, n * 512:(n + 1) * 512], k == 0, k == 7) for k in range(8)]
                            mms(lst, [*bh1, bW1], [bpo])
                            act(u1[:, n * 512:(n + 1) * 512], po[:, :], AF.Copy, [bpo], [bu1])
                            yield
                        for hf in range(2):
                            po, bpo = bankf()
                            lst = []
                            for bq in range(4):
                                col = 1024 + (hf * 4 + bq) * 128
                                for k in range(8):
                                    lst.append((po[:, bq * 128:(bq + 1) * 128], W1[:, k, col:col + 128], h1[:, k, :], k == 0, k == 7))
                            mms(lst, [*bh1, bW1], [bpo])
                            tg, btg = TG[q][hf]
                            act(tg, po[:, :], AF.Tanh, [bpo], [btg], scale=0.5)
                            stt(sg1[:, hf * 4:(hf + 1) * 4, :].rearrange("p a b -> p (a b)"), tg, 1.0, po[:, :], ALU.add, ALU.mult,
                                [btg, bpo], [bsg1])
                            yield

                    def C2(t, first, last_true, ci=ci, info=info):
                        i = info[t]
                        yr, byr = YR[q][i % NYR]
                        sg1, bsg1 = SG1[q][i % 3]
                        dt, bdt = DT[q]
                        z1, bz1 = Z1[q]
                        nbs = []
                        base = 0 if ci == 0 else 16
                        if not first:
                            nbs.append(((i - 1) % 4, base + 0))
                        nbs.append((i % 4, base + (4 if first else 8)))
                        if not last_true:
                            nbs.append(((i + 1) % 4, base + 12))
                        for hf in range(2):
                            po, bpo = bankf()
                            lst = []
                            for bq in range(4):
                                cc = hf * 4 + bq
                                g = cc // 2
                                for j_, (sl_, bi) in enumerate(nbs):
                                    lst.append((po[:, bq * 128:(bq + 1) * 128], U1[q][sl_][0][:, cc * 128:(cc + 1) * 128],
                                                BND[:, bi + g, :], j_ == 0, j_ == len(nbs) - 1))
                            rd = [U1[q][sl_][1] for (sl_, bi) in nbs]
                            mms(lst, rd + [bBND], [bpo])
                            act(dt[:, hf * 4:(hf + 1) * 4, :].rearrange("p a b -> p (a b)"), po[:, :], AF.Copy, [bpo], [bdt])
                            yield
                        for hf in range(2):
                            po, bpo = bankf()
                            lst = []
                            for bq in range(4):
                                ob_ = hf * 4 + bq
                                g, eb = ob_ // 2, ob_ % 2
                                for kc in range(2):
                                    lst.append((po[:, bq * 128:(bq + 1) * 128], PW[:, g * 2 + kc, eb * 128:(eb + 1) * 128],
                                                dt[:, g * 2 + kc, :], kc == 0, kc == 1))
                            mms(lst, [bPW, bdt], [bpo])
                            for bq in range(4):
                                ob_ = hf * 4 + bq
                                stt(z1[:, ob_, :], po[:, bq * 128:(bq + 1) * 128], PRM[:, 8 + ob_:9 + ob_], sg1[:, ob_, :],
                                    ALU.mult, ALU.mult, [bpo, bPRM, bsg1], [bz1])
                            yield
                        for n in range(2):
                            to, bto = TO[q][n]
                            po, bpo = bankf()
                            lst = [(po[:, :], z1[:, k, :], WO1[:, k, n * 512:(n + 1) * 512], k == 0, k == 7) for k in range(8)]
                            mms(lst, [bz1, bWO1], [bpo])
                            tt(to, po[:, :], G[:, 2 + ci, n * 512:(n + 1) * 512], ALU.mult, [bpo, bGl[1]], [bto])
                            tt(yr[:, n * 512:(n + 1) * 512], to, yr[:, n * 512:(n + 1) * 512], ALU.add, [bto, byr], [byr],
                               eng="pool")
                            yield
                        dst = yp[t * 128:(t + 1) * 128, :] if ci == 0 else ys[t * 128:(t + 1) * 128, :]
                        dma("pool", dst, yr, [byr], [])

                    yield from zipn([C1(tl[0])])
                    if len(tl) > 1:
                        yield from zipn([C1(tl[1])])
                    for i_, t in enumerate(tl):
                        g1 = C1(tl[i_ + 2]) if i_ + 2 < len(tl) else None
                        g2 = C2(t, i_ == 0, (ci == 0 and i_ == len(tl) - 1)) if i_ < nout else None
                        yield from zipn([g1, g2])

            seqP = [(0, [2 * b, 2 * b + 1], 2) for b in range(4)]
            seqS = [(1, list(range(NTB)), 16)]
            gS, gP = seq_stream(0, seqS), seq_stream(1, seqP)
            alive = [True, True]
            while any(alive):
                for gi, (g, reps) in enumerate(((gS, 1), (gP, 1))):
                    for _ in range(reps):
                        if alive[gi]:
                            try:
                                next(g)
                            except StopIteration:
                                alive[gi] = False

        S.finish()
        S.emit(block, sems, dsems)
    return nc


build_program.phases = 3
build_program.nst = 99
DG_ENG = "dve"


def _att_tables(rpb, flip):
    pats = [(10, 10 + d) for d in (-2, -1, 0, 1, 2)] + [(0, j) for j in range(4)] + [(1, j) for j in range(4)]
    idx = np.arange(128)
    bias = np.zeros((8, 13, 128, 128), np.float32)
    mask = np.zeros((13, 128, 128), np.float32)
    for pi, (t, j) in enumerate(pats):
        ik = 2 * j + idx // 64
        ckl = idx % 64
        iq = 2 * t + idx // 64
        cql = idx % 64
        if flip:
            rk, ck_, rq, cq = 63 - ik, 63 - ckl, 63 - iq, 63 - cql
        else:
            rk, ck_, rq, cq = ik, ckl, iq, cql
        RK, RQ = rk[:, None], rq[None, :]
        CK_, CQ = ck_[:, None], cq[None, :]
        start = np.clip(RQ - 4, 0, 56)
        rowok = (RK >= start) & (RK < start + 8)
        qcs = np.clip(CQ - 8, 0, 48)
        colok = (CK_ >= qcs) & (CK_ < qcs + 16)
        dr = np.clip(RK - RQ + 7, 0, 14)
        dc = np.clip(CK_ - CQ + 15, 0, 30)
        mask[pi] = (rowok & colok).astype(np.float32)
        dr = np.broadcast_to(dr, (128, 128))
        dc = np.broadcast_to(dc, (128, 128))
        bias[:, pi] = rpb[:, dr, dc]
    bias = np.where(mask[None] > 0, bias, np.float32(-30000.0)).astype(np.float32)
    eb = np.ascontiguousarray(bias.transpose(2, 0, 1, 3)).reshape(128, 8 * 13 * 128)
    em = np.ascontiguousarray(mask.transpose(1, 0, 2)).reshape(128, 13 * 128)
    return eb, em


def _band_tables2(flip):
    out = np.zeros((32, 128, 128), np.float32)
    wins = (2, 4, 8, 16)
    for sset in range(2):
        fl = flip
        for g, w in enumerate(wins):
            lo, hi = -(w // 2), w - w // 2
            if fl:
                lo, hi = -(w - w // 2) + 1, w // 2 + 1
            for role in range(4):
                M = np.zeros((128, 128), np.float32)
                for t in range(128):
                    a, b = t + lo, t + hi
                    ca, cbb = a, b
                    if role == 1:
                        ca = max(a, 0)
                    if role == 2 and sset == 0:
                        cbb = min(b, 128)
                    if role in (0, 3):
                        cnt = w
                    else:
                        cnt = cbb - ca
                    for s in range(ca, cbb):
                        if role == 0 and s < 0:
                            M[s + 128, t] += 1.0 / cnt
                        elif role == 3 and s >= 128:
                            M[s - 128, t] += 1.0 / cnt
                        elif role in (1, 2) and 0 <= s < 128:
                            M[s, t] += 1.0 / cnt
                    if role in (1, 2):
                        M[t, t] -= 1.0
                out[sset * 16 + role * 4 + g] = M
    return np.ascontiguousarray(out.transpose(1, 0, 2)).reshape(128, 32 * 128)


_CACHE = {}


def kernel(x_prompt, x_sample, cache_k_0, cache_v_0, c, c_ctx,
           norm_g_0, w_ada_0, b_ada_0, w_in_0, conv_w_0, conv_b_0, conv_ln_g_0, conv_ln_b_0,
           q_norm_0, k_norm_0, rpb_0, w_out_0,
           norm_g_1, w_ada_1, b_ada_1, w_in_1, pool_w_1, pool_scale_1, w_out_1):
    f = lambda a: np.ascontiguousarray(np.asarray(a, dtype=np.float32))
    x_prompt, x_sample, cache_k_0, cache_v_0, c, c_ctx = map(f, (x_prompt, x_sample, cache_k_0, cache_v_0, c, c_ctx))
    if "nc" not in _CACHE:
        _CACHE["nc"] = build_program()
    nc = _CACHE["nc"]
    fm = lambda v, nb: f(v).reshape(nb, 128).T
    shared = dict(w_ada_0=f(w_ada_0), w_ada_1=f(w_ada_1), w_in_0=f(w_in_0), w_out_0=f(w_out_0), w_in_1=f(w_in_1),
                  pool_w=f(pool_w_1), w_out_1=f(w_out_1), ident=np.eye(128, dtype=np.float32),
                  bdiag=np.kron(np.eye(2, dtype=np.float32), np.ones((64, 64), np.float32)),
                  ones32=np.full((128, 128), 1.0 / 512, np.float32))
    bgr = np.concatenate([np.broadcast_to(f(b_ada_0)[2048:], (128, 1024)),
                          np.broadcast_to(f(b_ada_1)[2048:], (128, 1024))], axis=1)
    shared["bg"] = np.ascontiguousarray(bgr)
    tabs = {fl: (_att_tables(f(rpb_0), fl), _band_tables2(fl)) for fl in (False, True)}
    in_maps = []
    for i in range(8):
        b, half = i // 2, i % 2
        flip = half == 1
        sm = np.zeros((128, NSM), np.float32)
        sm[:, NG0:NG0 + 8] = fm(norm_g_0, 8)
        sm[:, NG1:NG1 + 8] = fm(norm_g_1, 8)
        sm[:, BA0:BA0 + 24] = fm(b_ada_0, 24)
        sm[:, BA1:BA1 + 24] = fm(b_ada_1, 24)
        cw = f(conv_w_0)[::-1] if flip else f(conv_w_0)
        sm[:, CW:CW + 124] = cw.T.reshape(4, 128, 31).transpose(1, 0, 2).reshape(128, 124)
        sm[:, CB:CB + 4] = fm(conv_b_0, 4)
        sm[:, LG:LG + 4] = fm(conv_ln_g_0, 4)
        sm[:, LB:LB + 4] = fm(conv_ln_b_0, 4)
        sm[:, QN] = np.tile(f(q_norm_0), 2)
        sm[:, KN] = np.tile(f(k_norm_0), 2)
        sm[:, PSC:PSC + 8] = fm(pool_scale_1, 8)
        cv2 = np.stack([fm(c_ctx, 8), fm(c[b], 8)], axis=2)
        sm[:, CVEC:CVEC + 16] = cv2.reshape(128, 16)
        sm[:, KNR:KNR + 64] = np.broadcast_to(f(k_norm_0), (128, 64))
        xsb = x_sample[b][::-1] if flip else x_sample[b]
        (eb, em), bnd = tabs[flip]
        m = dict(shared)
        xpb = x_prompt[4 * i:4 * i + 4][:, ::-1] if flip else x_prompt[4 * i:4 * i + 4]
        m.update(xp=np.ascontiguousarray(xpb).reshape(1024, 1024),
                 xs=np.ascontiguousarray(xsb[:TS]),
                 ckT=np.ascontiguousarray(cache_k_0[b].reshape(512, 512).T),
                 cvv=np.ascontiguousarray(cache_v_0[b].reshape(512, 512)),
                 smallp=sm, ebias=eb, emask=em, band=bnd)
        in_maps.append(m)
    if _CACHE.get("in_maps_only"):
        return in_maps
    res = run_bass_kernel_spmd(nc, in_maps, core_ids=list(range(8)))
    y_prompt = np.zeros((32, 256, 1024), np.float32)
    y_sample = np.zeros((4, 4096, 1024), np.float32)
    k_ctx = np.zeros((32, 256, 8, 64), np.float32)
    v_ctx = np.zeros((32, 256, 8, 64), np.float32)
    for i in range(8):
        r = res.results[i]
        b, half = i // 2, i % 2
        sl = slice(None, None, -1) if half == 1 else slice(None)
        y_prompt[4 * i:4 * i + 4] = r["yp"].reshape(4, 256, 1024)[:, sl]
        k_ctx[4 * i:4 * i + 4] = r["ko"].reshape(4, 256, 8, 64)[:, sl]
        v_ctx[4 * i:4 * i + 4] = r["vo"].reshape(4, 256, 8, 64)[:, sl]
        if half == 0:
            y_sample[b, :2048] = r["ys"]
        else:
            y_sample[b, 2048:] = r["ys"][::-1]
    return (y_prompt, y_sample, k_ctx, v_ctx)
```
